# Optimizing a Trainium2 kernel written in Bass

```python
import math
import jax
import jax.numpy as jnp
from jax import lax

D_MODEL = 2048
BATCH = 8
SEQ = 2048
DEPTH = 1

GRID_W = 64
CTX_LEN = 256
EPS = 1e-6
N_MOD = 6

ATTN_WIDTH = D_MODEL // 2
ATTN_HEAD_DIM = 64
ATTN_VDIM = 2 * ATTN_HEAD_DIM
ATTN_HEADS = ATTN_WIDTH // ATTN_VDIM
ATTN_QK = ATTN_HEADS * 2 * ATTN_HEAD_DIM
Q_BLOCK = 128
ROPE_BASE = 10000.0

REC_WIDTH = D_MODEL // 2
REC_KDIM = 128
REC_VDIM = 128
REC_HEADS = REC_WIDTH // REC_VDIM
REC_K = REC_HEADS * REC_KDIM
REC_CHUNK = 64

FFN_DIM = 256 * ((8 * D_MODEL // 3 + 255) // 256)
CONV_W = 3

IN_SPLITS = (ATTN_QK, ATTN_WIDTH, REC_K, REC_K, REC_WIDTH,
             ATTN_QK, REC_K, REC_WIDTH, D_MODEL, D_MODEL)
CTX_KV_WIDTH = ATTN_QK + ATTN_WIDTH + 2 * REC_K + REC_WIDTH
IN_WIDTH = CTX_KV_WIDTH + ATTN_QK + REC_K + REC_WIDTH + 2 * D_MODEL

kernel_name = "hybrid_diffattn_hgrn2_convffn_dit"


def rms_norm(x, w):
    xf = x.astype(jnp.float32)
    y = xf * lax.rsqrt(jnp.mean(xf * xf, axis=-1, keepdims=True) + EPS)
    return y.astype(x.dtype) * w


def modulate(x, w, shift, scale):
    return rms_norm(x, w) * (1.0 + scale) + shift


def split_cols(z, n):
    out = []
    start = 0
    for width in IN_SPLITS[:n]:
        out.append(z[..., start:start + width])
        start += width
    return out


def to_qk_heads(a):
    return a.reshape(a.shape[0], a.shape[1], ATTN_HEADS, 2, ATTN_HEAD_DIM)


def to_v_heads(a):
    return a.reshape(a.shape[0], a.shape[1], ATTN_HEADS, ATTN_VDIM)


def to_rec_heads(a):
    return a.reshape(a.shape[0], a.shape[1], REC_HEADS, -1)


def axial_rope_tables(n_tokens, dtype):
    rows = n_tokens // GRID_W
    r, col = jnp.meshgrid(jnp.arange(rows), jnp.arange(GRID_W), indexing='ij')
    pos = jnp.stack([r.reshape(-1), col.reshape(-1)], axis=-1).astype(jnp.float32)
    nq = ATTN_HEAD_DIM // 4
    inv = ROPE_BASE ** (-jnp.arange(nq, dtype=jnp.float32) / nq)
    ang = pos[:, :, None] * inv
    return jnp.cos(ang).astype(dtype), jnp.sin(ang).astype(dtype)


def apply_axial_rope(x, cos, sin):
    B, S, H, C, d = x.shape
    xr = x.reshape(B, S, H, C, 2, 2, d // 4)
    x1, x2 = xr[..., 0, :], xr[..., 1, :]
    cs = cos[None, :, None, None]
    sn = sin[None, :, None, None]
    out = jnp.stack([x1 * cs - x2 * sn, x2 * cs + x1 * sn], axis=-2)
    return out.reshape(B, S, H, C, d)


def diff_attn_core(q, k, v, lam):
    s = jnp.einsum('bqhcd,bkhcd->bhcqk', q, k).astype(jnp.float32) * (ATTN_HEAD_DIM ** -0.5)
    p = jax.nn.softmax(s, axis=-1)
    a = p[:, :, 0] - lam * p[:, :, 1]
    return jnp.einsum('bhqk,bkhe->bqhe', a.astype(v.dtype), v)


def diff_attn_latent(q, k, v, lam):
    B, S, H, C, d = q.shape
    qb = q.reshape(B, S // Q_BLOCK, Q_BLOCK, H, C, d).transpose(1, 0, 2, 3, 4, 5)
    ob = lax.map(lambda blk: diff_attn_core(blk, k, v, lam), qb)
    return ob.transpose(1, 0, 2, 3, 4).reshape(B, S, H, v.shape[-1])


def diff_attn_readout(o, subln_w, lam_init):
    B, T = o.shape[0], o.shape[1]
    return (rms_norm(o, subln_w) * (1.0 - lam_init)).reshape(B, T, ATTN_WIDTH)


def rec_gate(f_raw, lower):
    f = lower + (1.0 - lower) * jax.nn.sigmoid(f_raw.astype(jnp.float32))
    return to_rec_heads(1.0 - f), to_rec_heads(jnp.log(f))


def gla_scan(q, k, v, logf, s0):
    B, T, H, _ = k.shape
    dv = v.shape[-1]
    nc = T // REC_CHUNK

    def chunks(a):
        return a.astype(jnp.float32).reshape(B, nc, REC_CHUNK, H, a.shape[-1]).transpose(1, 0, 3, 2, 4)

    mask = jnp.tril(jnp.ones((REC_CHUNK, REC_CHUNK), dtype=bool))[None, None, :, :, None]
    with_out = q is not None
    xs = (chunks(k), chunks(v), chunks(logf)) + ((chunks(q),) if with_out else ())

    def step(state, inp):
        kc, vc, gc = inp[0], inp[1], inp[2]
        b = jnp.cumsum(gc, axis=2)
        b_end = b[:, :, -1:, :]
        new_state = (jnp.exp(b_end)[:, :, 0, :, None] * state
                     + jnp.einsum('bhsk,bhsv->bhkv', kc * jnp.exp(b_end - b), vc))
        if not with_out:
            return new_state, None
        qc = inp[3]
        rel = jnp.exp(jnp.where(mask, b[:, :, :, None, :] - b[:, :, None, :, :], -jnp.inf))
        scores = jnp.einsum('bhtk,bhtsk,bhsk->bhts', qc, rel, kc)
        out = (jnp.einsum('bhts,bhsv->bhtv', scores, vc)
               + jnp.einsum('bhtk,bhkv->bhtv', qc * jnp.exp(b), state))
        return new_state, out

    state, out = lax.scan(step, s0.astype(jnp.float32), xs)
    if with_out:
        out = out.transpose(1, 0, 3, 2, 4).reshape(B, T, H, dv).astype(v.dtype)
    return state, out


def rec_direction(lat, ctx_feats, reverse):
    if reverse:
        flip = lambda a: None if a is None else jnp.flip(a, axis=1)
    else:
        flip = lambda a: a
    qc, kc, vc, gc = [flip(a) for a in ctx_feats]
    s0 = jnp.zeros((kc.shape[0], REC_HEADS, REC_KDIM, REC_VDIM), jnp.float32)
    s_ctx, o_ctx = gla_scan(qc, kc, vc, gc, s0)
    q, k, v, g = [flip(a) for a in lat]
    _, o_lat = gla_scan(q, k, v, g, s_ctx)
    return flip(o_lat), flip(o_ctx)


def rec_readout(o, g, w):
    B, T = o.shape[0], o.shape[1]
    return rms_norm(o.reshape(B, T, REC_WIDTH), w) * jax.nn.silu(g)


def merge_branches(att, rec, gate_a, gate_r, w_branch_attn, w_branch_rec, w_out):
    y = jax.nn.sigmoid(gate_a) * (att @ w_branch_attn) + jax.nn.sigmoid(gate_r) * (rec @ w_branch_rec)
    return y @ w_out


def conv_ffn(h, w_up, conv_w, conv_b, w_down):
    u = h @ w_up
    T = u.shape[1]
    pad = CONV_W // 2
    up = jnp.pad(u, ((0, 0), (pad, pad), (0, 0)))
    u = conv_b + sum(up[:, j:j + T] * conv_w[j] for j in range(CONV_W))
    a, b = jnp.split(u, 2, axis=-1)
    return (jax.nn.silu(a) * b) @ w_down


def hybrid_layer(x, ctx, mod, mod_c, lam, lam_init, lb_f, lb_b, norm1_w, w_in, subln_w, rec_gnorm_w,
                 w_branch_attn, w_branch_rec, w_out, norm2_w, w_up, conv_w, conv_b, w_down, ctx_out):
    B, S, _ = x.shape
    sh1, sc1, g1, sh2, sc2, g2 = jnp.split(mod[:, None, :], N_MOD, axis=-1)
    csh1, csc1, cg1, csh2, csc2, cg2 = jnp.split(mod_c, N_MOD, axis=-1)

    h = modulate(x, norm1_w, sh1, sc1)
    hc = modulate(ctx, norm1_w, csh1, csc1)
    ak, av, rff, rfb, ri, aq, rq, rg, gate_a, gate_r = split_cols(h @ w_in, len(IN_SPLITS))
    if ctx_out:
        akc, avc, rffc, rfbc, ric, aqc, rqc, rgc, gate_ac, gate_rc = split_cols(hc @ w_in, len(IN_SPLITS))
    else:
        akc, avc, rffc, rfbc, ric = split_cols(hc @ w_in[:, :CTX_KV_WIDTH], 5)

    cos, sin = axial_rope_tables(S, x.dtype)
    k_all = jnp.concatenate([apply_axial_rope(to_qk_heads(ak), cos, sin), to_qk_heads(akc)], axis=1)
    v_all = jnp.concatenate([to_v_heads(av), to_v_heads(avc)], axis=1)
    o_att = diff_attn_latent(apply_axial_rope(to_qk_heads(aq), cos, sin), k_all, v_all, lam)
    att = diff_attn_readout(o_att, subln_w, lam_init)

    v_r, vc_r = to_rec_heads(ri), to_rec_heads(ric)
    q_r = to_rec_heads(jax.nn.silu(rq))
    qc_r = to_rec_heads(jax.nn.silu(rqc)) if ctx_out else None
    kf, logf_f = rec_gate(rff, lb_f)
    kb, logf_b = rec_gate(rfb, lb_b)
    kfc, logfc_f = rec_gate(rffc, lb_f)
    kbc, logfc_b = rec_gate(rfbc, lb_b)
    o_f, oc_f = rec_direction((q_r, kf, v_r, logf_f), (qc_r, kfc, vc_r, logfc_f), False)
    o_b, oc_b = rec_direction((q_r, kb, v_r, logf_b), (qc_r, kbc, vc_r, logfc_b), True)
    rec = rec_readout(o_f + o_b, rg, rec_gnorm_w)

    x = x + g1 * merge_branches(att, rec, gate_a, gate_r, w_branch_attn, w_branch_rec, w_out)
    x = x + g2 * conv_ffn(modulate(x, norm2_w, sh2, sc2), w_up, conv_w, conv_b, w_down)

    if ctx_out:
        oc_att = diff_attn_core(to_qk_heads(aqc), to_qk_heads(akc), to_v_heads(avc), lam)
        att_c = diff_attn_readout(oc_att, subln_w, lam_init)
        rec_c = rec_readout(oc_f + oc_b, rgc, rec_gnorm_w)
        ctx = ctx + cg1 * merge_branches(att_c, rec_c, gate_ac, gate_rc, w_branch_attn, w_branch_rec, w_out)
        ctx = ctx + cg2 * conv_ffn(modulate(ctx, norm2_w, csh2, csc2), w_up, conv_w, conv_b, w_down)
    return x, ctx


def setup_inputs(seed: int = 0) -> dict:
    key = jax.random.key(seed)
    ks = jax.random.split(key, 24)
    f32 = jnp.float32

    def nrm(k, shape, fan_in):
        return jax.random.normal(k, shape, f32) * fan_in ** -0.5

    def gain(k, shape):
        return 1.0 + 0.02 * jax.random.normal(k, shape, f32)

    def small(k, shape, s):
        return s * jax.random.normal(k, shape, f32)

    return {
        "x": jax.random.normal(ks[0], (BATCH, SEQ, D_MODEL), f32),
        "c": jax.random.normal(ks[1], (BATCH, D_MODEL), f32),
        "ctx": jax.random.normal(ks[2], (BATCH, CTX_LEN, D_MODEL), f32),
        "c_ctx": jax.random.normal(ks[3], (D_MODEL,), f32),
        "w_mod": nrm(ks[4], (DEPTH, D_MODEL, N_MOD * D_MODEL), D_MODEL),
        "b_mod": small(ks[5], (DEPTH, N_MOD * D_MODEL), 0.02),
        "norm1_w": gain(ks[6], (DEPTH, D_MODEL)),
        "w_in": nrm(ks[7], (DEPTH, D_MODEL, IN_WIDTH), D_MODEL),
        "lam_q1": small(ks[8], (DEPTH, ATTN_HEAD_DIM), 0.1),
        "lam_k1": small(ks[9], (DEPTH, ATTN_HEAD_DIM), 0.1),
        "lam_q2": small(ks[10], (DEPTH, ATTN_HEAD_DIM), 0.1),
        "lam_k2": small(ks[11], (DEPTH, ATTN_HEAD_DIM), 0.1),
        "subln_w": gain(ks[12], (DEPTH, ATTN_VDIM)),
        "rec_lb": small(ks[13], (2, DEPTH + 1, REC_K), 0.5),
        "rec_gnorm_w": gain(ks[14], (DEPTH, REC_WIDTH)),
        "w_branch_attn": nrm(ks[15], (DEPTH, ATTN_WIDTH, D_MODEL), ATTN_WIDTH),
        "w_branch_rec": nrm(ks[16], (DEPTH, REC_WIDTH, D_MODEL), REC_WIDTH),
        "w_out": nrm(ks[17], (DEPTH, D_MODEL, D_MODEL), D_MODEL),
        "norm2_w": gain(ks[18], (DEPTH, D_MODEL)),
        "w_up": nrm(ks[19], (DEPTH, D_MODEL, 2 * FFN_DIM), D_MODEL),
        "conv_w": nrm(ks[20], (DEPTH, CONV_W, 2 * FFN_DIM), CONV_W),
        "conv_b": small(ks[21], (DEPTH, 2 * FFN_DIM), 0.02),
        "w_down": nrm(ks[22], (DEPTH, FFN_DIM, D_MODEL), FFN_DIM),
        "final_norm_w": gain(ks[23], (D_MODEL,)),
    }


def reference(x, c, ctx, c_ctx, w_mod, b_mod, norm1_w, w_in, lam_q1, lam_k1, lam_q2, lam_k2, subln_w,
              rec_lb, rec_gnorm_w, w_branch_attn, w_branch_rec, w_out, norm2_w, w_up, conv_w, conv_b,
              w_down, final_norm_w):
    lower = jnp.cumsum(jax.nn.softmax(rec_lb.astype(jnp.float32), axis=1), axis=1)
    for l in range(DEPTH):
        mod = jax.nn.silu(c) @ w_mod[l] + b_mod[l]
        mod_c = jax.nn.silu(c_ctx) @ w_mod[l] + b_mod[l]
        lam_init = 0.8 - 0.6 * math.exp(-0.3 * l)
        lam = (jnp.exp(jnp.sum(lam_q1[l].astype(jnp.float32) * lam_k1[l].astype(jnp.float32)))
               - jnp.exp(jnp.sum(lam_q2[l].astype(jnp.float32) * lam_k2[l].astype(jnp.float32)))
               + lam_init)
        x, ctx = hybrid_layer(x, ctx, mod, mod_c, lam, lam_init, lower[0, l], lower[1, l],
                              norm1_w[l], w_in[l], subln_w[l], rec_gnorm_w[l],
                              w_branch_attn[l], w_branch_rec[l], w_out[l], norm2_w[l],
                              w_up[l], conv_w[l], conv_b[l], w_down[l], l < DEPTH - 1)
    return rms_norm(x, final_norm_w)
```

```python
import contextlib
import numpy as np
import ml_dtypes
import concourse.bass as bass
import concourse.mybir as mybir
from concourse.bass_utils import run_bass_kernel_spmd

F32 = mybir.dt.float32
BF16 = mybir.dt.bfloat16
AF = mybir.ActivationFunctionType
ALU = mybir.AluOpType

D = 2048
T = 2048
TC = 256
TA = T + TC
NIN = 12288
FF = 5632
EPS = 1e-6
LAM_INIT = 0.2

V_C, V_CC, V_N1, V_N2, V_GN, V_LB, V_CW, V_CB = 0, 16, 32, 48, 64, 72, 104, 368
NV = 456


class Buf:
    __slots__ = ("w", "rs")

    def __init__(self):
        self.w = None
        self.rs = []


class Op:
    __slots__ = ("eng", "fn", "deps", "signaled", "val", "is_dma", "sem", "epoch")

    def __init__(self, eng, fn, is_dma, epoch):
        self.eng = eng
        self.fn = fn
        self.deps = []
        self.signaled = False
        self.val = None
        self.is_dma = is_dma
        self.sem = None
        self.epoch = epoch


class Sched:
    ENGS = ("pe", "act", "dve", "pool", "sp")
    NDMA = 8

    def __init__(self, nc, es):
        self.nc = nc
        self.csem = {e: es.enter_context(nc.semaphore("c_" + e)) for e in ("pe", "act", "dve", "pool")}
        self.dsem = {q: [es.enter_context(nc.semaphore("d_%s%d" % (q, i))) for i in range(self.NDMA)]
                     for q in ("sp", "pool", "act")}
        self.psem = es.enter_context(nc.semaphore("phase"))
        self.cval = {e: 0 for e in self.csem}
        self.dval = {q: [0] * self.NDMA for q in self.dsem}
        self.dlast = {q: [None] * self.NDMA for q in self.dsem}
        self.drr = {q: 0 for q in self.dsem}
        self.waited = {e: {} for e in self.ENGS}
        self.ops = {e: [] for e in self.ENGS}
        self.epoch = 0
        self.nops = 0

    def _add_dep(self, op, dep):
        if dep is None or dep is op or dep.epoch != self.epoch:
            return
        if dep.eng == "pe" and op.eng == "pe" and not dep.is_dma and not op.is_dma:
            return
        if dep not in op.deps:
            op.deps.append(dep)

    def op(self, eng, fn, reads=(), writes=()):
        o = Op(eng, fn, False, self.epoch)
        self._track(o, reads, writes)
        self.ops[eng].append(o)
        return o

    def dma(self, q, fn, reads=(), writes=()):
        o = Op(q, fn, True, self.epoch)
        self._track(o, reads, writes)
        self.ops[q].append(o)
        return o

    def _track(self, o, reads, writes):
        for b in reads:
            self._add_dep(o, b.w)
        for b in writes:
            for r in b.rs:
                self._add_dep(o, r)
            self._add_dep(o, b.w)
        for b in reads:
            b.rs.append(o)
        for b in writes:
            b.w = o
            b.rs = []

    def flush(self):
        nc = self.nc
        ops = self.ops
        fence = Op("sp", None, False, self.epoch)
        for e in self.ENGS:
            last = None
            for o in ops[e]:
                if o.is_dma:
                    fence.deps.append(o)
                else:
                    last = o
            if last is not None and e != "sp":
                fence.deps.append(last)
        ops["sp"].append(fence)
        for e in self.ENGS:
            for o in ops[e]:
                for d in o.deps:
                    d.signaled = True
        for e in self.ENGS:
            for o in ops[e]:
                if o.fn is None:
                    continue
                if o.is_dma:
                    q = o.eng
                    i = self.drr[q]
                    self.drr[q] = (i + 1) % self.NDMA
                    prev = self.dlast[q][i]
                    if prev is not None and prev.epoch == self.epoch:
                        o.deps.append(prev)
                    self.dval[q][i] += 16
                    o.sem = self.dsem[q][i]
                    o.val = self.dval[q][i]
                    self.dlast[q][i] = o
                elif o.signaled:
                    self.cval[e] += 1
                    o.sem = self.csem[e]
                    o.val = self.cval[e]
        waited = self.waited
        epoch = self.epoch
        psem = self.psem

        def emit(ename, eng):
            w = waited[ename]
            if epoch > 0:
                eng.wait_ge(psem, epoch)
            for o in ops[ename]:
                for d in o.deps:
                    k = d.sem.num
                    if w.get(k, 0) >= d.val:
                        continue
                    eng.wait_ge(d.sem, d.val)
                    w[k] = d.val
                if o.fn is None:
                    eng.sem_inc(psem, 1)
                    continue
                ins = o.fn(eng)
                self.nops += 1
                if o.is_dma:
                    ins.then_inc(o.sem, 16)
                elif o.signaled:
                    ins.then_inc(o.sem, 1)

        with nc.Block() as block:
            @block.sync
            def _(e):
                emit("sp", e)

            @block.gpsimd
            def _(e):
                emit("pool", e)

            @block.scalar
            def _(e):
                emit("act", e)

            @block.vector
            def _(e):
                emit("dve", e)

            @block.tensor
            def _(e):
                emit("pe", e)
        self.ops = {e: [] for e in self.ENGS}
        self.epoch += 1


def build(upto=99, debug=False):
    nc = bass.Bass("TRN2", target_bir_lowering=False)
    kscr = "ExternalOutput" if debug else "Internal"

    def din(name, shape, dt=F32):
        return nc.dram_tensor(name, list(shape), dt, kind="ExternalInput").ap()

    def dscr(name, shape, dt):
        return nc.dram_tensor(name, list(shape), dt, kind=kscr).ap()

    x_d = din("x", [T, D]); ctx_d = din("ctx", [TC, D]); vecs_d = din("vecs", [NV, 128])
    wmod_d = din("w_mod", [D, NIN]); bmod_d = din("b_mod2", [2, NIN]); win_d = din("w_in", [D, NIN])
    lam_d = din("lam4", [1, 256]); subln_d = din("subln", [1, 128])
    wba_d = din("w_ba", [1024, D]); wbr_d = din("w_br", [1024, D]); wout_d = din("w_out", [D, D])
    wup_d = din("w_up", [D, 2 * FF]); wdn_d = din("w_down", [FF, D]); fnw_d = din("fnw", [1, D])
    cos_d = din("cosT", [128, T]); sin_d = din("sinT", [128, T])
    identb_d = din("identb", [128, 128], BF16); identf_d = din("identf", [128, 128]); perm_d = din("perm", [128, 128], BF16)
    tri_d = din("tri", [64, 2 * 8 * 64], BF16); smask_d = din("smask", [128, TA])
    out_d = nc.dram_tensor("out", [T, D], F32, kind="ExternalOutput").ap()

    mod_s = dscr("mod_s", [2, NIN], F32)
    kT_s = dscr("kT_s", [8, 128, TA], BF16); qT_s = dscr("qT_s", [8, 128, T], BF16)
    v_s = dscr("v_s", [TA, 8 * 130], BF16); rf_s = dscr("rf_s", [2, 1024, TA], F32)
    rv_s = dscr("rv_s", [TA, 1024], BF16); rq_s = dscr("rq_s", [1024, T], BF16); rg_s = dscr("rg_s", [1024, T], BF16)
    ga_s = dscr("ga_s", [D, T], BF16); gr_s = dscr("gr_s", [D, T], BF16)
    attT_s = dscr("attT_s", [1024, T], BF16); recT_s = dscr("recT_s", [1024, T], BF16)
    yT_s = dscr("yT_s", [D, T], BF16); x1_s = dscr("x1_s", [T, D], F32); h2T_s = dscr("h2T_s", [D, T], BF16)
    x2_s = dscr("x2_s", [T, D], F32)
    B = {k: Buf() for k in ("mod_s", "kT_s", "qT_s", "v_s", "rf_s", "rv_s", "rq_s", "rg_s", "ga_s", "gr_s",
                            "attT_s", "recT_s", "yT_s", "x1_s", "h2T_s", "x2_s", "out")}

    with contextlib.ExitStack() as es:
        S = Sched(nc, es)

        uid = [0]

        def sb(st, name, shape, dt):
            uid[0] += 1
            return st.enter_context(nc.sbuf_tensor("s%d_%s" % (uid[0], name), list(shape), dt)), Buf()

        def ps(st, name, shape, dt=F32):
            uid[0] += 1
            return st.enter_context(nc.psum_tensor("p%d_%s" % (uid[0], name), list(shape), dt)), Buf()

        def load(q, dst, src, bdst, bsrc=None, slow=False):
            if slow:
                S.dma(q, lambda e: e.dma_start(out=dst, in_=src, allow_slow_non_contiguous=True), reads=[bsrc] if bsrc else [], writes=[bdst])
            else:
                S.dma(q, lambda e: e.dma_start(out=dst, in_=src), reads=[bsrc] if bsrc else [], writes=[bdst])

        def store(q, dst, src, bsrc, bdst=None):
            S.dma(q, lambda e: e.dma_start(out=dst, in_=src), reads=[bsrc], writes=[bdst] if bdst else [])

        fm, b_fm = sb(es, "fm", [128, NV], F32)
        modT, b_modT = sb(es, "modT", [128, 96, 2], F32)
        A1, b_A1 = sb(es, "A1", [128, 16, 2], F32)
        A2, b_A2 = sb(es, "A2", [128, 16], F32)
        nlam, b_nlam = sb(es, "nlam", [128, 1], F32)
        low, b_low = sb(es, "low", [128, 16], F32)
        oml, b_oml = sb(es, "oml", [128, 16], F32)
        slw, b_slw = sb(es, "slw", [128, 128], F32)
        identb, b_idb = sb(es, "identb", [128, 128], BF16)
        identf, b_idf = sb(es, "identf", [128, 128], F32)
        mhalf, b_mhalf = sb(es, "mhalf", [128, 1], F32)
        onesb, b_onesb = sb(es, "onesb", [128, 128], BF16)

        with contextlib.ExitStack() as st:
            vrow, b_vrow = sb(st, "vrow", [128, 4, 128], F32)
            scb, b_scb = sb(st, "scb", [128, 16, 2], BF16)
            bmod, b_bmod = sb(st, "bmod", [2, NIN], F32)
            modrow, b_modrow = sb(st, "modrow", [2, NIN], F32)
            wt = [sb(st, "wt%d" % i, [128, 16, 512], BF16) for i in range(2)]
            lamt, b_lamt = sb(st, "lamt", [128, 256], F32)
            lamp, b_lamp = sb(st, "lamp", [128, 2, 64], F32)
            lams, b_lams = sb(st, "lams", [128, 2], F32)
            tmp16, b_tmp16 = sb(st, "tmp16", [128, 16, 2], F32)
            pv, b_pv = ps(st, "pv", [128, 512], F32)
            pmm = [ps(st, "pmm%d" % i, [128, 512], F32) for i in range(2)]
            pmt, b_pmt = ps(st, "pmt", [128, 512], F32)

            load("sp", identb[:], identb_d[:, :], b_idb)
            load("sp", identf[:], identf_d[:, :], b_idf)
            S.op("pool", lambda e: e.memset(mhalf[:], -0.5), writes=[b_mhalf])
            S.op("pool", lambda e: e.memset(onesb[:], 1.0), writes=[b_onesb])
            nrows = [128, 128, 128, NV - 384]
            for i in range(4):
                load("sp", vrow[0:nrows[i], i, :], vecs_d[i * 128:i * 128 + nrows[i], :], b_vrow)
            for i in range(4):
                n = nrows[i]
                S.op("pe", lambda e, i=i, n=n: e.transpose(out=pv[:, 0:n], in_=vrow[0:n, i, :], identity=identf[0:n, 0:n]),
                     reads=[b_vrow, b_idf], writes=[b_pv])
                S.op("dve", lambda e, i=i, n=n: e.tensor_copy(out=fm[:, i * 128:i * 128 + n], in_=pv[:, 0:n]),
                     reads=[b_pv], writes=[b_fm])
            for j, off in enumerate((V_C, V_CC)):
                S.op("act", lambda e, j=j, off=off: e.activation(out=scb[:, :, j], in_=fm[:, off:off + 16], func=AF.Silu),
                     reads=[b_fm], writes=[b_scb])
            load("sp", bmod[:], bmod_d[:, :], b_bmod)
            wv = wmod_d.rearrange("(kc p) n -> p kc n", p=128)
            for g in range(24):
                s = g % 2
                w_t, b_w = wt[s]
                load("pool", w_t[:], wv[:, :, g * 512:(g + 1) * 512], b_w)
                pm, b_pm = pmm[s]
                for kc in range(16):
                    S.op("pe", lambda e, kc=kc, w_t=w_t, pm=pm: e.matmul(pm[0:2, :], lhsT=scb[:, kc, :], rhs=w_t[:, kc, :],
                                                                         start=(kc == 0), stop=(kc == 15)),
                         reads=[b_scb, b_w], writes=[b_pm])
                S.op("dve", lambda e, g=g, pm=pm: e.tensor_tensor(out=modrow[:, g * 512:(g + 1) * 512], in0=pm[0:2, :],
                                                                  in1=bmod[:, g * 512:(g + 1) * 512], op=ALU.add),
                     reads=[b_pm, b_bmod], writes=[b_modrow])
            store("sp", mod_s[:, :], modrow[:], b_modrow, B["mod_s"])
            for j in range(96):
                S.op("pe", lambda e, j=j: e.matmul(pmt[:, 2 * j:2 * j + 2], lhsT=modrow[:, j * 128:(j + 1) * 128],
                                                   rhs=identf[0:2, 0:2], start=True, stop=True),
                     reads=[b_modrow, b_idf], writes=[b_pmt])
            S.op("dve", lambda e: e.tensor_copy(out=modT[:].rearrange("p a b -> p (a b)"), in_=pmt[:, 0:192]),
                 reads=[b_pmt], writes=[b_modT])
            S.op("dve", lambda e: e.tensor_scalar(out=tmp16[:], in0=modT[:, 16:32, :], scalar1=1.0, scalar2=None, op0=ALU.add),
                 reads=[b_modT], writes=[b_tmp16])
            for j in range(2):
                S.op("dve", lambda e, j=j: e.tensor_tensor(out=A1[:, :, j], in0=tmp16[:, :, j], in1=fm[:, V_N1:V_N1 + 16], op=ALU.mult),
                     reads=[b_tmp16, b_fm], writes=[b_A1])
            S.op("dve", lambda e: e.tensor_scalar(out=tmp16[:, :, 0], in0=modT[:, 64:80, 0], scalar1=1.0, scalar2=None, op0=ALU.add),
                 reads=[b_modT, b_A1], writes=[b_tmp16])
            S.op("dve", lambda e: e.tensor_tensor(out=A2[:], in0=tmp16[:, :, 0], in1=fm[:, V_N2:V_N2 + 16], op=ALU.mult),
                 reads=[b_tmp16, b_fm], writes=[b_A2])
            load("sp", lamt[:], lam_d.partition_broadcast(128), b_lamt)
            S.op("dve", lambda e: e.tensor_tensor(out=lamp[:], in0=lamt[:].rearrange("p (a b c) -> p a b c", a=2, b=2)[:, :, 0, :],
                                                  in1=lamt[:].rearrange("p (a b c) -> p a b c", a=2, b=2)[:, :, 1, :], op=ALU.mult),
                 reads=[b_lamt], writes=[b_lamp])
            for j in range(2):
                S.op("dve", lambda e, j=j: e.reduce_sum(out=lams[:, j:j + 1], in_=lamp[:, j, :], axis=mybir.AxisListType.X),
                     reads=[b_lamp], writes=[b_lams])
            S.op("act", lambda e: e.activation(out=lams[:], in_=lams[:], func=AF.Exp), reads=[b_lams], writes=[b_lams])
            S.op("dve", lambda e: e.tensor_tensor(out=nlam[:], in0=lams[:, 1:2], in1=lams[:, 0:1], op=ALU.subtract),
                 reads=[b_lams], writes=[b_nlam])
            S.op("dve", lambda e: e.tensor_scalar(out=nlam[:], in0=nlam[:], scalar1=-LAM_INIT, scalar2=None, op0=ALU.add),
                 reads=[b_nlam], writes=[b_nlam])
            lbv = fm[:, V_LB:V_LB + 32].rearrange("p (d l h) -> p d l h", d=2, l=2)
            S.op("dve", lambda e: e.tensor_tensor(out=low[:].rearrange("p (d h) -> p d h", d=2), in0=lbv[:, :, 0, :], in1=lbv[:, :, 1, :],
                                                  op=ALU.subtract), reads=[b_fm], writes=[b_low])
            S.op("act", lambda e: e.activation(out=low[:], in_=low[:], func=AF.Sigmoid), reads=[b_low], writes=[b_low])
            S.op("dve", lambda e: e.tensor_scalar(out=oml[:], in0=low[:], scalar1=-1.0, scalar2=1.0, op0=ALU.mult, op1=ALU.add),
                 reads=[b_low], writes=[b_oml])
            load("sp", slw[:], subln_d.partition_broadcast(128), b_slw)
            S.op("dve", lambda e: e.tensor_scalar(out=slw[:], in0=slw[:], scalar1=1.0 - LAM_INIT, scalar2=None, op0=ALU.mult),
                 reads=[b_slw], writes=[b_slw])
            S.flush()
        if upto <= 0:
            return nc, S

        def norm_p1(xt, b_xt, junk, b_junk, ssq, b_ssq, xn, b_xn):
            S.op("act", lambda e: e.activation(out=junk[:], in_=xt, func=AF.Square), reads=[b_xt], writes=[b_junk])
            S.op("dve", lambda e: e.reduce_sum(out=ssq[:], in_=junk[:], axis=mybir.AxisListType.X), reads=[b_junk], writes=[b_ssq])
            S.op("pool", lambda e: e.tensor_scalar(out=ssq[:], in0=ssq[:], scalar1=1.0 / D, scalar2=EPS, op0=ALU.mult, op1=ALU.add),
                 reads=[b_ssq], writes=[b_ssq])
            S.op("pool", lambda e: e.tensor_tensor(out=ssq[:], in0=ssq[:], in1=mhalf[:], op=ALU.pow),
                 reads=[b_ssq, b_mhalf], writes=[b_ssq])
            S.op("dve", lambda e: e.tensor_scalar(out=xn[:], in0=xt, scalar1=ssq[:, 0:1], scalar2=None, op0=ALU.mult),
                 reads=[b_xt, b_ssq], writes=[b_xn])

        def norm_p2(xn, b_xn, pts, dst_fn, b_dst, Ascal, Bscal, bA, bB):
            for g in range(4):
                pt, b_pt = pts[g % 2]
                for j in range(4):
                    kc = g * 4 + j
                    S.op("pe", lambda e, kc=kc, j=j, pt=pt: e.transpose(out=pt[:, j * 128:(j + 1) * 128], in_=xn[:, kc * 128:(kc + 1) * 128],
                                                                        identity=identb[:]),
                         reads=[b_xn, b_idb], writes=[b_pt])
                for j in range(4):
                    kc = g * 4 + j
                    if False:
                        S.op("dve", lambda e, kc=kc, j=j, pt=pt: e.tensor_scalar(out=dst_fn(kc), in0=pt[:, j * 128:(j + 1) * 128],
                                                                                 scalar1=Ascal(kc), scalar2=Bscal(kc), op0=ALU.mult, op1=ALU.add),
                             reads=[b_pt, bA, bB], writes=[b_dst])
                    else:
                        S.op("act", lambda e, kc=kc, j=j, pt=pt: e.activation(out=dst_fn(kc), in_=pt[:, j * 128:(j + 1) * 128], func=AF.Identity,
                                                                              scale=Ascal(kc), bias=Bscal(kc)),
                             reads=[b_pt, bA, bB], writes=[b_dst])

        with contextlib.ExitStack() as st:
            hT, b_hT = sb(st, "hT", [128, 16, TA], BF16)
            with contextlib.ExitStack() as st2:
                xts = [sb(st2, "xt%d" % i, [128, D], F32) for i in range(2)]
                junk, b_junk = sb(st2, "junk", [128, D], BF16)
                ssqs = [sb(st2, "ssq%d" % i, [128, 1], F32) for i in range(2)]
                xns = [sb(st2, "xn%d" % i, [128, D], BF16) for i in range(2)]
                pts = [ps(st2, "pt%d" % i, [128, 1024], BF16) for i in range(2)]
                def p1a(tt):
                    s_ = tt % 2
                    xt, b_xt = xts[s_]
                    src = ctx_d[tt * 128:(tt + 1) * 128, :] if tt < 2 else x_d[(tt - 2) * 128:(tt - 1) * 128, :]
                    load("sp", xt[:], src, b_xt)
                    norm_p1(xt[:], b_xt, junk, b_junk, ssqs[s_][0], ssqs[s_][1], xns[s_][0], xns[s_][1])

                def p1b(tt):
                    s_ = tt % 2
                    jc = 1 if tt < 2 else 0
                    norm_p2(xns[s_][0], xns[s_][1], pts, lambda kc, tt=tt: hT[:, kc, tt * 128:(tt + 1) * 128], b_hT,
                            lambda kc, jc=jc: A1[:, kc, jc:jc + 1], lambda kc, jc=jc: modT[:, kc, jc:jc + 1], b_A1, b_modT)

                p1a(0)
                for tt in range(18):
                    if tt + 1 < 18:
                        p1a(tt + 1)
                    p1b(tt)
                S.flush()
            if upto <= 1:
                return nc, S
            wt = [sb(st, "wt%d" % i, [128, 16, 512], BF16) for i in range(2)]
            stf = [sb(st, "stf%d" % i, [128, TA], F32) for i in range(2)]
            stb = [sb(st, "stb%d" % i, [128, TA], BF16) for i in range(2)]
            cosT, b_cos = sb(st, "cosT", [128, T], F32)
            sinT, b_sin = sb(st, "sinT", [128, T], F32)
            perm, b_perm = sb(st, "perm", [128, 128], BF16)
            zbs = [sb(st, "zb%d" % i, [128, 512], BF16) for i in range(2)]
            t1s = [sb(st, "t1_%d" % i, [128, 512], F32) for i in range(2)]
            t2s = [sb(st, "t2_%d" % i, [128, 512], F32) for i in range(2)]
            vst = [sb(st, "vst%d" % i, [128, 4, 130], BF16) for i in range(2)]
            rst = [sb(st, "rst%d" % i, [128, 512], BF16) for i in range(2)]
            pms = [ps(st, "pm%d" % i, [128, 512], F32) for i in range(4)]
            pws = [ps(st, "pw%d" % i, [128, 512], F32) for i in range(2)]
            load("sp", cosT[:], cos_d[:, :], b_cos)
            load("sp", sinT[:], sin_d[:, :], b_sin)
            load("sp", perm[:], perm_d[:, :], b_perm)
            for i in range(2):
                S.op("pool", lambda e, i=i: e.memset(vst[i][0][:], 1.0), writes=[vst[i][1]])
            fams = ["ak"] * 2 + ["av"] * 2 + ["rf0"] * 2 + ["rf1"] * 2 + ["ri"] * 2 + ["aq"] * 2 + ["rq"] * 2 + ["rg"] * 2 + ["ga"] * 4 + ["gr"] * 4
            fstart = {}
            for g, f in enumerate(fams):
                fstart.setdefault(f, g)
            wv = win_d.rearrange("(kc p) n -> p kc n", p=128)
            ipm = 0
            irope = 0
            ist = 0
            ivs = 0
            import os
            for g in range(int(os.environ.get("K1B", "24"))):
                fam = fams[g]
                w_t, b_w = wt[g % 2]
                load("pool", w_t[:], wv[:, :, g * 512:(g + 1) * 512], b_w)
                has_ctx = g < 10
                if fam in ("av", "ri"):
                    for tt in range(18):
                        pm, b_pm = pms[ipm % 4]; ipm += 1
                        for kc in range(16):
                            S.op("pe", lambda e, kc=kc, tt=tt, pm=pm, w_t=w_t: e.matmul(pm[:], lhsT=hT[:, kc, tt * 128:(tt + 1) * 128], rhs=w_t[:, kc, :],
                                                                                         start=(kc == 0), stop=(kc == 15)),
                                 reads=[b_hT, b_w], writes=[b_pm])
                        if fam == "av":
                            v_t, b_v = vst[ivs % 2]; ivs += 1
                            S.op("dve", lambda e, pm=pm, v_t=v_t: e.tensor_copy(out=v_t[:, :, 0:128], in_=pm[:].rearrange("p (h e) -> p h e", h=4)),
                                 reads=[b_pm], writes=[b_v])
                            hh = (g - 2) * 4
                            store("sp", v_s[tt * 128:(tt + 1) * 128, hh * 130:(hh + 4) * 130], v_t[:].rearrange("p h e -> p (h e)"), b_v, B["v_s"])
                        else:
                            r_t, b_r = rst[ivs % 2]; ivs += 1
                            S.op("dve", lambda e, pm=pm, r_t=r_t: e.tensor_copy(out=r_t[:], in_=pm[:]), reads=[b_pm], writes=[b_r])
                            store("sp", rv_s[tt * 128:(tt + 1) * 128, (g - 8) * 512:(g - 7) * 512], r_t[:], b_r, B["rv_s"])
                    continue
                blocks = ([(0, 256)] if has_ctx else []) + [(256 + 512 * i, 512) for i in range(4)]
                for j in range(4):
                    fi = (g - fstart[fam]) * 4 + j
                    isf32 = fam in ("rf0", "rf1")
                    stg, b_stg = (stf if isf32 else stb)[ist % 2]; ist += 1
                    for (t0, n) in blocks:
                        c0 = t0 if has_ctx else t0 - 256
                        pm, b_pm = pms[ipm % 4]; ipm += 1
                        for kc in range(16):
                            S.op("pe", lambda e, kc=kc, j=j, t0=t0, n=n, pm=pm, w_t=w_t: e.matmul(pm[:, 0:n], lhsT=w_t[:, kc, j * 128:(j + 1) * 128],
                                                                                                 rhs=hT[:, kc, t0:t0 + n], start=(kc == 0), stop=(kc == 15)),
                                 reads=[b_hT, b_w], writes=[b_pm])
                        dst = stg[:, c0:c0 + n]
                        if fam in ("aq", "ak") and t0 >= 256 and os.environ.get("K1R", "1") == "1":
                            zb, b_zb = zbs[irope % 2]; t1, b_t1 = t1s[irope % 2]; t2, b_t2 = t2s[irope % 2]
                            pw, b_pw = pws[irope % 2]; irope += 1
                            s0 = t0 - 256
                            S.op("act", lambda e, pm=pm, zb=zb: e.activation(out=zb[:], in_=pm[:], func=AF.Copy), reads=[b_pm], writes=[b_zb])
                            S.op("pe", lambda e, pw=pw, zb=zb: e.matmul(pw[:], lhsT=perm[:], rhs=zb[:], start=True, stop=True),
                                 reads=[b_perm, b_zb], writes=[b_pw])
                            S.op("dve", lambda e, pm=pm, t1=t1, s0=s0: e.tensor_tensor(out=t1[:], in0=pm[:], in1=cosT[:, s0:s0 + 512], op=ALU.mult),
                                 reads=[b_pm, b_cos, b_zb, b_pw], writes=[b_t1])
                            S.op("dve", lambda e, pw=pw, t2=t2, s0=s0: e.tensor_tensor(out=t2[:], in0=pw[:], in1=sinT[:, s0:s0 + 512], op=ALU.mult),
                                 reads=[b_pw, b_sin], writes=[b_t2])
                            S.op("dve", lambda e, t1=t1, t2=t2, dst=dst: e.tensor_tensor(out=dst, in0=t1[:], in1=t2[:], op=ALU.add),
                                 reads=[b_t1, b_t2], writes=[b_stg])
                        else:
                            func = {"ak": AF.Copy, "aq": AF.Copy, "rf0": AF.Copy, "rf1": AF.Copy, "rq": AF.Silu, "rg": AF.Silu, "ga": AF.Sigmoid, "gr": AF.Sigmoid}[fam]
                            S.op("act", lambda e, pm=pm, n=n, dst=dst, func=func: e.activation(out=dst, in_=pm[:, 0:n], func=func),
                                 reads=[b_pm], writes=[b_stg])
                    ncol = TA if has_ctx else T
                    if fam == "ak":
                        dd, bd = kT_s[fi], B["kT_s"]
                    elif fam == "aq":
                        dd, bd = qT_s[fi], B["qT_s"]
                    elif isf32:
                        dd, bd = rf_s[int(fam[2]), fi * 128:(fi + 1) * 128, :], B["rf_s"]
                    else:
                        scr = {"rq": rq_s, "rg": rg_s, "ga": ga_s, "gr": gr_s}[fam]
                        dd, bd = scr[fi * 128:(fi + 1) * 128, :], B[fam + "_s"]
                    store("sp", dd, stg[:, 0:ncol], b_stg, bd)
            S.flush()
        if upto <= 2:
            return nc, S

        with contextlib.ExitStack() as st:
            vaug, b_vaug = sb(st, "vaug", [128, 18, 8 * 130], BF16)
            kTs = [sb(st, "kT%d" % i, [128, TA], BF16) for i in range(2)]
            qTs = [sb(st, "qT%d" % i, [128, T], BF16) for i in range(2)]
            eTs = [sb(st, "eT%d" % i, [128, 1024], BF16) for i in range(3)]
            attst = [sb(st, "attst%d" % i, [128, T], BF16) for i in range(2)]
            o_t = [sb(st, "o_t%d" % i, [128, 128], F32) for i in range(2)]
            t_t = [sb(st, "t_t%d" % i, [128, 128], F32) for i in range(2)]
            a_t = [sb(st, "a_t%d" % i, [128, 128], BF16) for i in range(2)]
            jk, b_jk = sb(st, "jk", [128, 128], F32)
            sms = [sb(st, "sm%d" % i, [128, 4], F32) for i in range(2)]
            scs = [ps(st, "sc%d" % i, [128, 1024], F32) for i in range(2)]
            accs = [ps(st, "acc%d" % i, [128, 512], F32) for i in range(3)]
            b_acc = [Buf() for _ in range(8)]
            pT, b_pT = ps(st, "pT", [128, 1024], BF16)

            def accap(idx, lo, hi):
                return accs[idx // 3][0][:, (idx % 3) * 130 + lo:(idx % 3) * 130 + hi]

            load("sp", vaug[:], v_s.rearrange("(t p) f -> p t f", p=128), b_vaug, B["v_s"])

            def head_loads(h):
                load("sp", kTs[h % 2][0][:], kT_s[h], kTs[h % 2][1], B["kT_s"])
                load("sp", qTs[h % 2][0][:], qT_s[h], qTs[h % 2][1], B["qT_s"])

            steps = [(h, qb, kt) for h in range(8) for qb in range(4) for kt in range(18)]

            def emit_scores(i):
                h, qb, kt = steps[i]
                kT, b_kT = kTs[h % 2]; qT, b_qT = qTs[h % 2]
                sc, b_sc = scs[i % 2]; eT, b_eT = eTs[i % 3]
                for c in range(2):
                    S.op("pe", lambda e, c=c, kt=kt, qb=qb, sc=sc, kT=kT, qT=qT: e.matmul(
                        sc[:, c * 512:(c + 1) * 512], lhsT=kT[c * 64:(c + 1) * 64, kt * 128:(kt + 1) * 128],
                        rhs=qT[c * 64:(c + 1) * 64, qb * 512:(qb + 1) * 512], start=True, stop=True),
                        reads=[b_kT, b_qT], writes=[b_sc])
                S.op("act", lambda e, sc=sc, eT=eT: e.activation(out=eT[:], in_=sc[:], func=AF.Exp, scale=0.125),
                     reads=[b_sc], writes=[b_eT])

            def emit_pv(i):
                h, qb, kt = steps[i]
                eT, b_eT = eTs[i % 3]
                for c in range(2):
                    for qt in range(4):
                        idx = c * 4 + qt
                        first = (kt == 0 and idx % 3 == 0)
                        S.op("pe", lambda e, idx=idx, c=c, qt=qt, kt=kt, h=h, eT=eT, first=first: e.matmul(
                            accap(idx, 0, 129), lhsT=eT[:, c * 512 + qt * 128:c * 512 + (qt + 1) * 128],
                            rhs=vaug[:, kt, h * 130:h * 130 + 129], start=first, stop=(kt == 17)),
                            reads=[b_eT, b_vaug], writes=[b_acc[idx]] + ([b_acc[j] for j in range(idx, min(idx + 3, 8))] if first else []))

            def emit_norm(h, qb):
                ast, b_ast = attst[h % 2]
                for qt in range(4):
                    sm, b_sm = sms[qt % 2]; o, b_o = o_t[qt % 2]; tq, b_tq = t_t[qt % 2]; at, b_at = a_t[qt % 2]
                    S.op("dve", lambda e, qt=qt, sm=sm: e.reciprocal(out=sm[:, 0:1], in_=accap(qt, 128, 129)),
                         reads=[b_acc[qt]], writes=[b_sm])
                    S.op("dve", lambda e, qt=qt, sm=sm: e.reciprocal(out=sm[:, 1:2], in_=accap(4 + qt, 128, 129)),
                         reads=[b_acc[4 + qt], b_sm], writes=[b_sm])
                    S.op("dve", lambda e, sm=sm: e.tensor_tensor(out=sm[:, 1:2], in0=sm[:, 1:2], in1=nlam[:], op=ALU.mult),
                         reads=[b_sm, b_nlam], writes=[b_sm])
                    S.op("dve", lambda e, qt=qt, sm=sm, tq=tq: e.tensor_scalar(out=tq[:], in0=accap(4 + qt, 0, 128), scalar1=sm[:, 1:2],
                                                                                scalar2=None, op0=ALU.mult),
                         reads=[b_acc[4 + qt], b_sm], writes=[b_tq])
                    S.op("dve", lambda e, qt=qt, sm=sm, tq=tq, o=o: e.scalar_tensor_tensor(out=o[:], in0=accap(qt, 0, 128), scalar=sm[:, 0:1],
                                                                                            in1=tq[:], op0=ALU.mult, op1=ALU.add),
                         reads=[b_acc[qt], b_sm, b_tq], writes=[b_o])
                    S.op("pool", lambda e, o=o: e.tensor_tensor(out=jk[:], in0=o[:], in1=o[:], op=ALU.mult), reads=[b_o], writes=[b_jk])
                    S.op("dve", lambda e, sm=sm: e.reduce_sum(out=sm[:, 2:3], in_=jk[:], axis=mybir.AxisListType.X), reads=[b_jk, b_sm], writes=[b_sm])
                    S.op("pool", lambda e, sm=sm: e.tensor_scalar(out=sm[:, 2:3], in0=sm[:, 2:3], scalar1=1.0 / 128, scalar2=EPS,
                                                                   op0=ALU.mult, op1=ALU.add), reads=[b_sm], writes=[b_sm])
                    S.op("pool", lambda e, sm=sm: e.tensor_tensor(out=sm[:, 3:4], in0=sm[:, 2:3], in1=mhalf[:], op=ALU.pow),
                         reads=[b_sm, b_mhalf], writes=[b_sm])
                    S.op("dve", lambda e, o=o, sm=sm, at=at: e.scalar_tensor_tensor(out=at[:], in0=o[:], scalar=sm[:, 3:4], in1=slw[:],
                                                                                     op0=ALU.mult, op1=ALU.mult),
                         reads=[b_o, b_sm, b_slw], writes=[b_at])
                    S.op("pe", lambda e, qt=qt, at=at: e.transpose(out=pT[:, qt * 128:(qt + 1) * 128], in_=at[:], identity=identb[:]),
                         reads=[b_at, b_idb], writes=[b_pT])
                S.op("dve", lambda e, qb=qb, ast=ast: e.tensor_copy(out=ast[:, qb * 512:(qb + 1) * 512], in_=pT[:, 0:512]),
                     reads=[b_pT], writes=[b_ast])
                if qb == 3:
                    store("sp", attT_s[h * 128:(h + 1) * 128, :], ast[:], b_ast, B["attT_s"])

            head_loads(0)
            head_loads(1)
            emit_scores(0)
            for i, (h, qb, kt) in enumerate(steps):
                if i + 1 < len(steps):
                    emit_scores(i + 1)
                emit_pv(i)
                if kt == 17:
                    emit_norm(h, qb)
                    if qb == 3 and h + 2 < 8:
                        head_loads(h + 2)
            S.flush()
        if upto <= 3:
            return nc, S

        with contextlib.ExitStack() as st:
            o_all, b_oall = sb(st, "o_all", [128, 8, T], BF16)
            smask, b_smask = sb(st, "smask", [128, TA], F32)
            tri, b_tri = sb(st, "tri", [64, 2 * 8 * 64], BF16)
            frs = [sb(st, "fr%d" % i, [128, TA], F32) for i in range(2)]
            T1, b_T1 = sb(st, "T1", [128, TA], F32)
            T2, b_T2 = sb(st, "T2", [128, TA], F32)
            T3, b_T3 = sb(st, "T3", [128, TA], F32)
            rq, b_rq = sb(st, "rq", [128, T], BF16)
            v64, b_v64 = sb(st, "v64", [64, 36, 128], BF16)
            QT = [sb(st, "QT%d" % i, [128, T], BF16) for i in range(2)]
            Q2 = [sb(st, "Q2%d" % i, [128, T], BF16) for i in range(2)]
            KT = [sb(st, "KT%d" % i, [128, TA], BF16) for i in range(2)]
            K2 = [sb(st, "K2%d" % i, [128, TA], BF16) for i in range(2)]
            K2tok = [sb(st, "K2tok%d" % i, [64, 36, 128], BF16) for i in range(2)]
            msc = [sb(st, "msc%d" % i, [64, 32 * 64], BF16) for i in range(2)]
            o_d = [sb(st, "o_d%d" % i, [128, T], F32) for i in range(2)]
            state = [sb(st, "state%d" % i, [128, 128], F32) for i in range(2)]
            statebf = [sb(st, "statebf%d" % i, [128, 128], BF16) for i in range(2)]
            dec = [sb(st, "dec%d" % i, [128, 36], F32) for i in range(2)]
            pk, b_pk = ps(st, "pk", [128, 1024], BF16)
            psc, b_psc = ps(st, "psc", [128, 512], F32)
            pout = [[ps(st, "pout%d%d" % (d, i), [128, 512], F32) for i in range(2)] for d in range(2)]
            pupd = [ps(st, "pupd%d" % d, [128, 512], F32) for d in range(2)]
            load("sp", smask[:], smask_d[:, :], b_smask)
            load("sp", tri[:], tri_d[:, :], b_tri)
            v3 = lambda t: t[:].rearrange("p (c t) -> p c t", t=64)
            rvv = rv_s.rearrange("(c p) f -> p c f", p=64)
            for h in range(8):
                for d in range(2):
                    load("sp", frs[d][0][:], rf_s[d, h * 128:(h + 1) * 128, :], frs[d][1], B["rf_s"])
                load("sp", rq[:], rq_s[h * 128:(h + 1) * 128, :], b_rq, B["rq_s"])
                load("sp", v64[:], rvv[:, :, h * 128:(h + 1) * 128], b_v64, B["rv_s"])
                for d in range(2):
                    Ft, b_F = frs[d]
                    col = d * 8 + h
                    S.op("act", lambda e, Ft=Ft: e.activation(out=Ft[:], in_=Ft[:], func=AF.Sigmoid), reads=[b_F], writes=[b_F])
                    S.op("dve", lambda e, Ft=Ft, col=col: e.tensor_scalar(out=Ft[:], in0=Ft[:], scalar1=oml[:, col:col + 1], scalar2=low[:, col:col + 1],
                                                                           op0=ALU.mult, op1=ALU.add), reads=[b_F, b_oml, b_low], writes=[b_F])
                    S.op("act", lambda e, Ft=Ft: e.activation(out=T1[:], in_=Ft[:], func=AF.Ln), reads=[b_F], writes=[b_T1])
                    S.op("dve", lambda e, Ft=Ft: e.tensor_scalar(out=Ft[:], in0=Ft[:], scalar1=-1.0, scalar2=1.0, op0=ALU.mult, op1=ALU.add),
                         reads=[b_F, b_T1], writes=[b_F])
                    S.op("dve", lambda e: e.tensor_tensor_scan(out=T2[:], data0=smask[:], data1=T1[:], initial=0.0, op0=ALU.mult, op1=ALU.add),
                         reads=[b_smask, b_T1], writes=[b_T2])
                    if d == 0:
                        Tb, b_Tb, Tf, b_Tf = T2, b_T2, T1, b_T1
                        refi, endi = 31, 63
                    else:
                        S.op("pool", lambda e: e.tensor_tensor(out=T1[:], in0=T1[:], in1=T2[:], op=ALU.subtract), reads=[b_T1, b_T2], writes=[b_T1])
                        S.op("pool", lambda e: e.tensor_tensor(out=v3(T1), in0=v3(T1), in1=v3(T2)[:, :, 63:64].to_broadcast([128, 36, 64]), op=ALU.add),
                             reads=[b_T1, b_T2], writes=[b_T1])
                        Tb, b_Tb, Tf, b_Tf = T1, b_T1, T2, b_T2
                        refi, endi = 32, 0
                    S.op("pool", lambda e, Tb=Tb, Tf=Tf, refi=refi: e.tensor_tensor(out=v3(Tf), in0=v3(Tb), in1=v3(Tb)[:, :, refi:refi + 1].to_broadcast([128, 36, 64]),
                                                                                      op=ALU.subtract), reads=[b_Tb, b_Tf], writes=[b_Tf])
                    S.op("act", lambda e, Tf=Tf: e.activation(out=T3[:], in_=Tf[:], func=AF.Exp), reads=[b_Tf], writes=[b_T3])
                    S.op("dve", lambda e, d=d: e.tensor_tensor(out=QT[d][0][:], in0=rq[:], in1=T3[:, TC:TA], op=ALU.mult), reads=[b_rq, b_T3], writes=[QT[d][1]])
                    S.op("act", lambda e, Tf=Tf: e.activation(out=T3[:], in_=Tf[:], func=AF.Exp, scale=-1.0), reads=[b_Tf, QT[d][1]], writes=[b_T3])
                    S.op("dve", lambda e, d=d, Ft=Ft: e.tensor_tensor(out=KT[d][0][:], in0=Ft[:], in1=T3[:], op=ALU.mult), reads=[b_F, b_T3], writes=[KT[d][1]])
                    S.op("act", lambda e, d=d, Tb=Tb, endi=endi: e.activation(out=dec[d][0][:], in_=v3(Tb)[:, :, endi], func=AF.Exp), reads=[b_Tb], writes=[dec[d][1]])
                    S.op("pool", lambda e, Tb=Tb, Tf=Tf, endi=endi: e.tensor_tensor(out=v3(Tf), in0=v3(Tb), in1=v3(Tb)[:, :, endi:endi + 1].to_broadcast([128, 36, 64]),
                                                                                      op=ALU.subtract), reads=[b_Tb, b_Tf, b_T3], writes=[b_Tf])
                    S.op("act", lambda e, Tf=Tf: e.activation(out=T3[:], in_=Tf[:], func=AF.Exp, scale=-1.0), reads=[b_Tf, KT[d][1]], writes=[b_T3])
                    S.op("dve", lambda e, d=d, Ft=Ft: e.tensor_tensor(out=K2[d][0][:], in0=Ft[:], in1=T3[:], op=ALU.mult), reads=[b_F, b_T3], writes=[K2[d][1]])
                    S.op("act", lambda e, Tb=Tb: e.activation(out=T3[:], in_=Tb[:], func=AF.Exp), reads=[b_Tb, K2[d][1]], writes=[b_T3])
                    S.op("dve", lambda e, d=d: e.tensor_tensor(out=Q2[d][0][:], in0=rq[:], in1=T3[:, TC:TA], op=ALU.mult), reads=[b_rq, b_T3], writes=[Q2[d][1]])
                    for c0 in range(0, 36, 8):
                        n = min(8, 36 - c0)
                        for cc in range(n):
                            c = c0 + cc
                            S.op("pe", lambda e, d=d, c=c, cc=cc: e.transpose(out=pk[0:64, cc * 128:(cc + 1) * 128], in_=K2[d][0][:, c * 64:(c + 1) * 64], identity=identb[:]),
                                 reads=[K2[d][1], b_idb], writes=[b_pk])
                        S.op("act", lambda e, d=d, c0=c0, n=n: e.activation(out=K2tok[d][0][:, c0:c0 + n, :].rearrange("p c k -> p (c k)"), in_=pk[0:64, 0:n * 128], func=AF.Copy),
                             reads=[b_pk], writes=[K2tok[d][1]])
                    for l0 in range(0, 32, 8):
                        for cc in range(8):
                            lc = l0 + cc
                            c = 4 + lc
                            S.op("pe", lambda e, d=d, c=c, lc=lc, cc=cc: e.matmul(psc[0:64, cc * 64:(cc + 1) * 64], lhsT=KT[d][0][:, c * 64:(c + 1) * 64],
                                                                                   rhs=QT[d][0][:, lc * 64:(lc + 1) * 64], start=True, stop=True),
                                 reads=[KT[d][1], QT[d][1]], writes=[b_psc])
                        S.op("dve", lambda e, d=d, l0=l0: e.tensor_tensor(out=msc[d][0][:, l0 * 64:(l0 + 8) * 64], in0=psc[0:64, :], in1=tri[:, d * 512:(d + 1) * 512], op=ALU.mult),
                             reads=[b_psc, b_tri], writes=[msc[d][1]])
                for d in range(2):
                    S.op("pool", lambda e, d=d: e.memset(state[d][0][:], 0.0), writes=[state[d][1]])
                    S.op("pool", lambda e, d=d: e.memset(statebf[d][0][:], 0.0), writes=[statebf[d][1]])
                order = [list(range(36)), [3, 2, 1, 0] + list(range(35, 3, -1))]
                for step in range(36):
                    for d in range(2):
                        c = order[d][step]
                        if c >= 4:
                            lc = c - 4
                            grp = lc // 8
                            po, b_po = pout[d][grp % 2]
                            slot = lc % 8
                            S.op("pe", lambda e, d=d, c=c, lc=lc, slot=slot, po=po: e.matmul(po[:, slot * 64:(slot + 1) * 64], lhsT=v64[:, c, :],
                                                                                           rhs=msc[d][0][:, lc * 64:(lc + 1) * 64], start=True, stop=False),
                                 reads=[b_v64, msc[d][1]], writes=[b_po])
                            S.op("pe", lambda e, d=d, lc=lc, slot=slot, po=po: e.matmul(po[:, slot * 64:(slot + 1) * 64], lhsT=statebf[d][0][:],
                                                                                      rhs=Q2[d][0][:, lc * 64:(lc + 1) * 64], start=False, stop=True),
                                 reads=[statebf[d][1], Q2[d][1]], writes=[b_po])
                            last_in_grp = (slot == 7) if d == 0 else (slot == 0)
                            if last_in_grp:
                                S.op("act", lambda e, d=d, grp=grp, po=po: e.activation(out=o_d[d][0][:, grp * 512:(grp + 1) * 512], in_=po[:], func=AF.Copy),
                                     reads=[b_po], writes=[o_d[d][1]])
                        if step == 35:
                            continue
                        pu, b_pu = pupd[d]
                        S.op("pe", lambda e, d=d, c=c, pu=pu: e.matmul(pu[:, 0:128], lhsT=K2tok[d][0][:, c, :], rhs=v64[:, c, :], start=True, stop=True),
                             reads=[K2tok[d][1], b_v64], writes=[b_pu])
                        S.op("dve", lambda e, d=d, c=c, pu=pu: e.scalar_tensor_tensor(out=state[d][0][:], in0=state[d][0][:], scalar=dec[d][0][:, c:c + 1], in1=pu[:, 0:128],
                                                                                       op0=ALU.mult, op1=ALU.add),
                             reads=[state[d][1], dec[d][1], b_pu], writes=[state[d][1]])
                        S.op("act", lambda e, d=d: e.activation(out=statebf[d][0][:], in_=state[d][0][:], func=AF.Copy),
                             reads=[state[d][1]], writes=[statebf[d][1]])
                S.op("pool", lambda e, h=h: e.tensor_tensor(out=o_all[:, h, :], in0=o_d[0][0][:], in1=o_d[1][0][:], op=ALU.add),
                     reads=[o_d[0][1], o_d[1][1]], writes=[b_oall])
            pss = [pout[0][0], pout[0][1], pout[1][0], pout[1][1]]
            sq, b_sq = frs[0]
            rstd, b_rstd = T1, b_T1
            mh2, b_mh2 = T2, b_T2
            rgt, b_rgt = rq, b_rq
            S.op("pool", lambda e: e.memset(mh2[:, 0:T], -0.5), reads=[], writes=[b_mh2])
            sqb = sq[:].bitcast(BF16)
            for h in range(8):
                S.op("dve", lambda e, h=h: e.tensor_tensor(out=sqb[:, 0:T], in0=o_all[:, h, :], in1=o_all[:, h, :], op=ALU.mult),
                     reads=[b_oall], writes=[b_sq])
                for tb in range(4):
                    S.op("pe", lambda e, h=h, tb=tb: e.matmul(pss[tb][0][:], lhsT=onesb[:], rhs=sqb[:, tb * 512:(tb + 1) * 512], start=(h == 0), stop=(h == 7)),
                         reads=[b_sq, b_onesb], writes=[pss[tb][1]])
            for tb in range(4):
                S.op("dve", lambda e, tb=tb: e.tensor_scalar(out=rstd[:, tb * 512:(tb + 1) * 512], in0=pss[tb][0][:], scalar1=1.0 / 1024, scalar2=EPS, op0=ALU.mult, op1=ALU.add),
                     reads=[pss[tb][1]], writes=[b_rstd])
            S.op("pool", lambda e: e.tensor_tensor(out=rstd[:, 0:T], in0=rstd[:, 0:T], in1=mh2[:, 0:T], op=ALU.pow), reads=[b_rstd, b_mh2], writes=[b_rstd])
            rec32, b_rec32 = T3, b_T3
            for h in range(8):
                rst_, b_rst_ = QT[h % 2]
                load("sp", rgt[:], rg_s[h * 128:(h + 1) * 128, :], b_rgt, B["rg_s"])
                S.op("dve", lambda e, h=h: e.scalar_tensor_tensor(out=rec32[:, 0:T], in0=o_all[:, h, :], scalar=fm[:, V_GN + h:V_GN + h + 1], in1=rstd[:, 0:T],
                                                                   op0=ALU.mult, op1=ALU.mult), reads=[b_oall, b_fm, b_rstd], writes=[b_rec32])
                S.op("pool", lambda e, rst_=rst_: e.tensor_tensor(out=rst_[:], in0=rec32[:, 0:T], in1=rgt[:], op=ALU.mult), reads=[b_rec32, b_rgt], writes=[b_rst_])
                store("sp", recT_s[h * 128:(h + 1) * 128, :], rst_[:], b_rst_, B["recT_s"])
            S.flush()
        if upto <= 4:
            return nc, S

        with contextlib.ExitStack() as st:
            wba, b_wba = sb(st, "wba", [128, 8, D], BF16)
            wbr, b_wbr = sb(st, "wbr", [128, 8, D], BF16)
            load("pool", wba[:], wba_d.rearrange("(kc p) n -> p kc n", p=128), b_wba)
            load("pool", wbr[:], wbr_d.rearrange("(kc p) n -> p kc n", p=128), b_wbr)
            attb = [sb(st, "attb%d" % i, [128, 8, 256], BF16) for i in range(2)]
            recb = [sb(st, "recb%d" % i, [128, 8, 256], BF16) for i in range(2)]
            gab = [sb(st, "gab%d" % i, [128, 16, 256], BF16) for i in range(2)]
            grb = [sb(st, "grb%d" % i, [128, 16, 256], BF16) for i in range(2)]
            yTb = [sb(st, "yTb%d" % i, [128, 16, 256], BF16) for i in range(2)]
            tas = [sb(st, "ta%d" % i, [128, 256], F32) for i in range(2)]
            trs = [sb(st, "tr%d" % i, [128, 256], F32) for i in range(2)]
            pas = [ps(st, "pa%d" % i, [128, 512], F32) for i in range(2)]
            prs = [ps(st, "pr%d" % i, [128, 512], F32) for i in range(2)]
            attv = attT_s.rearrange("(kc p) t -> p kc t", p=128)
            recv = recT_s.rearrange("(kc p) t -> p kc t", p=128)
            gav = ga_s.rearrange("(kc p) t -> p kc t", p=128)
            grv = gr_s.rearrange("(kc p) t -> p kc t", p=128)
            yTv = yT_s.rearrange("(kc p) t -> p kc t", p=128)
            for tb in range(8):
                s_ = tb % 2
                t0 = tb * 256
                load("sp", attb[s_][0][:], attv[:, :, t0:t0 + 256], attb[s_][1], B["attT_s"])
                load("sp", recb[s_][0][:], recv[:, :, t0:t0 + 256], recb[s_][1], B["recT_s"])
                load("sp", gab[s_][0][:], gav[:, :, t0:t0 + 256], gab[s_][1], B["ga_s"])
                load("sp", grb[s_][0][:], grv[:, :, t0:t0 + 256], grb[s_][1], B["gr_s"])
                for dc in range(16):
                    pa, b_pa = pas[dc % 2]; pr, b_pr = prs[dc % 2]
                    ta, b_ta = tas[dc % 2]; tr, b_tr = trs[dc % 2]
                    for kc in range(8):
                        S.op("pe", lambda e, kc=kc, dc=dc, pa=pa, s_=s_: e.matmul(pa[:, 0:256], lhsT=wba[:, kc, dc * 128:(dc + 1) * 128], rhs=attb[s_][0][:, kc, :],
                                                                                start=(kc == 0), stop=(kc == 7)), reads=[b_wba, attb[s_][1]], writes=[b_pa])
                    for kc in range(8):
                        S.op("pe", lambda e, kc=kc, dc=dc, pr=pr, s_=s_: e.matmul(pr[:, 0:256], lhsT=wbr[:, kc, dc * 128:(dc + 1) * 128], rhs=recb[s_][0][:, kc, :],
                                                                                start=(kc == 0), stop=(kc == 7)), reads=[b_wbr, recb[s_][1]], writes=[b_pr])
                    S.op("dve", lambda e, dc=dc, pa=pa, ta=ta, s_=s_: e.tensor_tensor(out=ta[:], in0=pa[:, 0:256], in1=gab[s_][0][:, dc, :], op=ALU.mult),
                         reads=[b_pa, gab[s_][1]], writes=[b_ta])
                    S.op("dve", lambda e, dc=dc, pr=pr, tr=tr, s_=s_: e.tensor_tensor(out=tr[:], in0=pr[:, 0:256], in1=grb[s_][0][:, dc, :], op=ALU.mult),
                         reads=[b_pr, grb[s_][1]], writes=[b_tr])
                    S.op("dve", lambda e, dc=dc, ta=ta, tr=tr, s_=s_: e.tensor_tensor(out=yTb[s_][0][:, dc, :], in0=ta[:], in1=tr[:], op=ALU.add),
                         reads=[b_ta, b_tr], writes=[yTb[s_][1]])
                store("sp", yTv[:, :, t0:t0 + 256], yTb[s_][0][:], yTb[s_][1], B["yT_s"])
            S.flush()
        if upto <= 5:
            return nc, S

        with contextlib.ExitStack() as st:
            wout, b_wout = sb(st, "wout", [128, 16, D], BF16)
            wov = wout_d.rearrange("(kc p) n -> p kc n", p=128)
            load("pool", wout[:, 0:8, :], wov[:, 0:8, :], b_wout)
            load("pool", wout[:, 8:16, :], wov[:, 8:16, :], b_wout)
            g1bc, b_g1 = sb(st, "g1bc", [128, D], F32)
            load("sp", g1bc[:], mod_s[0:1, 2 * D:3 * D].partition_broadcast(128), b_g1, B["mod_s"])
            ybs = [sb(st, "yb%d" % i, [128, 16, 512], BF16) for i in range(2)]
            xts = [sb(st, "xt%d" % i, [128, D], F32) for i in range(2)]
            x1ts = [sb(st, "x1t%d" % i, [128, D], F32) for i in range(2)]
            tts = [sb(st, "tt%d" % i, [128, 512], F32) for i in range(2)]
            junk, b_junk = sb(st, "junk", [128, D], BF16)
            ssqs = [sb(st, "ssq%d" % i, [128, 1], F32) for i in range(2)]
            xns = [sb(st, "xn%d" % i, [128, D], BF16) for i in range(2)]
            h2st, b_h2st = sb(st, "h2st", [128, 16, 512], BF16)
            pts = [ps(st, "pt%d" % i, [128, 1024], BF16) for i in range(2)]
            pos = [ps(st, "po%d" % i, [128, 512], F32) for i in range(4)]
            yTv = yT_s.rearrange("(kc p) t -> p kc t", p=128)
            h2v = h2T_s.rearrange("(kc p) t -> p kc t", p=128)
            def p5a(tt):
                tb, q = tt // 4, tt % 4
                yb, b_yb = ybs[tb % 2]
                if q == 0:
                    load("sp", yb[:], yTv[:, :, tb * 512:(tb + 1) * 512], b_yb, B["yT_s"])
                xt, b_xt = xts[tt % 2]; x1t, b_x1t = x1ts[tt % 2]
                load("sp", xt[:], x_d[tt * 128:(tt + 1) * 128, :], b_xt)
                for db in range(4):
                    po, b_po = pos[db]
                    tq, b_tq = tts[db % 2]
                    for dc in range(16):
                        S.op("pe", lambda e, dc=dc, q=q, db=db, po=po, yb=yb: e.matmul(po[:], lhsT=yb[:, dc, q * 128:(q + 1) * 128], rhs=wout[:, dc, db * 512:(db + 1) * 512],
                                                                                     start=(dc == 0), stop=(dc == 15)), reads=[b_yb, b_wout], writes=[b_po])
                    S.op("dve", lambda e, db=db, po=po, tq=tq: e.tensor_tensor(out=tq[:], in0=po[:], in1=g1bc[:, db * 512:(db + 1) * 512], op=ALU.mult),
                         reads=[b_po, b_g1], writes=[b_tq])
                    S.op("dve", lambda e, db=db, tq=tq, xt=xt, x1t=x1t: e.tensor_tensor(out=x1t[:, db * 512:(db + 1) * 512], in0=tq[:], in1=xt[:, db * 512:(db + 1) * 512], op=ALU.add),
                         reads=[b_tq, b_xt], writes=[b_x1t])
                store("sp", x1_s[tt * 128:(tt + 1) * 128, :], x1t[:], b_x1t, B["x1_s"])
                norm_p1(x1t[:], b_x1t, junk, b_junk, ssqs[tt % 2][0], ssqs[tt % 2][1], xns[tt % 2][0], xns[tt % 2][1])

            def p5b(tt):
                tb, q = tt // 4, tt % 4
                norm_p2(xns[tt % 2][0], xns[tt % 2][1], pts, lambda kc, q=q: h2st[:, kc, q * 128:(q + 1) * 128], b_h2st,
                        lambda kc: A2[:, kc:kc + 1], lambda kc: modT[:, 48 + kc, 0:1], b_A2, b_modT)
                if q == 3:
                    store("sp", h2v[:, :, tb * 512:(tb + 1) * 512], h2st[:], b_h2st, B["h2T_s"])

            p5a(0)
            for tt in range(16):
                if tt + 1 < 16:
                    p5a(tt + 1)
                p5b(tt)
            S.flush()
        if upto <= 6:
            return nc, S

        wupv = wup_d.rearrange("(kc p) n -> p kc n", p=128)
        wdnv = wdn_d.rearrange("(fc p) n -> p fc n", p=128)
        h2v = h2T_s.rearrange("(kc p) t -> p kc t", p=128)
        for blk in range(2):
            tok0 = blk * 1024
            with contextlib.ExitStack() as st:
                gT, b_gT = sb(st, "gT", [128, 44, 1024], BF16)
                with contextlib.ExitStack() as st2:
                    h2b, b_h2b = sb(st2, "h2b", [128, 16, 1024], BF16)
                    halo, b_halo = sb(st2, "halo", [128, 16, 2], BF16)
                    was = [sb(st2, "wa%d" % i, [128, 16, 256], BF16) for i in range(2)]
                    wbs = [sb(st2, "wb%d" % i, [128, 16, 256], BF16) for i in range(2)]
                    uxs = [sb(st2, "ux%d" % i, [128, 1026], F32) for i in range(2)]
                    tcs = [sb(st2, "tc%d" % i, [128, 1024], F32) for i in range(2)]
                    pus = [ps(st2, "pu%d" % i, [128, 512], F32) for i in range(6)]
                    ph, b_ph = ps(st2, "ph", [128, 512], F32)
                    load("sp", h2b[:], h2v[:, :, tok0:tok0 + 1024], b_h2b, B["h2T_s"])
                    S.op("pool", lambda e: e.memset(halo[:], 0.0), writes=[b_halo])
                    if blk == 1:
                        load("sp", halo[:, :, 0:1], h2v[:, :, tok0 - 1:tok0], b_halo, B["h2T_s"], slow=True)
                    else:
                        load("sp", halo[:, :, 1:2], h2v[:, :, tok0 + 1024:tok0 + 1025], b_halo, B["h2T_s"], slow=True)
                    ipu = 0
                    iph = 0
                    for i2 in range(22):
                        wa, b_wa = was[i2 % 2]; wb, b_wb = wbs[i2 % 2]
                        load("pool", wa[:], wupv[:, :, i2 * 256:(i2 + 1) * 256], b_wa)
                        load("pool", wb[:], wupv[:, :, FF + i2 * 256:FF + (i2 + 1) * 256], b_wb)
                        for jj in range(2):
                            i = i2 * 2 + jj
                            for part in range(2):
                                wtile, b_wt_ = (wa, b_wa) if part == 0 else (wb, b_wb)
                                ux, b_ux = uxs[part]; tcv, b_tc = tcs[part]
                                col = i + 44 * part
                                for sbk in range(2):
                                    pu, b_pu = pus[ipu % 6]; ipu += 1
                                    for kc in range(16):
                                        S.op("pe", lambda e, kc=kc, jj=jj, sbk=sbk, pu=pu, wtile=wtile: e.matmul(pu[:], lhsT=wtile[:, kc, jj * 128:(jj + 1) * 128],
                                                                                                                rhs=h2b[:, kc, sbk * 512:(sbk + 1) * 512], start=(kc == 0), stop=(kc == 15)),
                                             reads=[b_wt_, b_h2b], writes=[b_pu])
                                    S.op("act", lambda e, sbk=sbk, pu=pu, ux=ux: e.activation(out=ux[:, 1 + sbk * 512:1 + (sbk + 1) * 512], in_=pu[:], func=AF.Copy),
                                         reads=[b_pu], writes=[b_ux])
                                hs = (iph % 8) * 2; iph += 1
                                for kc in range(16):
                                    S.op("pe", lambda e, kc=kc, jj=jj, hs=hs, wtile=wtile: e.matmul(ph[:, hs:hs + 2], lhsT=wtile[:, kc, jj * 128:(jj + 1) * 128], rhs=halo[:, kc, :],
                                                                                                  start=(kc == 0), stop=(kc == 15)), reads=[b_wt_, b_halo], writes=[b_ph])
                                S.op("act", lambda e, hs=hs, ux=ux: e.activation(out=ux[:, 0:1], in_=ph[:, hs:hs + 1], func=AF.Copy), reads=[b_ph], writes=[b_ux])
                                S.op("act", lambda e, hs=hs, ux=ux: e.activation(out=ux[:, 1025:1026], in_=ph[:, hs + 1:hs + 2], func=AF.Copy), reads=[b_ph], writes=[b_ux])
                                S.op("act", lambda e, ux=ux, tcv=tcv, col=col: e.activation(out=tcv[:], in_=ux[:, 1:1025], func=AF.Identity,
                                                                                           scale=fm[:, V_CW + 88 + col:V_CW + 88 + col + 1], bias=fm[:, V_CB + col:V_CB + col + 1]),
                                     reads=[b_ux, b_fm], writes=[b_tc])
                                S.op("dve", lambda e, ux=ux, tcv=tcv, col=col: e.scalar_tensor_tensor(out=tcv[:], in0=ux[:, 0:1024], scalar=fm[:, V_CW + col:V_CW + col + 1], in1=tcv[:],
                                                                                                     op0=ALU.mult, op1=ALU.add), reads=[b_ux, b_fm, b_tc], writes=[b_tc])
                                S.op("dve", lambda e, ux=ux, tcv=tcv, col=col: e.scalar_tensor_tensor(out=tcv[:], in0=ux[:, 2:1026], scalar=fm[:, V_CW + 176 + col:V_CW + 176 + col + 1], in1=tcv[:],
                                                                                                     op0=ALU.mult, op1=ALU.add), reads=[b_ux, b_fm, b_tc], writes=[b_tc])
                            S.op("act", lambda e: e.activation(out=tcs[0][0][:], in_=tcs[0][0][:], func=AF.Silu), reads=[tcs[0][1]], writes=[tcs[0][1]])
                            S.op("dve", lambda e, i=i: e.tensor_tensor(out=gT[:, i, :], in0=tcs[0][0][:], in1=tcs[1][0][:], op=ALU.mult),
                                 reads=[tcs[0][1], tcs[1][1]], writes=[b_gT])
                    S.flush()
                with contextlib.ExitStack() as st2:
                    g2bc, b_g2 = sb(st2, "g2bc", [128, D], F32)
                    load("sp", g2bc[:], mod_s[0:1, 5 * D:6 * D].partition_broadcast(128), b_g2, B["mod_s"])
                    wds = [sb(st2, "wd%d" % i, [128, 4, 512], BF16) for i in range(10)]
                    x1p = [sb(st2, "x1p%d" % i, [128, 512], F32) for i in range(4)]
                    tps = [sb(st2, "tp%d" % i, [128, 512], F32) for i in range(2)]
                    pds = [ps(st2, "pd%d" % i, [128, 512], F32) for i in range(8)]
                    iw = 0
                    ix = 0
                    for db in range(4):
                        for f4 in range(11):
                            wd, b_wd = wds[iw % 10]; iw += 1
                            load("pool", wd[:], wdnv[:, f4 * 4:(f4 + 1) * 4, db * 512:(db + 1) * 512], b_wd)
                            for fj in range(4):
                                fc = f4 * 4 + fj
                                for tt in range(8):
                                    S.op("pe", lambda e, fc=fc, fj=fj, tt=tt, wd=wd: e.matmul(pds[tt][0][:], lhsT=gT[:, fc, tt * 128:(tt + 1) * 128], rhs=wd[:, fj, :],
                                                                                            start=(fc == 0), stop=(fc == 43)), reads=[b_gT, b_wd], writes=[pds[tt][1]])
                        for tt in range(8):
                            xp, b_xp = x1p[ix % 4]; tp, b_tp = tps[ix % 2]; ix += 1
                            r0 = tok0 + tt * 128
                            load("sp", xp[:], x1_s[r0:r0 + 128, db * 512:(db + 1) * 512], b_xp, B["x1_s"])
                            S.op("dve", lambda e, tt=tt, db=db, tp=tp: e.tensor_tensor(out=tp[:], in0=pds[tt][0][:], in1=g2bc[:, db * 512:(db + 1) * 512], op=ALU.mult),
                                 reads=[pds[tt][1], b_g2], writes=[b_tp])
                            S.op("dve", lambda e, tp=tp, xp=xp: e.tensor_tensor(out=xp[:], in0=tp[:], in1=xp[:], op=ALU.add), reads=[b_tp, b_xp], writes=[b_xp])
                            store("sp", x2_s[r0:r0 + 128, db * 512:(db + 1) * 512], xp[:], b_xp, B["x2_s"])
                    S.flush()
        if upto <= 7:
            return nc, S

        with contextlib.ExitStack() as st:
            fnw, b_fnw = sb(st, "fnw", [128, D], F32)
            load("sp", fnw[:], fnw_d.partition_broadcast(128), b_fnw)
            xts = [sb(st, "xf%d" % i, [128, D], F32) for i in range(2)]
            ots = [sb(st, "of%d" % i, [128, D], F32) for i in range(2)]
            junk, b_junk = sb(st, "junk", [128, D], BF16)
            ssqs = [sb(st, "ssq%d" % i, [128, 1], F32) for i in range(2)]
            for tt in range(16):
                xt, b_xt = xts[tt % 2]; ot, b_ot = ots[tt % 2]; ssq, b_ssq = ssqs[tt % 2]
                load("sp", xt[:], x2_s[tt * 128:(tt + 1) * 128, :], b_xt, B["x2_s"])
                S.op("act", lambda e, xt=xt: e.activation(out=junk[:], in_=xt[:], func=AF.Square), reads=[b_xt], writes=[b_junk])
                S.op("dve", lambda e, ssq=ssq: e.reduce_sum(out=ssq[:], in_=junk[:], axis=mybir.AxisListType.X), reads=[b_junk], writes=[b_ssq])
                S.op("pool", lambda e, ssq=ssq: e.tensor_scalar(out=ssq[:], in0=ssq[:], scalar1=1.0 / D, scalar2=EPS, op0=ALU.mult, op1=ALU.add), reads=[b_ssq], writes=[b_ssq])
                S.op("pool", lambda e, ssq=ssq: e.tensor_tensor(out=ssq[:], in0=ssq[:], in1=mhalf[:], op=ALU.pow), reads=[b_ssq, b_mhalf], writes=[b_ssq])
                S.op("dve", lambda e, xt=xt, ot=ot, ssq=ssq: e.scalar_tensor_tensor(out=ot[:], in0=xt[:], scalar=ssq[:, 0:1], in1=fnw[:], op0=ALU.mult, op1=ALU.mult),
                     reads=[b_xt, b_ssq, b_fnw], writes=[b_ot])
                store("sp", out_d[tt * 128:(tt + 1) * 128, :], ot[:], b_ot, B["out"])
            S.flush()
        return nc, S


_CONST = None


def _consts():
    global _CONST
    if _CONST is not None:
        return _CONST
    bf = ml_dtypes.bfloat16
    rows = T // 64
    r, col = np.meshgrid(np.arange(rows), np.arange(64), indexing="ij")
    pos = np.stack([r.reshape(-1), col.reshape(-1)], axis=-1).astype(np.float32)
    inv = (np.float32(10000.0) ** (-(np.arange(16, dtype=np.float32)) / np.float32(16))).astype(np.float32)
    ang = (pos[:, :, None] * inv).astype(np.float32)
    cs, sn = np.cos(ang).astype(np.float32), np.sin(ang).astype(np.float32)
    cosT = np.zeros((128, T), np.float32); sinT = np.zeros((128, T), np.float32)
    perm = np.zeros((128, 128), np.float32)
    for p in range(128):
        a, h, i = (p % 64) // 32, (p % 32) // 16, p % 16
        cosT[p] = cs[:, a, i]
        sinT[p] = sn[:, a, i] * (-1.0 if h == 0 else 1.0)
        partner = p + 16 if h == 0 else p - 16
        perm[partner, p] = 1.0
    s_, t_ = np.meshgrid(np.arange(64), np.arange(64), indexing="ij")
    tri = np.stack([(t_ >= s_), (t_ <= s_)], 0).astype(np.float32)
    tri = np.broadcast_to(tri.transpose(1, 0, 2)[:, :, None, :], (64, 2, 8, 64)).reshape(64, 1024)
    smask = np.ones((128, TA), np.float32); smask[:, ::64] = 0.0
    _CONST = dict(cosT=cosT, sinT=sinT, perm=perm.astype(bf), identb=np.eye(128, dtype=np.float32).astype(bf),
                  identf=np.eye(128, dtype=np.float32), tri=np.ascontiguousarray(tri).astype(bf), smask=smask)
    return _CONST


def make_in_maps(inputs):
    f = lambda a: np.ascontiguousarray(np.asarray(a, dtype=np.float32))
    x = f(inputs["x"]); c = f(inputs["c"]); ctx = f(inputs["ctx"]); c_ctx = f(inputs["c_ctx"])
    shared = dict(_consts())
    shared["w_mod"] = f(inputs["w_mod"][0]); shared["w_in"] = f(inputs["w_in"][0])
    shared["b_mod2"] = np.ascontiguousarray(np.stack([f(inputs["b_mod"][0])] * 2, 0))
    shared["lam4"] = np.concatenate([f(inputs[k][0]) for k in ("lam_q1", "lam_k1", "lam_q2", "lam_k2")]).reshape(1, 256)
    shared["subln"] = f(inputs["subln_w"][0]).reshape(1, 128)
    shared["w_ba"] = f(inputs["w_branch_attn"][0]); shared["w_br"] = f(inputs["w_branch_rec"][0]); shared["w_out"] = f(inputs["w_out"][0])
    shared["w_up"] = f(inputs["w_up"][0]); shared["w_down"] = f(inputs["w_down"][0]); shared["fnw"] = f(inputs["final_norm_w"]).reshape(1, D)
    vec_tail = np.concatenate([f(inputs["norm1_w"][0]), f(inputs["norm2_w"][0]), f(inputs["rec_gnorm_w"][0]),
                               f(inputs["rec_lb"]).reshape(-1), f(inputs["conv_w"][0]).reshape(-1), f(inputs["conv_b"][0])])
    maps = []
    for b in range(8):
        m = dict(shared)
        m["x"] = x[b]; m["ctx"] = ctx[b]
        m["vecs"] = np.ascontiguousarray(np.concatenate([c[b], c_ctx, vec_tail]).reshape(NV, 128))
        maps.append(m)
    return maps


_NC = None


def kernel(**inputs):
    global _NC
    if _NC is None:
        _NC = build()[0]
    maps = make_in_maps(inputs)
    res = run_bass_kernel_spmd(_NC, maps, core_ids=list(range(8)))
    return np.stack([np.asarray(r["out"], dtype=np.float32) for r in res.results], 0)
```

```python
import contextlib
import numpy as np
import ml_dtypes
import concourse.bass as bass
import concourse.mybir as mybir
from concourse.bass_utils import run_bass_kernel_spmd

F32 = mybir.dt.float32
BF16 = mybir.dt.bfloat16
AF = mybir.ActivationFunctionType
ALU = mybir.AluOpType

D = 2048
T = 2048
TC = 256
TA = T + TC
NIN = 12288
FF = 5632
EPS = 1e-6
LAM_INIT = 0.2

V_C, V_CC, V_N1, V_N2, V_GN, V_LB, V_CW, V_CB = 0, 16, 32, 48, 64, 72, 104, 368
NV = 456


class Buf:
    __slots__ = ("w", "rs")

    def __init__(self):
        self.w = None
        self.rs = []


class Op:
    __slots__ = ("eng", "fn", "deps", "signaled", "val", "is_dma", "sem", "epoch")

    def __init__(self, eng, fn, is_dma, epoch):
        self.eng = eng
        self.fn = fn
        self.deps = []
        self.signaled = False
        self.val = None
        self.is_dma = is_dma
        self.sem = None
        self.epoch = epoch


class Sched:
    ENGS = ("pe", "act", "dve", "pool", "sp")
    NDMA = 8

    def __init__(self, nc, es):
        self.nc = nc
        self.csem = {e: es.enter_context(nc.semaphore("c_" + e)) for e in ("pe", "act", "dve", "pool")}
        self.dsem = {q: [es.enter_context(nc.semaphore("d_%s%d" % (q, i))) for i in range(self.NDMA)]
                     for q in ("sp", "pool", "act")}
        self.psem = es.enter_context(nc.semaphore("phase"))
        self.cval = {e: 0 for e in self.csem}
        self.dval = {q: [0] * self.NDMA for q in self.dsem}
        self.dlast = {q: [None] * self.NDMA for q in self.dsem}
        self.drr = {q: 0 for q in self.dsem}
        self.waited = {e: {} for e in self.ENGS}
        self.ops = {e: [] for e in self.ENGS}
        self.epoch = 0
        self.nops = 0

    def _add_dep(self, op, dep):
        if dep is None or dep is op or dep.epoch != self.epoch:
            return
        if dep.eng == "pe" and op.eng == "pe" and not dep.is_dma and not op.is_dma:
            return
        if dep not in op.deps:
            op.deps.append(dep)

    def op(self, eng, fn, reads=(), writes=()):
        o = Op(eng, fn, False, self.epoch)
        self._track(o, reads, writes)
        self.ops[eng].append(o)
        return o

    def dma(self, q, fn, reads=(), writes=()):
        o = Op(q, fn, True, self.epoch)
        self._track(o, reads, writes)
        self.ops[q].append(o)
        return o

    def _track(self, o, reads, writes):
        for b in reads:
            self._add_dep(o, b.w)
        for b in writes:
            for r in b.rs:
                self._add_dep(o, r)
            self._add_dep(o, b.w)
        for b in reads:
            b.rs.append(o)
        for b in writes:
            b.w = o
            b.rs = []

    def flush(self):
        nc = self.nc
        ops = self.ops
        fence = Op("sp", None, False, self.epoch)
        for e in self.ENGS:
            last = None
            for o in ops[e]:
                if o.is_dma:
                    fence.deps.append(o)
                else:
                    last = o
            if last is not None and e != "sp":
                fence.deps.append(last)
        ops["sp"].append(fence)
        for e in self.ENGS:
            for o in ops[e]:
                for d in o.deps:
                    d.signaled = True
        for e in self.ENGS:
            for o in ops[e]:
                if o.fn is None:
                    continue
                if o.is_dma:
                    q = o.eng
                    i = self.drr[q]
                    self.drr[q] = (i + 1) % self.NDMA
                    prev = self.dlast[q][i]
                    if prev is not None and prev.epoch == self.epoch:
                        o.deps.append(prev)
                    self.dval[q][i] += 16
                    o.sem = self.dsem[q][i]
                    o.val = self.dval[q][i]
                    self.dlast[q][i] = o
                elif o.signaled:
                    self.cval[e] += 1
                    o.sem = self.csem[e]
                    o.val = self.cval[e]
        waited = self.waited
        epoch = self.epoch
        psem = self.psem

        def emit(ename, eng):
            w = waited[ename]
            if epoch > 0:
                eng.wait_ge(psem, epoch)
            for o in ops[ename]:
                for d in o.deps:
                    k = d.sem.num
                    if w.get(k, 0) >= d.val:
                        continue
                    eng.wait_ge(d.sem, d.val)
                    w[k] = d.val
                if o.fn is None:
                    eng.sem_inc(psem, 1)
                    continue
                ins = o.fn(eng)
                self.nops += 1
                if o.is_dma:
                    ins.then_inc(o.sem, 16)
                elif o.signaled:
                    ins.then_inc(o.sem, 1)

        with nc.Block() as block:
            @block.sync
            def _(e):
                emit("sp", e)

            @block.gpsimd
            def _(e):
                emit("pool", e)

            @block.scalar
            def _(e):
                emit("act", e)

            @block.vector
            def _(e):
                emit("dve", e)

            @block.tensor
            def _(e):
                emit("pe", e)
        self.ops = {e: [] for e in self.ENGS}
        self.epoch += 1


def build(upto=99, debug=False):
    nc = bass.Bass("TRN2", target_bir_lowering=False)
    kscr = "ExternalOutput" if debug else "Internal"

    def din(name, shape, dt=F32):
        return nc.dram_tensor(name, list(shape), dt, kind="ExternalInput").ap()

    def dscr(name, shape, dt):
        return nc.dram_tensor(name, list(shape), dt, kind=kscr).ap()

    x_d = din("x", [T, D]); ctx_d = din("ctx", [TC, D]); vecs_d = din("vecs", [NV, 128])
    wmod_d = din("w_mod", [D, NIN]); bmod_d = din("b_mod2", [2, NIN]); win_d = din("w_in", [D, NIN])
    lam_d = din("lam4", [1, 256]); subln_d = din("subln", [1, 128])
    wba_d = din("w_ba", [1024, D]); wbr_d = din("w_br", [1024, D]); wout_d = din("w_out", [D, D])
    wup_d = din("w_up", [D, 2 * FF]); wdn_d = din("w_down", [FF, D]); fnw_d = din("fnw", [1, D])
    cos_d = din("cosT", [128, T]); sin_d = din("sinT", [128, T])
    identb_d = din("identb", [128, 128], BF16); identf_d = din("identf", [128, 128]); perm_d = din("perm", [128, 128], BF16)
    tri_d = din("tri", [64, 2 * 8 * 64], BF16); smask_d = din("smask", [128, TA])
    out_d = nc.dram_tensor("out", [T, D], F32, kind="ExternalOutput").ap()

    mod_s = dscr("mod_s", [2, NIN], F32)
    kT_s = dscr("kT_s", [8, 128, TA], BF16); qT_s = dscr("qT_s", [8, 128, T], BF16)
    v_s = dscr("v_s", [TA, 8 * 130], BF16); rf_s = dscr("rf_s", [2, 1024, TA], F32)
    rv_s = dscr("rv_s", [TA, 1024], BF16); rq_s = dscr("rq_s", [1024, T], BF16); rg_s = dscr("rg_s", [1024, T], BF16)
    ga_s = dscr("ga_s", [D, T], BF16); gr_s = dscr("gr_s", [D, T], BF16)
    attT_s = dscr("attT_s", [1024, T], BF16); recT_s = dscr("recT_s", [1024, T], BF16)
    yT_s = dscr("yT_s", [D, T], BF16); x1_s = dscr("x1_s", [T, D], F32); h2T_s = dscr("h2T_s", [D, T], BF16)
    x2_s = dscr("x2_s", [T, D], F32)
    o_s = dscr("o_s", [1024, T], BF16)
    B = {k: Buf() for k in ("mod_s", "kT_s", "qT_s", "v_s", "rf_s", "rv_s", "rq_s", "rg_s", "ga_s", "gr_s",
                            "attT_s", "recT_s", "yT_s", "x1_s", "h2T_s", "x2_s", "out", "o_s")}

    with contextlib.ExitStack() as es:
        S = Sched(nc, es)

        uid = [0]

        def sb(st, name, shape, dt):
            uid[0] += 1
            return st.enter_context(nc.sbuf_tensor("s%d_%s" % (uid[0], name), list(shape), dt)), Buf()

        def ps(st, name, shape, dt=F32):
            uid[0] += 1
            return st.enter_context(nc.psum_tensor("p%d_%s" % (uid[0], name), list(shape), dt)), Buf()

        def load(q, dst, src, bdst, bsrc=None, slow=False):
            if slow:
                S.dma(q, lambda e: e.dma_start(out=dst, in_=src, allow_slow_non_contiguous=True), reads=[bsrc] if bsrc else [], writes=[bdst])
            else:
                S.dma(q, lambda e: e.dma_start(out=dst, in_=src), reads=[bsrc] if bsrc else [], writes=[bdst])

        def store(q, dst, src, bsrc, bdst=None):
            S.dma(q, lambda e: e.dma_start(out=dst, in_=src), reads=[bsrc], writes=[bdst] if bdst else [])

        fm, b_fm = sb(es, "fm", [128, NV], F32)
        modT, b_modT = sb(es, "modT", [128, 96, 2], F32)
        A1, b_A1 = sb(es, "A1", [128, 16, 2], F32)
        A2, b_A2 = sb(es, "A2", [128, 16], F32)
        nlam, b_nlam = sb(es, "nlam", [128, 1], F32)
        low, b_low = sb(es, "low", [128, 16], F32)
        oml, b_oml = sb(es, "oml", [128, 16], F32)
        slw, b_slw = sb(es, "slw", [128, 128], F32)
        identb, b_idb = sb(es, "identb", [128, 128], BF16)
        identf, b_idf = sb(es, "identf", [128, 128], F32)
        mhalf, b_mhalf = sb(es, "mhalf", [128, 1], F32)
        onesb, b_onesb = sb(es, "onesb", [128, 128], BF16)

        with contextlib.ExitStack() as st:
            vrow, b_vrow = sb(st, "vrow", [128, 4, 128], F32)
            scb, b_scb = sb(st, "scb", [128, 16, 2], BF16)
            bmod, b_bmod = sb(st, "bmod", [2, NIN], F32)
            modrow, b_modrow = sb(st, "modrow", [2, NIN], F32)
            wt = [sb(st, "wt%d" % i, [128, 16, 512], BF16) for i in range(2)]
            lamt, b_lamt = sb(st, "lamt", [128, 256], F32)
            lamp, b_lamp = sb(st, "lamp", [128, 2, 64], F32)
            lams, b_lams = sb(st, "lams", [128, 2], F32)
            tmp16, b_tmp16 = sb(st, "tmp16", [128, 16, 2], F32)
            pv, b_pv = ps(st, "pv", [128, 512], F32)
            pmm = [ps(st, "pmm%d" % i, [128, 512], F32) for i in range(2)]
            pmt, b_pmt = ps(st, "pmt", [128, 512], F32)

            load("sp", identb[:], identb_d[:, :], b_idb)
            load("sp", identf[:], identf_d[:, :], b_idf)
            S.op("pool", lambda e: e.memset(mhalf[:], -0.5), writes=[b_mhalf])
            S.op("pool", lambda e: e.memset(onesb[:], 1.0), writes=[b_onesb])
            nrows = [128, 128, 128, NV - 384]
            for i in range(4):
                load("sp", vrow[0:nrows[i], i, :], vecs_d[i * 128:i * 128 + nrows[i], :], b_vrow)
            for i in range(4):
                n = nrows[i]
                S.op("pe", lambda e, i=i, n=n: e.transpose(out=pv[:, 0:n], in_=vrow[0:n, i, :], identity=identf[0:n, 0:n]),
                     reads=[b_vrow, b_idf], writes=[b_pv])
                S.op("dve", lambda e, i=i, n=n: e.tensor_copy(out=fm[:, i * 128:i * 128 + n], in_=pv[:, 0:n]),
                     reads=[b_pv], writes=[b_fm])
            for j, off in enumerate((V_C, V_CC)):
                S.op("act", lambda e, j=j, off=off: e.activation(out=scb[:, :, j], in_=fm[:, off:off + 16], func=AF.Silu),
                     reads=[b_fm], writes=[b_scb])
            load("sp", bmod[:], bmod_d[:, :], b_bmod)
            wv = wmod_d.rearrange("(kc p) n -> p kc n", p=128)
            for g in range(24):
                s = g % 2
                w_t, b_w = wt[s]
                load("pool", w_t[:], wv[:, :, g * 512:(g + 1) * 512], b_w)
                pm, b_pm = pmm[s]
                for kc in range(16):
                    S.op("pe", lambda e, kc=kc, w_t=w_t, pm=pm: e.matmul(pm[0:2, :], lhsT=scb[:, kc, :], rhs=w_t[:, kc, :],
                                                                         start=(kc == 0), stop=(kc == 15)),
                         reads=[b_scb, b_w], writes=[b_pm])
                S.op("dve", lambda e, g=g, pm=pm: e.tensor_tensor(out=modrow[:, g * 512:(g + 1) * 512], in0=pm[0:2, :],
                                                                  in1=bmod[:, g * 512:(g + 1) * 512], op=ALU.add),
                     reads=[b_pm, b_bmod], writes=[b_modrow])
            store("sp", mod_s[:, :], modrow[:], b_modrow, B["mod_s"])
            for j in range(96):
                S.op("pe", lambda e, j=j: e.matmul(pmt[:, 2 * j:2 * j + 2], lhsT=modrow[:, j * 128:(j + 1) * 128],
                                                   rhs=identf[0:2, 0:2], start=True, stop=True),
                     reads=[b_modrow, b_idf], writes=[b_pmt])
            S.op("dve", lambda e: e.tensor_copy(out=modT[:].rearrange("p a b -> p (a b)"), in_=pmt[:, 0:192]),
                 reads=[b_pmt], writes=[b_modT])
            S.op("dve", lambda e: e.tensor_scalar(out=tmp16[:], in0=modT[:, 16:32, :], scalar1=1.0, scalar2=None, op0=ALU.add),
                 reads=[b_modT], writes=[b_tmp16])
            for j in range(2):
                S.op("dve", lambda e, j=j: e.tensor_tensor(out=A1[:, :, j], in0=tmp16[:, :, j], in1=fm[:, V_N1:V_N1 + 16], op=ALU.mult),
                     reads=[b_tmp16, b_fm], writes=[b_A1])
            S.op("dve", lambda e: e.tensor_scalar(out=tmp16[:, :, 0], in0=modT[:, 64:80, 0], scalar1=1.0, scalar2=None, op0=ALU.add),
                 reads=[b_modT, b_A1], writes=[b_tmp16])
            S.op("dve", lambda e: e.tensor_tensor(out=A2[:], in0=tmp16[:, :, 0], in1=fm[:, V_N2:V_N2 + 16], op=ALU.mult),
                 reads=[b_tmp16, b_fm], writes=[b_A2])
            load("sp", lamt[:], lam_d.partition_broadcast(128), b_lamt)
            S.op("dve", lambda e: e.tensor_tensor(out=lamp[:], in0=lamt[:].rearrange("p (a b c) -> p a b c", a=2, b=2)[:, :, 0, :],
                                                  in1=lamt[:].rearrange("p (a b c) -> p a b c", a=2, b=2)[:, :, 1, :], op=ALU.mult),
                 reads=[b_lamt], writes=[b_lamp])
            for j in range(2):
                S.op("dve", lambda e, j=j: e.reduce_sum(out=lams[:, j:j + 1], in_=lamp[:, j, :], axis=mybir.AxisListType.X),
                     reads=[b_lamp], writes=[b_lams])
            S.op("act", lambda e: e.activation(out=lams[:], in_=lams[:], func=AF.Exp), reads=[b_lams], writes=[b_lams])
            S.op("dve", lambda e: e.tensor_tensor(out=nlam[:], in0=lams[:, 1:2], in1=lams[:, 0:1], op=ALU.subtract),
                 reads=[b_lams], writes=[b_nlam])
            S.op("dve", lambda e: e.tensor_scalar(out=nlam[:], in0=nlam[:], scalar1=-LAM_INIT, scalar2=None, op0=ALU.add),
                 reads=[b_nlam], writes=[b_nlam])
            lbv = fm[:, V_LB:V_LB + 32].rearrange("p (d l h) -> p d l h", d=2, l=2)
            S.op("dve", lambda e: e.tensor_tensor(out=low[:].rearrange("p (d h) -> p d h", d=2), in0=lbv[:, :, 0, :], in1=lbv[:, :, 1, :],
                                                  op=ALU.subtract), reads=[b_fm], writes=[b_low])
            S.op("act", lambda e: e.activation(out=low[:], in_=low[:], func=AF.Sigmoid), reads=[b_low], writes=[b_low])
            S.op("dve", lambda e: e.tensor_scalar(out=oml[:], in0=low[:], scalar1=-1.0, scalar2=1.0, op0=ALU.mult, op1=ALU.add),
                 reads=[b_low], writes=[b_oml])
            load("sp", slw[:], subln_d.partition_broadcast(128), b_slw)
            S.op("dve", lambda e: e.tensor_scalar(out=slw[:], in0=slw[:], scalar1=1.0 - LAM_INIT, scalar2=None, op0=ALU.mult),
                 reads=[b_slw], writes=[b_slw])
            S.flush()
        if upto <= 0:
            return nc, S

        def norm_p1(xt, b_xt, junk, b_junk, ssq, b_ssq, xn, b_xn):
            S.op("act", lambda e: e.activation(out=junk[:], in_=xt, func=AF.Square), reads=[b_xt], writes=[b_junk])
            S.op("dve", lambda e: e.reduce_sum(out=ssq[:], in_=junk[:], axis=mybir.AxisListType.X), reads=[b_junk], writes=[b_ssq])
            S.op("pool", lambda e: e.tensor_scalar(out=ssq[:], in0=ssq[:], scalar1=1.0 / D, scalar2=EPS, op0=ALU.mult, op1=ALU.add),
                 reads=[b_ssq], writes=[b_ssq])
            S.op("pool", lambda e: e.tensor_tensor(out=ssq[:], in0=ssq[:], in1=mhalf[:], op=ALU.pow),
                 reads=[b_ssq, b_mhalf], writes=[b_ssq])
            S.op("dve", lambda e: e.tensor_scalar(out=xn[:], in0=xt, scalar1=ssq[:, 0:1], scalar2=None, op0=ALU.mult),
                 reads=[b_xt, b_ssq], writes=[b_xn])

        def norm_p2(xn, b_xn, pts, dst_fn, b_dst, Ascal, Bscal, bA, bB):
            for g in range(4):
                pt, b_pt = pts[g % 2]
                for j in range(4):
                    kc = g * 4 + j
                    S.op("pe", lambda e, kc=kc, j=j, pt=pt: e.transpose(out=pt[:, j * 128:(j + 1) * 128], in_=xn[:, kc * 128:(kc + 1) * 128],
                                                                        identity=identb[:]),
                         reads=[b_xn, b_idb], writes=[b_pt])
                for j in range(4):
                    kc = g * 4 + j
                    if False:
                        S.op("dve", lambda e, kc=kc, j=j, pt=pt: e.tensor_scalar(out=dst_fn(kc), in0=pt[:, j * 128:(j + 1) * 128],
                                                                                 scalar1=Ascal(kc), scalar2=Bscal(kc), op0=ALU.mult, op1=ALU.add),
                             reads=[b_pt, bA, bB], writes=[b_dst])
                    else:
                        S.op("act", lambda e, kc=kc, j=j, pt=pt: e.activation(out=dst_fn(kc), in_=pt[:, j * 128:(j + 1) * 128], func=AF.Identity,
                                                                              scale=Ascal(kc), bias=Bscal(kc)),
                             reads=[b_pt, bA, bB], writes=[b_dst])

        with contextlib.ExitStack() as st:
            hT, b_hT = sb(st, "hT", [128, 16, TA], BF16)
            with contextlib.ExitStack() as st2:
                xts = [sb(st2, "xt%d" % i, [128, D], F32) for i in range(2)]
                junk, b_junk = sb(st2, "junk", [128, D], BF16)
                ssqs = [sb(st2, "ssq%d" % i, [128, 1], F32) for i in range(2)]
                xns = [sb(st2, "xn%d" % i, [128, D], BF16) for i in range(2)]
                pts = [ps(st2, "pt%d" % i, [128, 1024], BF16) for i in range(2)]
                def p1a(tt):
                    s_ = tt % 2
                    xt, b_xt = xts[s_]
                    src = ctx_d[tt * 128:(tt + 1) * 128, :] if tt < 2 else x_d[(tt - 2) * 128:(tt - 1) * 128, :]
                    load("sp", xt[:], src, b_xt)
                    norm_p1(xt[:], b_xt, junk, b_junk, ssqs[s_][0], ssqs[s_][1], xns[s_][0], xns[s_][1])

                def p1b(tt):
                    s_ = tt % 2
                    jc = 1 if tt < 2 else 0
                    norm_p2(xns[s_][0], xns[s_][1], pts, lambda kc, tt=tt: hT[:, kc, tt * 128:(tt + 1) * 128], b_hT,
                            lambda kc, jc=jc: A1[:, kc, jc:jc + 1], lambda kc, jc=jc: modT[:, kc, jc:jc + 1], b_A1, b_modT)

                p1a(0)
                for tt in range(18):
                    if tt + 1 < 18:
                        p1a(tt + 1)
                    p1b(tt)
                S.flush()
            if upto <= 1:
                return nc, S
            wt = [sb(st, "wt%d" % i, [128, 16, 512], BF16) for i in range(2)]
            stf = [sb(st, "stf%d" % i, [128, TA], F32) for i in range(2)]
            stb = [sb(st, "stb%d" % i, [128, TA], BF16) for i in range(2)]
            cosT, b_cos = sb(st, "cosT", [128, T], F32)
            sinT, b_sin = sb(st, "sinT", [128, T], F32)
            perm, b_perm = sb(st, "perm", [128, 128], BF16)
            zbs = [sb(st, "zb%d" % i, [128, 512], BF16) for i in range(2)]
            t1s = [sb(st, "t1_%d" % i, [128, 512], F32) for i in range(2)]
            t2s = [sb(st, "t2_%d" % i, [128, 512], F32) for i in range(2)]
            vst = [sb(st, "vst%d" % i, [128, 4, 130], BF16) for i in range(2)]
            rst = [sb(st, "rst%d" % i, [128, 512], BF16) for i in range(2)]
            pms = [ps(st, "pm%d" % i, [128, 512], F32) for i in range(4)]
            pws = [ps(st, "pw%d" % i, [128, 512], F32) for i in range(2)]
            load("sp", cosT[:], cos_d[:, :], b_cos)
            load("sp", sinT[:], sin_d[:, :], b_sin)
            load("sp", perm[:], perm_d[:, :], b_perm)
            for i in range(2):
                S.op("pool", lambda e, i=i: e.memset(vst[i][0][:], 1.0), writes=[vst[i][1]])
            fams = ["ak"] * 2 + ["av"] * 2 + ["rf0"] * 2 + ["rf1"] * 2 + ["ri"] * 2 + ["aq"] * 2 + ["rq"] * 2 + ["rg"] * 2 + ["ga"] * 4 + ["gr"] * 4
            fstart = {}
            for g, f in enumerate(fams):
                fstart.setdefault(f, g)
            wv = win_d.rearrange("(kc p) n -> p kc n", p=128)
            ipm = 0
            irope = 0
            ist = 0
            ivs = 0
            import os
            for g in range(int(os.environ.get("K1B", "24"))):
                fam = fams[g]
                w_t, b_w = wt[g % 2]
                load("pool", w_t[:], wv[:, :, g * 512:(g + 1) * 512], b_w)
                has_ctx = g < 10
                if fam in ("av", "ri"):
                    for tt in range(18):
                        pm, b_pm = pms[ipm % 4]; ipm += 1
                        for kc in range(16):
                            S.op("pe", lambda e, kc=kc, tt=tt, pm=pm, w_t=w_t: e.matmul(pm[:], lhsT=hT[:, kc, tt * 128:(tt + 1) * 128], rhs=w_t[:, kc, :],
                                                                                         start=(kc == 0), stop=(kc == 15)),
                                 reads=[b_hT, b_w], writes=[b_pm])
                        if fam == "av":
                            v_t, b_v = vst[ivs % 2]; ivs += 1
                            S.op("dve", lambda e, pm=pm, v_t=v_t: e.tensor_copy(out=v_t[:, :, 0:128], in_=pm[:].rearrange("p (h e) -> p h e", h=4)),
                                 reads=[b_pm], writes=[b_v])
                            hh = (g - 2) * 4
                            store("sp", v_s[tt * 128:(tt + 1) * 128, hh * 130:(hh + 4) * 130], v_t[:].rearrange("p h e -> p (h e)"), b_v, B["v_s"])
                        else:
                            r_t, b_r = rst[ivs % 2]; ivs += 1
                            S.op("dve", lambda e, pm=pm, r_t=r_t: e.tensor_copy(out=r_t[:], in_=pm[:]), reads=[b_pm], writes=[b_r])
                            store("sp", rv_s[tt * 128:(tt + 1) * 128, (g - 8) * 512:(g - 7) * 512], r_t[:], b_r, B["rv_s"])
                    continue
                blocks = ([(0, 256)] if has_ctx else []) + [(256 + 512 * i, 512) for i in range(4)]
                for j in range(4):
                    fi = (g - fstart[fam]) * 4 + j
                    isf32 = fam in ("rf0", "rf1")
                    stg, b_stg = (stf if isf32 else stb)[ist % 2]; ist += 1
                    for (t0, n) in blocks:
                        c0 = t0 if has_ctx else t0 - 256
                        pm, b_pm = pms[ipm % 4]; ipm += 1
                        for kc in range(16):
                            S.op("pe", lambda e, kc=kc, j=j, t0=t0, n=n, pm=pm, w_t=w_t: e.matmul(pm[:, 0:n], lhsT=w_t[:, kc, j * 128:(j + 1) * 128],
                                                                                                 rhs=hT[:, kc, t0:t0 + n], start=(kc == 0), stop=(kc == 15)),
                                 reads=[b_hT, b_w], writes=[b_pm])
                        dst = stg[:, c0:c0 + n]
                        if fam in ("aq", "ak") and t0 >= 256 and os.environ.get("K1R", "1") == "1":
                            zb, b_zb = zbs[irope % 2]; t1, b_t1 = t1s[irope % 2]; t2, b_t2 = t2s[irope % 2]
                            pw, b_pw = pws[irope % 2]; irope += 1
                            s0 = t0 - 256
                            S.op("act", lambda e, pm=pm, zb=zb: e.activation(out=zb[:], in_=pm[:], func=AF.Copy), reads=[b_pm], writes=[b_zb])
                            S.op("pe", lambda e, pw=pw, zb=zb: e.matmul(pw[:], lhsT=perm[:], rhs=zb[:], start=True, stop=True),
                                 reads=[b_perm, b_zb], writes=[b_pw])
                            S.op("dve", lambda e, pm=pm, t1=t1, s0=s0: e.tensor_tensor(out=t1[:], in0=pm[:], in1=cosT[:, s0:s0 + 512], op=ALU.mult),
                                 reads=[b_pm, b_cos, b_zb, b_pw], writes=[b_t1])
                            S.op("dve", lambda e, pw=pw, t2=t2, s0=s0: e.tensor_tensor(out=t2[:], in0=pw[:], in1=sinT[:, s0:s0 + 512], op=ALU.mult),
                                 reads=[b_pw, b_sin], writes=[b_t2])
                            S.op("dve", lambda e, t1=t1, t2=t2, dst=dst: e.tensor_tensor(out=dst, in0=t1[:], in1=t2[:], op=ALU.add),
                                 reads=[b_t1, b_t2], writes=[b_stg])
                        else:
                            func = {"ak": AF.Copy, "aq": AF.Copy, "rf0": AF.Copy, "rf1": AF.Copy, "rq": AF.Silu, "rg": AF.Silu, "ga": AF.Sigmoid, "gr": AF.Sigmoid}[fam]
                            S.op("act", lambda e, pm=pm, n=n, dst=dst, func=func: e.activation(out=dst, in_=pm[:, 0:n], func=func),
                                 reads=[b_pm], writes=[b_stg])
                    ncol = TA if has_ctx else T
                    if fam == "ak":
                        dd, bd = kT_s[fi], B["kT_s"]
                    elif fam == "aq":
                        dd, bd = qT_s[fi], B["qT_s"]
                    elif isf32:
                        dd, bd = rf_s[int(fam[2]), fi * 128:(fi + 1) * 128, :], B["rf_s"]
                    else:
                        scr = {"rq": rq_s, "rg": rg_s, "ga": ga_s, "gr": gr_s}[fam]
                        dd, bd = scr[fi * 128:(fi + 1) * 128, :], B[fam + "_s"]
                    store("sp", dd, stg[:, 0:ncol], b_stg, bd)
            S.flush()
        if upto <= 2:
            return nc, S

        with contextlib.ExitStack() as st:
            vaug, b_vaug = sb(st, "vaug", [128, 18, 8 * 130], BF16)
            kTs = [sb(st, "kT%d" % i, [128, TA], BF16) for i in range(2)]
            qTs = [sb(st, "qT%d" % i, [128, T], BF16) for i in range(2)]
            eTs = [sb(st, "eT%d" % i, [128, 1024], BF16) for i in range(3)]
            attst = [sb(st, "attst%d" % i, [128, T], BF16) for i in range(2)]
            o_t = [sb(st, "o_t%d" % i, [128, 128], F32) for i in range(2)]
            t_t = [sb(st, "t_t%d" % i, [128, 128], F32) for i in range(2)]
            a_t = [sb(st, "a_t%d" % i, [128, 128], BF16) for i in range(2)]
            jk, b_jk = sb(st, "jk", [128, 128], F32)
            sms = [sb(st, "sm%d" % i, [128, 4], F32) for i in range(2)]
            scs = [ps(st, "sc%d" % i, [128, 1024], F32) for i in range(2)]
            accs = [ps(st, "acc%d" % i, [128, 512], F32) for i in range(3)]
            b_acc = [Buf() for _ in range(8)]
            pT, b_pT = ps(st, "pT", [128, 1024], BF16)

            def accap(idx, lo, hi):
                return accs[idx // 3][0][:, (idx % 3) * 130 + lo:(idx % 3) * 130 + hi]

            load("sp", vaug[:], v_s.rearrange("(t p) f -> p t f", p=128), b_vaug, B["v_s"])

            def head_loads(h):
                load("sp", kTs[h % 2][0][:], kT_s[h], kTs[h % 2][1], B["kT_s"])
                load("sp", qTs[h % 2][0][:], qT_s[h], qTs[h % 2][1], B["qT_s"])

            steps = [(h, qb, kt) for h in range(8) for qb in range(4) for kt in range(18)]

            def emit_scores(i):
                h, qb, kt = steps[i]
                kT, b_kT = kTs[h % 2]; qT, b_qT = qTs[h % 2]
                sc, b_sc = scs[i % 2]; eT, b_eT = eTs[i % 3]
                for c in range(2):
                    S.op("pe", lambda e, c=c, kt=kt, qb=qb, sc=sc, kT=kT, qT=qT: e.matmul(
                        sc[:, c * 512:(c + 1) * 512], lhsT=kT[c * 64:(c + 1) * 64, kt * 128:(kt + 1) * 128],
                        rhs=qT[c * 64:(c + 1) * 64, qb * 512:(qb + 1) * 512], start=True, stop=True),
                        reads=[b_kT, b_qT], writes=[b_sc])
                S.op("act", lambda e, sc=sc, eT=eT: e.activation(out=eT[:], in_=sc[:], func=AF.Exp, scale=0.125),
                     reads=[b_sc], writes=[b_eT])

            def emit_pv(i):
                h, qb, kt = steps[i]
                eT, b_eT = eTs[i % 3]
                for c in range(2):
                    for qt in range(4):
                        idx = c * 4 + qt
                        first = (kt == 0 and idx % 3 == 0)
                        S.op("pe", lambda e, idx=idx, c=c, qt=qt, kt=kt, h=h, eT=eT, first=first: e.matmul(
                            accap(idx, 0, 129), lhsT=eT[:, c * 512 + qt * 128:c * 512 + (qt + 1) * 128],
                            rhs=vaug[:, kt, h * 130:h * 130 + 129], start=first, stop=(kt == 17)),
                            reads=[b_eT, b_vaug], writes=[b_acc[idx]] + ([b_acc[j] for j in range(idx, min(idx + 3, 8))] if first else []))

            def emit_norm(h, qb):
                ast, b_ast = attst[h % 2]
                for qt in range(4):
                    sm, b_sm = sms[qt % 2]; o, b_o = o_t[qt % 2]; tq, b_tq = t_t[qt % 2]; at, b_at = a_t[qt % 2]
                    S.op("dve", lambda e, qt=qt, sm=sm: e.reciprocal(out=sm[:, 0:1], in_=accap(qt, 128, 129)),
                         reads=[b_acc[qt]], writes=[b_sm])
                    S.op("dve", lambda e, qt=qt, sm=sm: e.reciprocal(out=sm[:, 1:2], in_=accap(4 + qt, 128, 129)),
                         reads=[b_acc[4 + qt], b_sm], writes=[b_sm])
                    S.op("dve", lambda e, sm=sm: e.tensor_tensor(out=sm[:, 1:2], in0=sm[:, 1:2], in1=nlam[:], op=ALU.mult),
                         reads=[b_sm, b_nlam], writes=[b_sm])
                    S.op("dve", lambda e, qt=qt, sm=sm, tq=tq: e.tensor_scalar(out=tq[:], in0=accap(4 + qt, 0, 128), scalar1=sm[:, 1:2],
                                                                                scalar2=None, op0=ALU.mult),
                         reads=[b_acc[4 + qt], b_sm], writes=[b_tq])
                    S.op("dve", lambda e, qt=qt, sm=sm, tq=tq, o=o: e.scalar_tensor_tensor(out=o[:], in0=accap(qt, 0, 128), scalar=sm[:, 0:1],
                                                                                            in1=tq[:], op0=ALU.mult, op1=ALU.add),
                         reads=[b_acc[qt], b_sm, b_tq], writes=[b_o])
                    S.op("pool", lambda e, o=o: e.tensor_tensor(out=jk[:], in0=o[:], in1=o[:], op=ALU.mult), reads=[b_o], writes=[b_jk])
                    S.op("dve", lambda e, sm=sm: e.reduce_sum(out=sm[:, 2:3], in_=jk[:], axis=mybir.AxisListType.X), reads=[b_jk, b_sm], writes=[b_sm])
                    S.op("pool", lambda e, sm=sm: e.tensor_scalar(out=sm[:, 2:3], in0=sm[:, 2:3], scalar1=1.0 / 128, scalar2=EPS,
                                                                   op0=ALU.mult, op1=ALU.add), reads=[b_sm], writes=[b_sm])
                    S.op("pool", lambda e, sm=sm: e.tensor_tensor(out=sm[:, 3:4], in0=sm[:, 2:3], in1=mhalf[:], op=ALU.pow),
                         reads=[b_sm, b_mhalf], writes=[b_sm])
                    S.op("dve", lambda e, o=o, sm=sm, at=at: e.scalar_tensor_tensor(out=at[:], in0=o[:], scalar=sm[:, 3:4], in1=slw[:],
                                                                                     op0=ALU.mult, op1=ALU.mult),
                         reads=[b_o, b_sm, b_slw], writes=[b_at])
                    S.op("pe", lambda e, qt=qt, at=at: e.transpose(out=pT[:, qt * 128:(qt + 1) * 128], in_=at[:], identity=identb[:]),
                         reads=[b_at, b_idb], writes=[b_pT])
                S.op("act", lambda e, qb=qb, ast=ast: e.activation(out=ast[:, qb * 512:(qb + 1) * 512], in_=pT[:, 0:512], func=AF.Copy),
                     reads=[b_pT], writes=[b_ast])
                if qb == 3:
                    store("sp", attT_s[h * 128:(h + 1) * 128, :], ast[:], b_ast, B["attT_s"])

            head_loads(0)
            head_loads(1)
            emit_scores(0)
            for i, (h, qb, kt) in enumerate(steps):
                if i + 1 < len(steps):
                    emit_scores(i + 1)
                emit_pv(i)
                if kt == 17:
                    emit_norm(h, qb)
                    if qb == 3 and h + 2 < 8:
                        head_loads(h + 2)
            S.flush()
        if upto <= 3:
            return nc, S

        with contextlib.ExitStack() as st:
            smask, b_smask = sb(st, "smask", [128, TA], F32)
            tri, b_tri = sb(st, "tri", [64, 2 * 8 * 64], BF16)
            frs = [sb(st, "fr%d" % i, [128, TA], F32) for i in range(2)]
            T1s = [sb(st, "T1_%d" % i, [128, TA], F32) for i in range(2)]
            T2s = [sb(st, "T2_%d" % i, [128, TA], F32) for i in range(2)]
            T3s = [sb(st, "T3_%d" % i, [128, TA], F32) for i in range(2)]
            rq, b_rq = sb(st, "rq", [128, T], BF16)
            v64, b_v64 = sb(st, "v64", [64, 36, 128], BF16)
            QT = [sb(st, "QT%d" % i, [128, T], BF16) for i in range(2)]
            Q2 = [sb(st, "Q2%d" % i, [128, T], BF16) for i in range(2)]
            KT = [sb(st, "KT%d" % i, [128, TA], BF16) for i in range(2)]
            K2 = [sb(st, "K2%d" % i, [128, TA], BF16) for i in range(2)]
            K2tok = [sb(st, "K2tok%d" % i, [64, 36, 128], BF16) for i in range(2)]
            msc = [sb(st, "msc%d" % i, [64, 32 * 64], BF16) for i in range(2)]
            o_d = [sb(st, "o_d%d" % i, [128, T], F32) for i in range(2)]
            osts = [sb(st, "ost%d" % i, [128, T], BF16) for i in range(2)]
            state = [sb(st, "state%d" % i, [128, 128], F32) for i in range(2)]
            statebf = [sb(st, "statebf%d" % i, [128, 128], BF16) for i in range(2)]
            statebf2 = [sb(st, "statebf2_%d" % i, [128, 128], BF16) for i in range(2)]
            dec = [sb(st, "dec%d" % i, [128, 36], F32) for i in range(2)]
            pks = [ps(st, "pk%d" % i, [128, 1024], BF16) for i in range(2)]
            pscs = [ps(st, "psc%d" % i, [128, 512], F32) for i in range(2)]
            pout = [[ps(st, "pout%d%d" % (d, i), [128, 512], F32) for i in range(1)] for d in range(2)]
            pupd = [ps(st, "pupd%d" % d, [128, 512], F32) for d in range(2)]
            load("sp", smask[:], smask_d[:, :], b_smask)
            load("sp", tri[:], tri_d[:, :], b_tri)
            v3 = lambda t: t[:].rearrange("p (c t) -> p c t", t=64)
            rvv = rv_s.rearrange("(c p) f -> p c f", p=64)

            def prep_ops(h, d):
                L = []
                Ft, b_F = frs[d]
                T1, b_T1 = T1s[d]; T2, b_T2 = T2s[d]; T3, b_T3 = T3s[d]
                pk, b_pk = pks[d]; psc, b_psc = pscs[d]
                col = d * 8 + h
                L.append(lambda: S.op("act", lambda e: e.activation(out=Ft[:], in_=Ft[:], func=AF.Sigmoid), reads=[b_F], writes=[b_F]))
                L.append(lambda: S.op("dve", lambda e: e.tensor_scalar(out=Ft[:], in0=Ft[:], scalar1=oml[:, col:col + 1], scalar2=low[:, col:col + 1],
                                                                        op0=ALU.mult, op1=ALU.add), reads=[b_F, b_oml, b_low], writes=[b_F]))
                L.append(lambda: S.op("act", lambda e: e.activation(out=T1[:], in_=Ft[:], func=AF.Ln), reads=[b_F], writes=[b_T1]))
                L.append(lambda: S.op("dve", lambda e: e.tensor_scalar(out=Ft[:], in0=Ft[:], scalar1=-1.0, scalar2=1.0, op0=ALU.mult, op1=ALU.add),
                                      reads=[b_F, b_T1], writes=[b_F]))
                L.append(lambda: S.op("dve", lambda e: e.tensor_tensor_scan(out=T2[:], data0=smask[:], data1=T1[:], initial=0.0, op0=ALU.mult, op1=ALU.add),
                                      reads=[b_smask, b_T1], writes=[b_T2]))
                if d == 0:
                    Tb, b_Tb, Tf, b_Tf = T2, b_T2, T1, b_T1
                    refi, endi = 31, 63
                else:
                    L.append(lambda: S.op("pool", lambda e: e.tensor_tensor(out=T1[:], in0=T1[:], in1=T2[:], op=ALU.subtract), reads=[b_T1, b_T2], writes=[b_T1]))
                    L.append(lambda: S.op("pool", lambda e: e.tensor_tensor(out=v3(T1), in0=v3(T1), in1=v3(T2)[:, :, 63:64].to_broadcast([128, 36, 64]), op=ALU.add),
                                          reads=[b_T1, b_T2], writes=[b_T1]))
                    Tb, b_Tb, Tf, b_Tf = T1, b_T1, T2, b_T2
                    refi, endi = 32, 0
                L.append(lambda: S.op("pool", lambda e: e.tensor_tensor(out=v3(Tf), in0=v3(Tb), in1=v3(Tb)[:, :, refi:refi + 1].to_broadcast([128, 36, 64]),
                                                                         op=ALU.subtract), reads=[b_Tb, b_Tf], writes=[b_Tf]))
                L.append(lambda: S.op("act", lambda e: e.activation(out=T3[:], in_=Tf[:], func=AF.Exp), reads=[b_Tf], writes=[b_T3]))
                L.append(lambda: S.op("dve", lambda e: e.tensor_tensor(out=QT[d][0][:], in0=rq[:], in1=T3[:, TC:TA], op=ALU.mult), reads=[b_rq, b_T3], writes=[QT[d][1]]))
                L.append(lambda: S.op("act", lambda e: e.activation(out=T3[:], in_=Tf[:], func=AF.Exp, scale=-1.0), reads=[b_Tf, QT[d][1]], writes=[b_T3]))
                L.append(lambda: S.op("dve", lambda e: e.tensor_tensor(out=KT[d][0][:], in0=Ft[:], in1=T3[:], op=ALU.mult), reads=[b_F, b_T3], writes=[KT[d][1]]))
                L.append(lambda: S.op("act", lambda e: e.activation(out=dec[d][0][:], in_=v3(Tb)[:, :, endi], func=AF.Exp), reads=[b_Tb], writes=[dec[d][1]]))
                L.append(lambda: S.op("pool", lambda e: e.tensor_tensor(out=v3(Tf), in0=v3(Tb), in1=v3(Tb)[:, :, endi:endi + 1].to_broadcast([128, 36, 64]),
                                                                         op=ALU.subtract), reads=[b_Tb, b_Tf, b_T3], writes=[b_Tf]))
                L.append(lambda: S.op("act", lambda e: e.activation(out=T3[:], in_=Tf[:], func=AF.Exp, scale=-1.0), reads=[b_Tf, KT[d][1]], writes=[b_T3]))
                L.append(lambda: S.op("dve", lambda e: e.tensor_tensor(out=K2[d][0][:], in0=Ft[:], in1=T3[:], op=ALU.mult), reads=[b_F, b_T3], writes=[K2[d][1]]))
                L.append(lambda: S.op("act", lambda e: e.activation(out=T3[:], in_=Tb[:], func=AF.Exp), reads=[b_Tb, K2[d][1]], writes=[b_T3]))
                L.append(lambda: S.op("dve", lambda e: e.tensor_tensor(out=Q2[d][0][:], in0=rq[:], in1=T3[:, TC:TA], op=ALU.mult), reads=[b_rq, b_T3], writes=[Q2[d][1]]))

                def transposes():
                    for c0 in range(0, 36, 8):
                        n = min(8, 36 - c0)
                        for cc in range(n):
                            c = c0 + cc
                            S.op("pe", lambda e, c=c, cc=cc: e.transpose(out=pk[0:64, cc * 128:(cc + 1) * 128], in_=K2[d][0][:, c * 64:(c + 1) * 64], identity=identb[:]),
                                 reads=[K2[d][1], b_idb], writes=[b_pk])
                        S.op("act", lambda e, c0=c0, n=n: e.activation(out=K2tok[d][0][:, c0:c0 + n, :].rearrange("p c k -> p (c k)"), in_=pk[0:64, 0:n * 128], func=AF.Copy),
                             reads=[b_pk], writes=[K2tok[d][1]])
                L.append(transposes)

                def scores():
                    for l0 in range(0, 32, 8):
                        for cc in range(8):
                            lc = l0 + cc
                            c = 4 + lc
                            S.op("pe", lambda e, c=c, lc=lc, cc=cc: e.matmul(psc[0:64, cc * 64:(cc + 1) * 64], lhsT=KT[d][0][:, c * 64:(c + 1) * 64],
                                                                              rhs=QT[d][0][:, lc * 64:(lc + 1) * 64], start=True, stop=True),
                                 reads=[KT[d][1], QT[d][1]], writes=[b_psc])
                        S.op("dve", lambda e, l0=l0: e.tensor_tensor(out=msc[d][0][:, l0 * 64:(l0 + 8) * 64], in0=psc[0:64, :], in1=tri[:, d * 512:(d + 1) * 512], op=ALU.mult),
                             reads=[b_psc, b_tri], writes=[msc[d][1]])
                L.append(scores)
                return L

            for h in range(8):
                for d in range(2):
                    load("sp", frs[d][0][:], rf_s[d, h * 128:(h + 1) * 128, :], frs[d][1], B["rf_s"])
                load("sp", rq[:], rq_s[h * 128:(h + 1) * 128, :], b_rq, B["rq_s"])
                load("sp", v64[:], rvv[:, :, h * 128:(h + 1) * 128], b_v64, B["rv_s"])
                Ls = [prep_ops(h, 0), prep_ops(h, 1)]
                for k in range(max(len(Ls[0]), len(Ls[1]))):
                    for d in range(2):
                        if k < len(Ls[d]):
                            Ls[d][k]()
                for d in range(2):
                    S.op("pool", lambda e, d=d: e.memset(state[d][0][:], 0.0), writes=[state[d][1]])
                    S.op("pool", lambda e, d=d: e.memset(statebf[d][0][:], 0.0), writes=[statebf[d][1]])
                order = [list(range(36)), [3, 2, 1, 0] + list(range(35, 3, -1))]
                for step in range(36):
                    for d in range(2):
                        c = order[d][step]
                        sb_prev, b_sbp = (statebf, statebf2)[(step + 1) % 2][d]
                        sb_next, b_sbn = (statebf, statebf2)[step % 2][d]
                        pu, b_pu = pupd[d]
                        if step < 35:
                            S.op("pe", lambda e, d=d, c=c, pu=pu: e.matmul(pu[:, 0:128], lhsT=K2tok[d][0][:, c, :], rhs=v64[:, c, :], start=True, stop=True),
                                 reads=[K2tok[d][1], b_v64], writes=[b_pu])
                        if c >= 4:
                            lc = c - 4
                            grp = lc // 8
                            po, b_po = pout[d][0]
                            slot = lc % 8
                            S.op("pe", lambda e, d=d, c=c, lc=lc, slot=slot, po=po: e.matmul(po[:, slot * 64:(slot + 1) * 64], lhsT=v64[:, c, :],
                                                                                           rhs=msc[d][0][:, lc * 64:(lc + 1) * 64], start=True, stop=False),
                                 reads=[b_v64, msc[d][1]], writes=[b_po])
                            S.op("pe", lambda e, d=d, lc=lc, slot=slot, po=po, sb_prev=sb_prev: e.matmul(po[:, slot * 64:(slot + 1) * 64], lhsT=sb_prev[:],
                                                                                                       rhs=Q2[d][0][:, lc * 64:(lc + 1) * 64], start=False, stop=True),
                                 reads=[b_sbp, Q2[d][1]], writes=[b_po])
                            last_in_grp = (slot == 7) if d == 0 else (slot == 0)
                            if last_in_grp:
                                S.op("act", lambda e, d=d, grp=grp, po=po: e.activation(out=o_d[d][0][:, grp * 512:(grp + 1) * 512], in_=po[:], func=AF.Copy),
                                     reads=[b_po], writes=[o_d[d][1]])
                        if step == 35:
                            continue
                        S.op("dve", lambda e, d=d, c=c, pu=pu: e.scalar_tensor_tensor(out=state[d][0][:], in0=state[d][0][:], scalar=dec[d][0][:, c:c + 1], in1=pu[:, 0:128],
                                                                                       op0=ALU.mult, op1=ALU.add),
                             reads=[state[d][1], dec[d][1], b_pu], writes=[state[d][1]])
                        S.op("act", lambda e, d=d, sb_next=sb_next: e.activation(out=sb_next[:], in_=state[d][0][:], func=AF.Copy),
                             reads=[state[d][1]], writes=[b_sbn])
                ost, b_ost = osts[h % 2]
                S.op("dve", lambda e, ost=ost: e.tensor_tensor(out=ost[:], in0=o_d[0][0][:], in1=o_d[1][0][:], op=ALU.add),
                     reads=[o_d[0][1], o_d[1][1]], writes=[b_ost])
                store("sp", o_s[h * 128:(h + 1) * 128, :], ost[:], b_ost, B["o_s"])
            pss = [pout[0][0], pout[1][0], pupd[0], pupd[1]]
            sq, b_sq = frs[0]
            rstd, b_rstd = T1s[0]
            mh2, b_mh2 = T2s[0]
            rec32, b_rec32 = T3s[0]
            S.op("pool", lambda e: e.memset(mh2[:, 0:T], -0.5), reads=[], writes=[b_mh2])
            sqb = sq[:].bitcast(BF16)
            for h in range(8):
                oh, b_oh = QT[h % 2]
                load("sp", oh[:], o_s[h * 128:(h + 1) * 128, :], b_oh, B["o_s"])
                S.op("dve", lambda e, oh=oh: e.tensor_tensor(out=sqb[:, 0:T], in0=oh[:], in1=oh[:], op=ALU.mult),
                     reads=[b_oh], writes=[b_sq])
                for tb in range(4):
                    S.op("pe", lambda e, h=h, tb=tb: e.matmul(pss[tb][0][:], lhsT=onesb[:], rhs=sqb[:, tb * 512:(tb + 1) * 512], start=(h == 0), stop=(h == 7)),
                         reads=[b_sq, b_onesb], writes=[pss[tb][1]])
            for tb in range(4):
                S.op("dve", lambda e, tb=tb: e.tensor_scalar(out=rstd[:, tb * 512:(tb + 1) * 512], in0=pss[tb][0][:], scalar1=1.0 / 1024, scalar2=EPS, op0=ALU.mult, op1=ALU.add),
                     reads=[pss[tb][1]], writes=[b_rstd])
            S.op("pool", lambda e: e.tensor_tensor(out=rstd[:, 0:T], in0=rstd[:, 0:T], in1=mh2[:, 0:T], op=ALU.pow), reads=[b_rstd, b_mh2], writes=[b_rstd])
            for h in range(8):
                oh, b_oh = QT[h % 2]; rgt, b_rgt = Q2[h % 2]; rst_, b_rst_ = osts[h % 2]
                load("sp", oh[:], o_s[h * 128:(h + 1) * 128, :], b_oh, B["o_s"])
                load("sp", rgt[:], rg_s[h * 128:(h + 1) * 128, :], b_rgt, B["rg_s"])
                S.op("dve", lambda e, h=h, oh=oh: e.scalar_tensor_tensor(out=rec32[:, 0:T], in0=oh[:], scalar=fm[:, V_GN + h:V_GN + h + 1], in1=rstd[:, 0:T],
                                                                          op0=ALU.mult, op1=ALU.mult), reads=[b_oh, b_fm, b_rstd], writes=[b_rec32])
                S.op("dve", lambda e, rst_=rst_, rgt=rgt: e.tensor_tensor(out=rst_[:], in0=rec32[:, 0:T], in1=rgt[:], op=ALU.mult), reads=[b_rec32, b_rgt], writes=[b_rst_])
                store("sp", recT_s[h * 128:(h + 1) * 128, :], rst_[:], b_rst_, B["recT_s"])
            S.flush()
        if upto <= 4:
            return nc, S

        with contextlib.ExitStack() as st:
            wba, b_wba = sb(st, "wba", [128, 8, D], BF16)
            wbr, b_wbr = sb(st, "wbr", [128, 8, D], BF16)
            load("pool", wba[:], wba_d.rearrange("(kc p) n -> p kc n", p=128), b_wba)
            load("pool", wbr[:], wbr_d.rearrange("(kc p) n -> p kc n", p=128), b_wbr)
            attb = [sb(st, "attb%d" % i, [128, 8, 256], BF16) for i in range(2)]
            recb = [sb(st, "recb%d" % i, [128, 8, 256], BF16) for i in range(2)]
            gab = [sb(st, "gab%d" % i, [128, 16, 256], BF16) for i in range(2)]
            grb = [sb(st, "grb%d" % i, [128, 16, 256], BF16) for i in range(2)]
            yTb = [sb(st, "yTb%d" % i, [128, 16, 256], BF16) for i in range(2)]
            tas = [sb(st, "ta%d" % i, [128, 256], F32) for i in range(2)]
            trs = [sb(st, "tr%d" % i, [128, 256], F32) for i in range(2)]
            pas = [ps(st, "pa%d" % i, [128, 512], F32) for i in range(2)]
            prs = [ps(st, "pr%d" % i, [128, 512], F32) for i in range(2)]
            attv = attT_s.rearrange("(kc p) t -> p kc t", p=128)
            recv = recT_s.rearrange("(kc p) t -> p kc t", p=128)
            gav = ga_s.rearrange("(kc p) t -> p kc t", p=128)
            grv = gr_s.rearrange("(kc p) t -> p kc t", p=128)
            yTv = yT_s.rearrange("(kc p) t -> p kc t", p=128)
            for tb in range(8):
                s_ = tb % 2
                t0 = tb * 256
                load("sp", attb[s_][0][:], attv[:, :, t0:t0 + 256], attb[s_][1], B["attT_s"])
                load("sp", recb[s_][0][:], recv[:, :, t0:t0 + 256], recb[s_][1], B["recT_s"])
                load("sp", gab[s_][0][:], gav[:, :, t0:t0 + 256], gab[s_][1], B["ga_s"])
                load("sp", grb[s_][0][:], grv[:, :, t0:t0 + 256], grb[s_][1], B["gr_s"])
                for dc in range(16):
                    pa, b_pa = pas[dc % 2]; pr, b_pr = prs[dc % 2]
                    ta, b_ta = tas[dc % 2]; tr, b_tr = trs[dc % 2]
                    for kc in range(8):
                        S.op("pe", lambda e, kc=kc, dc=dc, pa=pa, s_=s_: e.matmul(pa[:, 0:256], lhsT=wba[:, kc, dc * 128:(dc + 1) * 128], rhs=attb[s_][0][:, kc, :],
                                                                                start=(kc == 0), stop=(kc == 7)), reads=[b_wba, attb[s_][1]], writes=[b_pa])
                    for kc in range(8):
                        S.op("pe", lambda e, kc=kc, dc=dc, pr=pr, s_=s_: e.matmul(pr[:, 0:256], lhsT=wbr[:, kc, dc * 128:(dc + 1) * 128], rhs=recb[s_][0][:, kc, :],
                                                                                start=(kc == 0), stop=(kc == 7)), reads=[b_wbr, recb[s_][1]], writes=[b_pr])
                    S.op("dve", lambda e, dc=dc, pa=pa, ta=ta, s_=s_: e.tensor_tensor(out=ta[:], in0=pa[:, 0:256], in1=gab[s_][0][:, dc, :], op=ALU.mult),
                         reads=[b_pa, gab[s_][1]], writes=[b_ta])
                    S.op("dve", lambda e, dc=dc, pr=pr, tr=tr, s_=s_: e.tensor_tensor(out=tr[:], in0=pr[:, 0:256], in1=grb[s_][0][:, dc, :], op=ALU.mult),
                         reads=[b_pr, grb[s_][1]], writes=[b_tr])
                    S.op("dve", lambda e, dc=dc, ta=ta, tr=tr, s_=s_: e.tensor_tensor(out=yTb[s_][0][:, dc, :], in0=ta[:], in1=tr[:], op=ALU.add),
                         reads=[b_ta, b_tr], writes=[yTb[s_][1]])
                store("sp", yTv[:, :, t0:t0 + 256], yTb[s_][0][:], yTb[s_][1], B["yT_s"])
            S.flush()
        if upto <= 5:
            return nc, S

        with contextlib.ExitStack() as st:
            wout, b_wout = sb(st, "wout", [128, 16, D], BF16)
            wov = wout_d.rearrange("(kc p) n -> p kc n", p=128)
            load("pool", wout[:, 0:8, :], wov[:, 0:8, :], b_wout)
            load("pool", wout[:, 8:16, :], wov[:, 8:16, :], b_wout)
            g1bc, b_g1 = sb(st, "g1bc", [128, D], F32)
            load("sp", g1bc[:], mod_s[0:1, 2 * D:3 * D].partition_broadcast(128), b_g1, B["mod_s"])
            ybs = [sb(st, "yb%d" % i, [128, 16, 512], BF16) for i in range(2)]
            xts = [sb(st, "xt%d" % i, [128, D], F32) for i in range(2)]
            x1ts = [sb(st, "x1t%d" % i, [128, D], F32) for i in range(2)]
            tts = [sb(st, "tt%d" % i, [128, 512], F32) for i in range(2)]
            junk, b_junk = sb(st, "junk", [128, D], BF16)
            ssqs = [sb(st, "ssq%d" % i, [128, 1], F32) for i in range(2)]
            xns = [sb(st, "xn%d" % i, [128, D], BF16) for i in range(2)]
            h2st, b_h2st = sb(st, "h2st", [128, 16, 512], BF16)
            pts = [ps(st, "pt%d" % i, [128, 1024], BF16) for i in range(2)]
            pos = [ps(st, "po%d" % i, [128, 512], F32) for i in range(4)]
            yTv = yT_s.rearrange("(kc p) t -> p kc t", p=128)
            h2v = h2T_s.rearrange("(kc p) t -> p kc t", p=128)
            def p5a(tt):
                tb, q = tt // 4, tt % 4
                yb, b_yb = ybs[tb % 2]
                if q == 0:
                    load("sp", yb[:], yTv[:, :, tb * 512:(tb + 1) * 512], b_yb, B["yT_s"])
                xt, b_xt = xts[tt % 2]; x1t, b_x1t = x1ts[tt % 2]
                load("sp", xt[:], x_d[tt * 128:(tt + 1) * 128, :], b_xt)
                for db in range(4):
                    po, b_po = pos[db]
                    tq, b_tq = tts[db % 2]
                    for dc in range(16):
                        S.op("pe", lambda e, dc=dc, q=q, db=db, po=po, yb=yb: e.matmul(po[:], lhsT=yb[:, dc, q * 128:(q + 1) * 128], rhs=wout[:, dc, db * 512:(db + 1) * 512],
                                                                                     start=(dc == 0), stop=(dc == 15)), reads=[b_yb, b_wout], writes=[b_po])
                    S.op("dve", lambda e, db=db, po=po, tq=tq: e.tensor_tensor(out=tq[:], in0=po[:], in1=g1bc[:, db * 512:(db + 1) * 512], op=ALU.mult),
                         reads=[b_po, b_g1], writes=[b_tq])
                    S.op("dve", lambda e, db=db, tq=tq, xt=xt, x1t=x1t: e.tensor_tensor(out=x1t[:, db * 512:(db + 1) * 512], in0=tq[:], in1=xt[:, db * 512:(db + 1) * 512], op=ALU.add),
                         reads=[b_tq, b_xt], writes=[b_x1t])
                store("sp", x1_s[tt * 128:(tt + 1) * 128, :], x1t[:], b_x1t, B["x1_s"])
                norm_p1(x1t[:], b_x1t, junk, b_junk, ssqs[tt % 2][0], ssqs[tt % 2][1], xns[tt % 2][0], xns[tt % 2][1])

            def p5b(tt):
                tb, q = tt // 4, tt % 4
                norm_p2(xns[tt % 2][0], xns[tt % 2][1], pts, lambda kc, q=q: h2st[:, kc, q * 128:(q + 1) * 128], b_h2st,
                        lambda kc: A2[:, kc:kc + 1], lambda kc: modT[:, 48 + kc, 0:1], b_A2, b_modT)
                if q == 3:
                    store("sp", h2v[:, :, tb * 512:(tb + 1) * 512], h2st[:], b_h2st, B["h2T_s"])

            p5a(0)
            for tt in range(16):
                if tt + 1 < 16:
                    p5a(tt + 1)
                p5b(tt)
            S.flush()
        if upto <= 6:
            return nc, S

        wupv = wup_d.rearrange("(kc p) n -> p kc n", p=128)
        wdnv = wdn_d.rearrange("(fc p) n -> p fc n", p=128)
        h2v = h2T_s.rearrange("(kc p) t -> p kc t", p=128)
        for blk in range(2):
            tok0 = blk * 1024
            with contextlib.ExitStack() as st:
                gT, b_gT = sb(st, "gT", [128, 44, 1024], BF16)
                with contextlib.ExitStack() as st2:
                    h2b, b_h2b = sb(st2, "h2b", [128, 16, 1024], BF16)
                    halo, b_halo = sb(st2, "halo", [128, 16, 2], BF16)
                    was = [sb(st2, "wa%d" % i, [128, 16, 256], BF16) for i in range(2)]
                    wbs = [sb(st2, "wb%d" % i, [128, 16, 256], BF16) for i in range(2)]
                    uxs = [sb(st2, "ux%d" % i, [128, 1026], F32) for i in range(2)]
                    tcs = [sb(st2, "tc%d" % i, [128, 1024], F32) for i in range(2)]
                    pus = [ps(st2, "pu%d" % i, [128, 512], F32) for i in range(6)]
                    ph, b_ph = ps(st2, "ph", [128, 512], F32)
                    load("sp", h2b[:], h2v[:, :, tok0:tok0 + 1024], b_h2b, B["h2T_s"])
                    S.op("pool", lambda e: e.memset(halo[:], 0.0), writes=[b_halo])
                    if blk == 1:
                        load("sp", halo[:, :, 0:1], h2v[:, :, tok0 - 1:tok0], b_halo, B["h2T_s"], slow=True)
                    else:
                        load("sp", halo[:, :, 1:2], h2v[:, :, tok0 + 1024:tok0 + 1025], b_halo, B["h2T_s"], slow=True)
                    ipu = 0
                    iph = 0
                    for i2 in range(22):
                        wa, b_wa = was[i2 % 2]; wb, b_wb = wbs[i2 % 2]
                        load("pool", wa[:], wupv[:, :, i2 * 256:(i2 + 1) * 256], b_wa)
                        load("pool", wb[:], wupv[:, :, FF + i2 * 256:FF + (i2 + 1) * 256], b_wb)
                        for jj in range(2):
                            i = i2 * 2 + jj
                            for part in range(2):
                                wtile, b_wt_ = (wa, b_wa) if part == 0 else (wb, b_wb)
                                ux, b_ux = uxs[part]; tcv, b_tc = tcs[part]
                                col = i + 44 * part
                                for sbk in range(2):
                                    pu, b_pu = pus[ipu % 6]; ipu += 1
                                    for kc in range(16):
                                        S.op("pe", lambda e, kc=kc, jj=jj, sbk=sbk, pu=pu, wtile=wtile: e.matmul(pu[:], lhsT=wtile[:, kc, jj * 128:(jj + 1) * 128],
                                                                                                                rhs=h2b[:, kc, sbk * 512:(sbk + 1) * 512], start=(kc == 0), stop=(kc == 15)),
                                             reads=[b_wt_, b_h2b], writes=[b_pu])
                                    S.op("act", lambda e, sbk=sbk, pu=pu, ux=ux: e.activation(out=ux[:, 1 + sbk * 512:1 + (sbk + 1) * 512], in_=pu[:], func=AF.Copy),
                                         reads=[b_pu], writes=[b_ux])
                                hs = (iph % 8) * 2; iph += 1
                                for kc in range(16):
                                    S.op("pe", lambda e, kc=kc, jj=jj, hs=hs, wtile=wtile: e.matmul(ph[:, hs:hs + 2], lhsT=wtile[:, kc, jj * 128:(jj + 1) * 128], rhs=halo[:, kc, :],
                                                                                                  start=(kc == 0), stop=(kc == 15)), reads=[b_wt_, b_halo], writes=[b_ph])
                                S.op("act", lambda e, hs=hs, ux=ux: e.activation(out=ux[:, 0:1], in_=ph[:, hs:hs + 1], func=AF.Copy), reads=[b_ph], writes=[b_ux])
                                S.op("act", lambda e, hs=hs, ux=ux: e.activation(out=ux[:, 1025:1026], in_=ph[:, hs + 1:hs + 2], func=AF.Copy), reads=[b_ph], writes=[b_ux])
                                S.op("act", lambda e, ux=ux, tcv=tcv, col=col: e.activation(out=tcv[:], in_=ux[:, 1:1025], func=AF.Identity,
                                                                                           scale=fm[:, V_CW + 88 + col:V_CW + 88 + col + 1], bias=fm[:, V_CB + col:V_CB + col + 1]),
                                     reads=[b_ux, b_fm], writes=[b_tc])
                                S.op("dve", lambda e, ux=ux, tcv=tcv, col=col: e.scalar_tensor_tensor(out=tcv[:], in0=ux[:, 0:1024], scalar=fm[:, V_CW + col:V_CW + col + 1], in1=tcv[:],
                                                                                                     op0=ALU.mult, op1=ALU.add), reads=[b_ux, b_fm, b_tc], writes=[b_tc])
                                S.op("dve", lambda e, ux=ux, tcv=tcv, col=col: e.scalar_tensor_tensor(out=tcv[:], in0=ux[:, 2:1026], scalar=fm[:, V_CW + 176 + col:V_CW + 176 + col + 1], in1=tcv[:],
                                                                                                     op0=ALU.mult, op1=ALU.add), reads=[b_ux, b_fm, b_tc], writes=[b_tc])
                            S.op("act", lambda e: e.activation(out=tcs[0][0][:], in_=tcs[0][0][:], func=AF.Silu), reads=[tcs[0][1]], writes=[tcs[0][1]])
                            S.op("dve", lambda e, i=i: e.tensor_tensor(out=gT[:, i, :], in0=tcs[0][0][:], in1=tcs[1][0][:], op=ALU.mult),
                                 reads=[tcs[0][1], tcs[1][1]], writes=[b_gT])
                    S.flush()
                with contextlib.ExitStack() as st2:
                    g2bc, b_g2 = sb(st2, "g2bc", [128, D], F32)
                    load("sp", g2bc[:], mod_s[0:1, 5 * D:6 * D].partition_broadcast(128), b_g2, B["mod_s"])
                    wds = [sb(st2, "wd%d" % i, [128, 4, 512], BF16) for i in range(4)]
                    wdf = [sb(st2, "wdf%d" % i, [128, 4, 512], F32) for i in range(4)]
                    x1p = [sb(st2, "x1p%d" % i, [128, 512], F32) for i in range(8)]
                    tps = [sb(st2, "tp%d" % i, [128, 512], F32) for i in range(8)]
                    pds = [ps(st2, "pd%d" % i, [128, 512], F32) for i in range(8)]
                    iw = 0
                    for db in range(4):
                        for tt in range(8):
                            r0 = tok0 + tt * 128
                            load("sp", x1p[tt][0][:], x1_s[r0:r0 + 128, db * 512:(db + 1) * 512], x1p[tt][1], B["x1_s"])
                        for f4 in range(11):
                            wd, b_wd = wds[iw % 4]; wf, b_wf = wdf[iw % 4]; iw += 1
                            load("act", wf[:], wdnv[:, f4 * 4:(f4 + 1) * 4, db * 512:(db + 1) * 512], b_wf)
                            S.op("act", lambda e, wf=wf, wd=wd: e.activation(out=wd[:].rearrange("p a b -> p (a b)"), in_=wf[:].rearrange("p a b -> p (a b)"), func=AF.Copy),
                                 reads=[b_wf], writes=[b_wd])
                            for fj in range(4):
                                fc = f4 * 4 + fj
                                for tt in range(8):
                                    S.op("pe", lambda e, fc=fc, fj=fj, tt=tt, wd=wd: e.matmul(pds[tt][0][:], lhsT=gT[:, fc, tt * 128:(tt + 1) * 128], rhs=wd[:, fj, :],
                                                                                            start=(fc == 0), stop=(fc == 43)), reads=[b_gT, b_wd], writes=[pds[tt][1]])
                        for tt in range(8):
                            S.op("dve", lambda e, tt=tt, db=db: e.tensor_tensor(out=tps[tt][0][:], in0=pds[tt][0][:], in1=g2bc[:, db * 512:(db + 1) * 512], op=ALU.mult),
                                 reads=[pds[tt][1], b_g2], writes=[tps[tt][1]])
                        for tt in range(8):
                            r0 = tok0 + tt * 128
                            S.op("pool", lambda e, tt=tt: e.tensor_tensor(out=tps[tt][0][:], in0=tps[tt][0][:], in1=x1p[tt][0][:], op=ALU.add),
                                 reads=[tps[tt][1], x1p[tt][1]], writes=[tps[tt][1]])
                            store("sp", x2_s[r0:r0 + 128, db * 512:(db + 1) * 512], tps[tt][0][:], tps[tt][1], B["x2_s"])
                    S.flush()
        if upto <= 7:
            return nc, S

        with contextlib.ExitStack() as st:
            fnw, b_fnw = sb(st, "fnw", [128, D], F32)
            load("sp", fnw[:], fnw_d.partition_broadcast(128), b_fnw)
            xts = [sb(st, "xf%d" % i, [128, D], F32) for i in range(2)]
            ots = [sb(st, "of%d" % i, [128, D], F32) for i in range(2)]
            junk, b_junk = sb(st, "junk", [128, D], BF16)
            ssqs = [sb(st, "ssq%d" % i, [128, 1], F32) for i in range(2)]
            for tt in range(16):
                xt, b_xt = xts[tt % 2]; ot, b_ot = ots[tt % 2]; ssq, b_ssq = ssqs[tt % 2]
                load("sp", xt[:], x2_s[tt * 128:(tt + 1) * 128, :], b_xt, B["x2_s"])
                S.op("act", lambda e, xt=xt: e.activation(out=junk[:], in_=xt[:], func=AF.Square), reads=[b_xt], writes=[b_junk])
                S.op("dve", lambda e, ssq=ssq: e.reduce_sum(out=ssq[:], in_=junk[:], axis=mybir.AxisListType.X), reads=[b_junk], writes=[b_ssq])
                S.op("pool", lambda e, ssq=ssq: e.tensor_scalar(out=ssq[:], in0=ssq[:], scalar1=1.0 / D, scalar2=EPS, op0=ALU.mult, op1=ALU.add), reads=[b_ssq], writes=[b_ssq])
                S.op("pool", lambda e, ssq=ssq: e.tensor_tensor(out=ssq[:], in0=ssq[:], in1=mhalf[:], op=ALU.pow), reads=[b_ssq, b_mhalf], writes=[b_ssq])
                S.op("dve", lambda e, xt=xt, ot=ot, ssq=ssq: e.scalar_tensor_tensor(out=ot[:], in0=xt[:], scalar=ssq[:, 0:1], in1=fnw[:], op0=ALU.mult, op1=ALU.mult),
                     reads=[b_xt, b_ssq, b_fnw], writes=[b_ot])
                store("sp", out_d[tt * 128:(tt + 1) * 128, :], ot[:], b_ot, B["out"])
            S.flush()
        return nc, S


_CONST = None


def _consts():
    global _CONST
    if _CONST is not None:
        return _CONST
    bf = ml_dtypes.bfloat16
    rows = T // 64
    r, col = np.meshgrid(np.arange(rows), np.arange(64), indexing="ij")
    pos = np.stack([r.reshape(-1), col.reshape(-1)], axis=-1).astype(np.float32)
    inv = (np.float32(10000.0) ** (-(np.arange(16, dtype=np.float32)) / np.float32(16))).astype(np.float32)
    ang = (pos[:, :, None] * inv).astype(np.float32)
    cs, sn = np.cos(ang).astype(np.float32), np.sin(ang).astype(np.float32)
    cosT = np.zeros((128, T), np.float32); sinT = np.zeros((128, T), np.float32)
    perm = np.zeros((128, 128), np.float32)
    for p in range(128):
        a, h, i = (p % 64) // 32, (p % 32) // 16, p % 16
        cosT[p] = cs[:, a, i]
        sinT[p] = sn[:, a, i] * (-1.0 if h == 0 else 1.0)
        partner = p + 16 if h == 0 else p - 16
        perm[partner, p] = 1.0
    s_, t_ = np.meshgrid(np.arange(64), np.arange(64), indexing="ij")
    tri = np.stack([(t_ >= s_), (t_ <= s_)], 0).astype(np.float32)
    tri = np.broadcast_to(tri.transpose(1, 0, 2)[:, :, None, :], (64, 2, 8, 64)).reshape(64, 1024)
    smask = np.ones((128, TA), np.float32); smask[:, ::64] = 0.0
    _CONST = dict(cosT=cosT, sinT=sinT, perm=perm.astype(bf), identb=np.eye(128, dtype=np.float32).astype(bf),
                  identf=np.eye(128, dtype=np.float32), tri=np.ascontiguousarray(tri).astype(bf), smask=smask)
    return _CONST


def make_in_maps(inputs):
    f = lambda a: np.ascontiguousarray(np.asarray(a, dtype=np.float32))
    x = f(inputs["x"]); c = f(inputs["c"]); ctx = f(inputs["ctx"]); c_ctx = f(inputs["c_ctx"])
    shared = dict(_consts())
    shared["w_mod"] = f(inputs["w_mod"][0]); shared["w_in"] = f(inputs["w_in"][0])
    shared["b_mod2"] = np.ascontiguousarray(np.stack([f(inputs["b_mod"][0])] * 2, 0))
    shared["lam4"] = np.concatenate([f(inputs[k][0]) for k in ("lam_q1", "lam_k1", "lam_q2", "lam_k2")]).reshape(1, 256)
    shared["subln"] = f(inputs["subln_w"][0]).reshape(1, 128)
    shared["w_ba"] = f(inputs["w_branch_attn"][0]); shared["w_br"] = f(inputs["w_branch_rec"][0]); shared["w_out"] = f(inputs["w_out"][0])
    shared["w_up"] = f(inputs["w_up"][0]); shared["w_down"] = f(inputs["w_down"][0]); shared["fnw"] = f(inputs["final_norm_w"]).reshape(1, D)
    vec_tail = np.concatenate([f(inputs["norm1_w"][0]), f(inputs["norm2_w"][0]), f(inputs["rec_gnorm_w"][0]),
                               f(inputs["rec_lb"]).reshape(-1), f(inputs["conv_w"][0]).reshape(-1), f(inputs["conv_b"][0])])
    maps = []
    for b in range(8):
        m = dict(shared)
        m["x"] = x[b]; m["ctx"] = ctx[b]
        m["vecs"] = np.ascontiguousarray(np.concatenate([c[b], c_ctx, vec_tail]).reshape(NV, 128))
        maps.append(m)
    return maps


_NC = None


def kernel(**inputs):
    global _NC
    if _NC is None:
        _NC = build()[0]
    maps = make_in_maps(inputs)
    res = run_bass_kernel_spmd(_NC, maps, core_ids=list(range(8)))
    return np.stack([np.asarray(r["out"], dtype=np.float32) for r in res.results], 0)
```

```python
import contextlib
import numpy as np
import ml_dtypes
import concourse.bass as bass
import concourse.mybir as mybir
from concourse.bass_utils import run_bass_kernel_spmd

F32 = mybir.dt.float32
BF16 = mybir.dt.bfloat16
AF = mybir.ActivationFunctionType
ALU = mybir.AluOpType

D = 2048
T = 2048
TC = 256
TA = T + TC
NIN = 12288
FF = 5632
EPS = 1e-6
LAM_INIT = 0.2

V_C, V_CC, V_N1, V_N2, V_GN, V_LB, V_CW, V_CB = 0, 16, 32, 48, 64, 72, 104, 368
NV = 456


class Buf:
    __slots__ = ("w", "rs")

    def __init__(self):
        self.w = None
        self.rs = []


class Op:
    __slots__ = ("eng", "fn", "deps", "signaled", "val", "is_dma", "sem", "epoch")

    def __init__(self, eng, fn, is_dma, epoch):
        self.eng = eng
        self.fn = fn
        self.deps = []
        self.signaled = False
        self.val = None
        self.is_dma = is_dma
        self.sem = None
        self.epoch = epoch


class Sched:
    ENGS = ("pe", "act", "dve", "pool", "sp")
    NDMA = 8

    def __init__(self, nc, es):
        self.nc = nc
        self.csem = {e: es.enter_context(nc.semaphore("c_" + e)) for e in ("pe", "act", "dve", "pool")}
        self.dsem = {q: [es.enter_context(nc.semaphore("d_%s%d" % (q, i))) for i in range(self.NDMA)]
                     for q in ("sp", "pool", "act")}
        self.psem = es.enter_context(nc.semaphore("phase"))
        self.cval = {e: 0 for e in self.csem}
        self.dval = {q: [0] * self.NDMA for q in self.dsem}
        self.dlast = {q: [None] * self.NDMA for q in self.dsem}
        self.drr = {q: 0 for q in self.dsem}
        self.waited = {e: {} for e in self.ENGS}
        self.ops = {e: [] for e in self.ENGS}
        self.epoch = 0
        self.nops = 0

    def _add_dep(self, op, dep):
        if dep is None or dep is op or dep.epoch != self.epoch:
            return
        if dep.eng == "pe" and op.eng == "pe" and not dep.is_dma and not op.is_dma:
            return
        if dep not in op.deps:
            op.deps.append(dep)

    def op(self, eng, fn, reads=(), writes=()):
        o = Op(eng, fn, False, self.epoch)
        self._track(o, reads, writes)
        self.ops[eng].append(o)
        return o

    def dma(self, q, fn, reads=(), writes=()):
        o = Op(q, fn, True, self.epoch)
        self._track(o, reads, writes)
        self.ops[q].append(o)
        return o

    def _track(self, o, reads, writes):
        for b in reads:
            self._add_dep(o, b.w)
        for b in writes:
            for r in b.rs:
                self._add_dep(o, r)
            self._add_dep(o, b.w)
        for b in reads:
            b.rs.append(o)
        for b in writes:
            b.w = o
            b.rs = []

    def flush(self):
        nc = self.nc
        ops = self.ops
        fence = Op("sp", None, False, self.epoch)
        for e in self.ENGS:
            last = None
            for o in ops[e]:
                if o.is_dma:
                    fence.deps.append(o)
                else:
                    last = o
            if last is not None and e != "sp":
                fence.deps.append(last)
        ops["sp"].append(fence)
        for e in self.ENGS:
            for o in ops[e]:
                for d in o.deps:
                    d.signaled = True
        for e in self.ENGS:
            for o in ops[e]:
                if o.fn is None:
                    continue
                if o.is_dma:
                    q = o.eng
                    i = self.drr[q]
                    self.drr[q] = (i + 1) % self.NDMA
                    prev = self.dlast[q][i]
                    if prev is not None and prev.epoch == self.epoch:
                        o.deps.append(prev)
                    self.dval[q][i] += 16
                    o.sem = self.dsem[q][i]
                    o.val = self.dval[q][i]
                    self.dlast[q][i] = o
                elif o.signaled:
                    self.cval[e] += 1
                    o.sem = self.csem[e]
                    o.val = self.cval[e]
        waited = self.waited
        epoch = self.epoch
        psem = self.psem

        def emit(ename, eng):
            w = waited[ename]
            if epoch > 0:
                eng.wait_ge(psem, epoch)
            for o in ops[ename]:
                for d in o.deps:
                    k = d.sem.num
                    if w.get(k, 0) >= d.val:
                        continue
                    eng.wait_ge(d.sem, d.val)
                    w[k] = d.val
                if o.fn is None:
                    eng.sem_inc(psem, 1)
                    continue
                ins = o.fn(eng)
                self.nops += 1
                if o.is_dma:
                    ins.then_inc(o.sem, 16)
                elif o.signaled:
                    ins.then_inc(o.sem, 1)

        with nc.Block() as block:
            @block.sync
            def _(e):
                emit("sp", e)

            @block.gpsimd
            def _(e):
                emit("pool", e)

            @block.scalar
            def _(e):
                emit("act", e)

            @block.vector
            def _(e):
                emit("dve", e)

            @block.tensor
            def _(e):
                emit("pe", e)
        self.ops = {e: [] for e in self.ENGS}
        self.epoch += 1


def build(upto=99, debug=False):
    nc = bass.Bass("TRN2", target_bir_lowering=False)
    kscr = "ExternalOutput" if debug else "Internal"

    def din(name, shape, dt=F32):
        return nc.dram_tensor(name, list(shape), dt, kind="ExternalInput").ap()

    def dscr(name, shape, dt):
        return nc.dram_tensor(name, list(shape), dt, kind=kscr).ap()

    x_d = din("x", [T, D]); ctx_d = din("ctx", [TC, D]); vecs_d = din("vecs", [NV, 128])
    wmod_d = din("w_mod", [D, NIN]); bmod_d = din("b_mod2", [2, NIN]); win_d = din("w_in", [D, NIN])
    lam_d = din("lam4", [1, 256]); subln_d = din("subln", [1, 128])
    wba_d = din("w_ba", [1024, D]); wbr_d = din("w_br", [1024, D]); wout_d = din("w_out", [D, D])
    wup_d = din("w_up", [D, 2 * FF]); wdn_d = din("w_down", [FF, D]); fnw_d = din("fnw", [1, D])
    cos_d = din("cosT", [128, T]); sin_d = din("sinT", [128, T])
    identb_d = din("identb", [128, 128], BF16); identf_d = din("identf", [128, 128]); perm_d = din("perm", [128, 128], BF16)
    tri_d = din("tri", [64, 2 * 8 * 64], BF16); smask_d = din("smask", [128, TA])
    out_d = nc.dram_tensor("out", [T, D], F32, kind="ExternalOutput").ap()

    mod_s = dscr("mod_s", [2, NIN], F32)
    kT_s = dscr("kT_s", [8, 128, TA], BF16); qT_s = dscr("qT_s", [8, 128, T], BF16)
    v_s = dscr("v_s", [TA, 8 * 130], BF16); rf_s = dscr("rf_s", [2, 1024, TA], F32)
    rv_s = dscr("rv_s", [TA, 1024], BF16); rq_s = dscr("rq_s", [1024, T], BF16); rg_s = dscr("rg_s", [1024, T], BF16)
    ga_s = dscr("ga_s", [D, T], BF16); gr_s = dscr("gr_s", [D, T], BF16)
    attT_s = dscr("attT_s", [1024, T], BF16); recT_s = dscr("recT_s", [1024, T], BF16)
    yT_s = dscr("yT_s", [D, T], BF16); x1_s = dscr("x1_s", [T, D], F32); h2T_s = dscr("h2T_s", [D, T], BF16)
    x2_s = dscr("x2_s", [T, D], F32)
    o_s = dscr("o_s", [1024, T], BF16)
    B = {k: Buf() for k in ("mod_s", "kT_s", "qT_s", "v_s", "rf_s", "rv_s", "rq_s", "rg_s", "ga_s", "gr_s",
                            "attT_s", "recT_s", "yT_s", "x1_s", "h2T_s", "x2_s", "out", "o_s")}

    with contextlib.ExitStack() as es:
        S = Sched(nc, es)

        uid = [0]

        def sb(st, name, shape, dt):
            uid[0] += 1
            return st.enter_context(nc.sbuf_tensor("s%d_%s" % (uid[0], name), list(shape), dt)), Buf()

        def ps(st, name, shape, dt=F32):
            uid[0] += 1
            return st.enter_context(nc.psum_tensor("p%d_%s" % (uid[0], name), list(shape), dt)), Buf()

        def load(q, dst, src, bdst, bsrc=None, slow=False):
            if slow:
                S.dma(q, lambda e: e.dma_start(out=dst, in_=src, allow_slow_non_contiguous=True), reads=[bsrc] if bsrc else [], writes=[bdst])
            else:
                S.dma(q, lambda e: e.dma_start(out=dst, in_=src), reads=[bsrc] if bsrc else [], writes=[bdst])

        def store(q, dst, src, bsrc, bdst=None):
            S.dma(q, lambda e: e.dma_start(out=dst, in_=src), reads=[bsrc], writes=[bdst] if bdst else [])

        fm, b_fm = sb(es, "fm", [128, NV], F32)
        modT, b_modT = sb(es, "modT", [128, 96, 2], F32)
        A1, b_A1 = sb(es, "A1", [128, 16, 2], F32)
        A2, b_A2 = sb(es, "A2", [128, 16], F32)
        nlam, b_nlam = sb(es, "nlam", [128, 1], F32)
        low, b_low = sb(es, "low", [128, 16], F32)
        oml, b_oml = sb(es, "oml", [128, 16], F32)
        slw, b_slw = sb(es, "slw", [128, 128], F32)
        identb, b_idb = sb(es, "identb", [128, 128], BF16)
        identf, b_idf = sb(es, "identf", [128, 128], F32)
        mhalf, b_mhalf = sb(es, "mhalf", [128, 1], F32)
        onesb, b_onesb = sb(es, "onesb", [128, 128], BF16)

        with contextlib.ExitStack() as st:
            vrow, b_vrow = sb(st, "vrow", [128, 4, 128], F32)
            scb, b_scb = sb(st, "scb", [128, 16, 2], BF16)
            bmod, b_bmod = sb(st, "bmod", [2, NIN], F32)
            modrow, b_modrow = sb(st, "modrow", [2, NIN], F32)
            wt = [sb(st, "wt%d" % i, [128, 16, 512], BF16) for i in range(2)]
            lamt, b_lamt = sb(st, "lamt", [128, 256], F32)
            lamp, b_lamp = sb(st, "lamp", [128, 2, 64], F32)
            lams, b_lams = sb(st, "lams", [128, 2], F32)
            tmp16, b_tmp16 = sb(st, "tmp16", [128, 16, 2], F32)
            pv, b_pv = ps(st, "pv", [128, 512], F32)
            pmm = [ps(st, "pmm%d" % i, [128, 512], F32) for i in range(2)]
            pmt, b_pmt = ps(st, "pmt", [128, 512], F32)

            load("sp", identb[:], identb_d[:, :], b_idb)
            load("sp", identf[:], identf_d[:, :], b_idf)
            S.op("pool", lambda e: e.memset(mhalf[:], -0.5), writes=[b_mhalf])
            S.op("pool", lambda e: e.memset(onesb[:], 1.0), writes=[b_onesb])
            nrows = [128, 128, 128, NV - 384]
            for i in range(4):
                load("sp", vrow[0:nrows[i], i, :], vecs_d[i * 128:i * 128 + nrows[i], :], b_vrow)
            for i in range(4):
                n = nrows[i]
                S.op("pe", lambda e, i=i, n=n: e.transpose(out=pv[:, 0:n], in_=vrow[0:n, i, :], identity=identf[0:n, 0:n]),
                     reads=[b_vrow, b_idf], writes=[b_pv])
                S.op("dve", lambda e, i=i, n=n: e.tensor_copy(out=fm[:, i * 128:i * 128 + n], in_=pv[:, 0:n]),
                     reads=[b_pv], writes=[b_fm])
            for j, off in enumerate((V_C, V_CC)):
                S.op("act", lambda e, j=j, off=off: e.activation(out=scb[:, :, j], in_=fm[:, off:off + 16], func=AF.Silu),
                     reads=[b_fm], writes=[b_scb])
            load("sp", bmod[:], bmod_d[:, :], b_bmod)
            wv = wmod_d.rearrange("(kc p) n -> p kc n", p=128)
            for g in range(24):
                s = g % 2
                w_t, b_w = wt[s]
                load("pool", w_t[:], wv[:, :, g * 512:(g + 1) * 512], b_w)
                pm, b_pm = pmm[s]
                for kc in range(16):
                    S.op("pe", lambda e, kc=kc, w_t=w_t, pm=pm: e.matmul(pm[0:2, :], lhsT=scb[:, kc, :], rhs=w_t[:, kc, :],
                                                                         start=(kc == 0), stop=(kc == 15)),
                         reads=[b_scb, b_w], writes=[b_pm])
                S.op("dve", lambda e, g=g, pm=pm: e.tensor_tensor(out=modrow[:, g * 512:(g + 1) * 512], in0=pm[0:2, :],
                                                                  in1=bmod[:, g * 512:(g + 1) * 512], op=ALU.add),
                     reads=[b_pm, b_bmod], writes=[b_modrow])
            store("sp", mod_s[:, :], modrow[:], b_modrow, B["mod_s"])
            for j in range(96):
                S.op("pe", lambda e, j=j: e.matmul(pmt[:, 2 * j:2 * j + 2], lhsT=modrow[:, j * 128:(j + 1) * 128],
                                                   rhs=identf[0:2, 0:2], start=True, stop=True),
                     reads=[b_modrow, b_idf], writes=[b_pmt])
            S.op("dve", lambda e: e.tensor_copy(out=modT[:].rearrange("p a b -> p (a b)"), in_=pmt[:, 0:192]),
                 reads=[b_pmt], writes=[b_modT])
            S.op("dve", lambda e: e.tensor_scalar(out=tmp16[:], in0=modT[:, 16:32, :], scalar1=1.0, scalar2=None, op0=ALU.add),
                 reads=[b_modT], writes=[b_tmp16])
            for j in range(2):
                S.op("dve", lambda e, j=j: e.tensor_tensor(out=A1[:, :, j], in0=tmp16[:, :, j], in1=fm[:, V_N1:V_N1 + 16], op=ALU.mult),
                     reads=[b_tmp16, b_fm], writes=[b_A1])
            S.op("dve", lambda e: e.tensor_scalar(out=tmp16[:, :, 0], in0=modT[:, 64:80, 0], scalar1=1.0, scalar2=None, op0=ALU.add),
                 reads=[b_modT, b_A1], writes=[b_tmp16])
            S.op("dve", lambda e: e.tensor_tensor(out=A2[:], in0=tmp16[:, :, 0], in1=fm[:, V_N2:V_N2 + 16], op=ALU.mult),
                 reads=[b_tmp16, b_fm], writes=[b_A2])
            load("sp", lamt[:], lam_d.partition_broadcast(128), b_lamt)
            S.op("dve", lambda e: e.tensor_tensor(out=lamp[:], in0=lamt[:].rearrange("p (a b c) -> p a b c", a=2, b=2)[:, :, 0, :],
                                                  in1=lamt[:].rearrange("p (a b c) -> p a b c", a=2, b=2)[:, :, 1, :], op=ALU.mult),
                 reads=[b_lamt], writes=[b_lamp])
            for j in range(2):
                S.op("dve", lambda e, j=j: e.reduce_sum(out=lams[:, j:j + 1], in_=lamp[:, j, :], axis=mybir.AxisListType.X),
                     reads=[b_lamp], writes=[b_lams])
            S.op("act", lambda e: e.activation(out=lams[:], in_=lams[:], func=AF.Exp), reads=[b_lams], writes=[b_lams])
            S.op("dve", lambda e: e.tensor_tensor(out=nlam[:], in0=lams[:, 1:2], in1=lams[:, 0:1], op=ALU.subtract),
                 reads=[b_lams], writes=[b_nlam])
            S.op("dve", lambda e: e.tensor_scalar(out=nlam[:], in0=nlam[:], scalar1=-LAM_INIT, scalar2=None, op0=ALU.add),
                 reads=[b_nlam], writes=[b_nlam])
            lbv = fm[:, V_LB:V_LB + 32].rearrange("p (d l h) -> p d l h", d=2, l=2)
            S.op("dve", lambda e: e.tensor_tensor(out=low[:].rearrange("p (d h) -> p d h", d=2), in0=lbv[:, :, 0, :], in1=lbv[:, :, 1, :],
                                                  op=ALU.subtract), reads=[b_fm], writes=[b_low])
            S.op("act", lambda e: e.activation(out=low[:], in_=low[:], func=AF.Sigmoid), reads=[b_low], writes=[b_low])
            S.op("dve", lambda e: e.tensor_scalar(out=oml[:], in0=low[:], scalar1=-1.0, scalar2=1.0, op0=ALU.mult, op1=ALU.add),
                 reads=[b_low], writes=[b_oml])
            load("sp", slw[:], subln_d.partition_broadcast(128), b_slw)
            S.op("dve", lambda e: e.tensor_scalar(out=slw[:], in0=slw[:], scalar1=1.0 - LAM_INIT, scalar2=None, op0=ALU.mult),
                 reads=[b_slw], writes=[b_slw])
            S.flush()
        if upto <= 0:
            return nc, S

        def norm_p1(xt, b_xt, junk, b_junk, ssq, b_ssq, xn, b_xn):
            S.op("act", lambda e: e.activation(out=junk[:], in_=xt, func=AF.Square), reads=[b_xt], writes=[b_junk])
            S.op("dve", lambda e: e.reduce_sum(out=ssq[:], in_=junk[:], axis=mybir.AxisListType.X), reads=[b_junk], writes=[b_ssq])
            S.op("pool", lambda e: e.tensor_scalar(out=ssq[:], in0=ssq[:], scalar1=1.0 / D, scalar2=EPS, op0=ALU.mult, op1=ALU.add),
                 reads=[b_ssq], writes=[b_ssq])
            S.op("pool", lambda e: e.tensor_tensor(out=ssq[:], in0=ssq[:], in1=mhalf[:], op=ALU.pow),
                 reads=[b_ssq, b_mhalf], writes=[b_ssq])
            S.op("dve", lambda e: e.tensor_scalar(out=xn[:], in0=xt, scalar1=ssq[:, 0:1], scalar2=None, op0=ALU.mult),
                 reads=[b_xt, b_ssq], writes=[b_xn])

        def norm_p2(xn, b_xn, pts, dst_fn, b_dst, Ascal, Bscal, bA, bB):
            for g in range(4):
                pt, b_pt = pts[g % len(pts)]
                for j in range(4):
                    kc = g * 4 + j
                    S.op("pe", lambda e, kc=kc, j=j, pt=pt: e.transpose(out=pt[:, j * 128:(j + 1) * 128], in_=xn[:, kc * 128:(kc + 1) * 128],
                                                                        identity=identb[:]),
                         reads=[b_xn, b_idb], writes=[b_pt])
                for j in range(4):
                    kc = g * 4 + j
                    if False:
                        S.op("dve", lambda e, kc=kc, j=j, pt=pt: e.tensor_scalar(out=dst_fn(kc), in0=pt[:, j * 128:(j + 1) * 128],
                                                                                 scalar1=Ascal(kc), scalar2=Bscal(kc), op0=ALU.mult, op1=ALU.add),
                             reads=[b_pt, bA, bB], writes=[b_dst])
                    else:
                        S.op("act", lambda e, kc=kc, j=j, pt=pt: e.activation(out=dst_fn(kc), in_=pt[:, j * 128:(j + 1) * 128], func=AF.Identity,
                                                                              scale=Ascal(kc), bias=Bscal(kc)),
                             reads=[b_pt, bA, bB], writes=[b_dst])

        with contextlib.ExitStack() as st:
            hT, b_hT = sb(st, "hT", [128, 16, TA], BF16)
            with contextlib.ExitStack() as st2:
                xts = [sb(st2, "xt%d" % i, [128, D], F32) for i in range(2)]
                junk, b_junk = sb(st2, "junk", [128, D], BF16)
                ssqs = [sb(st2, "ssq%d" % i, [128, 1], F32) for i in range(2)]
                xns = [sb(st2, "xn%d" % i, [128, D], BF16) for i in range(2)]
                pts = [ps(st2, "pt%d" % i, [128, 1024], BF16) for i in range(4)]
                def p1a(tt):
                    s_ = tt % 2
                    xt, b_xt = xts[s_]
                    src = ctx_d[tt * 128:(tt + 1) * 128, :] if tt < 2 else x_d[(tt - 2) * 128:(tt - 1) * 128, :]
                    load("sp", xt[:], src, b_xt)
                    norm_p1(xt[:], b_xt, junk, b_junk, ssqs[s_][0], ssqs[s_][1], xns[s_][0], xns[s_][1])

                def p1b(tt):
                    s_ = tt % 2
                    jc = 1 if tt < 2 else 0
                    norm_p2(xns[s_][0], xns[s_][1], pts, lambda kc, tt=tt: hT[:, kc, tt * 128:(tt + 1) * 128], b_hT,
                            lambda kc, jc=jc: A1[:, kc, jc:jc + 1], lambda kc, jc=jc: modT[:, kc, jc:jc + 1], b_A1, b_modT)

                p1a(0)
                for tt in range(18):
                    if tt + 1 < 18:
                        p1a(tt + 1)
                    p1b(tt)
                S.flush()
            if upto <= 1:
                return nc, S
            wt = [sb(st, "wt%d" % i, [128, 16, 512], BF16) for i in range(2)]
            stf = [sb(st, "stf%d" % i, [128, TA], F32) for i in range(2)]
            stb = [sb(st, "stb%d" % i, [128, TA], BF16) for i in range(2)]
            cosT, b_cos = sb(st, "cosT", [128, T], F32)
            sinT, b_sin = sb(st, "sinT", [128, T], F32)
            perm, b_perm = sb(st, "perm", [128, 128], BF16)
            zbs = [sb(st, "zb%d" % i, [128, 512], BF16) for i in range(2)]
            t1s = [sb(st, "t1_%d" % i, [128, 512], F32) for i in range(2)]
            t2s = [sb(st, "t2_%d" % i, [128, 512], F32) for i in range(2)]
            vst = [sb(st, "vst%d" % i, [128, 4, 130], BF16) for i in range(2)]
            rst = [sb(st, "rst%d" % i, [128, 512], BF16) for i in range(2)]
            pms = [ps(st, "pm%d" % i, [128, 512], F32) for i in range(4)]
            pws = [ps(st, "pw%d" % i, [128, 512], F32) for i in range(2)]
            load("sp", cosT[:], cos_d[:, :], b_cos)
            load("sp", sinT[:], sin_d[:, :], b_sin)
            load("sp", perm[:], perm_d[:, :], b_perm)
            for i in range(2):
                S.op("pool", lambda e, i=i: e.memset(vst[i][0][:], 1.0), writes=[vst[i][1]])
            fams = ["ak"] * 2 + ["av"] * 2 + ["rf0"] * 2 + ["rf1"] * 2 + ["ri"] * 2 + ["aq"] * 2 + ["rq"] * 2 + ["rg"] * 2 + ["ga"] * 4 + ["gr"] * 4
            fstart = {}
            for g, f in enumerate(fams):
                fstart.setdefault(f, g)
            wv = win_d.rearrange("(kc p) n -> p kc n", p=128)
            ipm = 0
            irope = 0
            ist = 0
            ivs = 0
            import os
            for g in range(int(os.environ.get("K1B", "24"))):
                fam = fams[g]
                w_t, b_w = wt[g % 2]
                load("pool", w_t[:], wv[:, :, g * 512:(g + 1) * 512], b_w)
                has_ctx = g < 10
                if fam in ("av", "ri"):
                    for tt in range(18):
                        pm, b_pm = pms[ipm % 4]; ipm += 1
                        for kc in range(16):
                            S.op("pe", lambda e, kc=kc, tt=tt, pm=pm, w_t=w_t: e.matmul(pm[:], lhsT=hT[:, kc, tt * 128:(tt + 1) * 128], rhs=w_t[:, kc, :],
                                                                                         start=(kc == 0), stop=(kc == 15)),
                                 reads=[b_hT, b_w], writes=[b_pm])
                        if fam == "av":
                            v_t, b_v = vst[ivs % 2]; ivs += 1
                            S.op("dve", lambda e, pm=pm, v_t=v_t: e.tensor_copy(out=v_t[:, :, 0:128], in_=pm[:].rearrange("p (h e) -> p h e", h=4)),
                                 reads=[b_pm], writes=[b_v])
                            hh = (g - 2) * 4
                            store("sp", v_s[tt * 128:(tt + 1) * 128, hh * 130:(hh + 4) * 130], v_t[:].rearrange("p h e -> p (h e)"), b_v, B["v_s"])
                        else:
                            r_t, b_r = rst[ivs % 2]; ivs += 1
                            S.op("dve", lambda e, pm=pm, r_t=r_t: e.tensor_copy(out=r_t[:], in_=pm[:]), reads=[b_pm], writes=[b_r])
                            store("sp", rv_s[tt * 128:(tt + 1) * 128, (g - 8) * 512:(g - 7) * 512], r_t[:], b_r, B["rv_s"])
                    continue
                blocks = ([(0, 256)] if has_ctx else []) + [(256 + 512 * i, 512) for i in range(4)]
                for j in range(4):
                    fi = (g - fstart[fam]) * 4 + j
                    isf32 = fam in ("rf0", "rf1")
                    stg, b_stg = (stf if isf32 else stb)[ist % 2]; ist += 1
                    for (t0, n) in blocks:
                        c0 = t0 if has_ctx else t0 - 256
                        pm, b_pm = pms[ipm % 4]; ipm += 1
                        for kc in range(16):
                            S.op("pe", lambda e, kc=kc, j=j, t0=t0, n=n, pm=pm, w_t=w_t: e.matmul(pm[:, 0:n], lhsT=w_t[:, kc, j * 128:(j + 1) * 128],
                                                                                                 rhs=hT[:, kc, t0:t0 + n], start=(kc == 0), stop=(kc == 15)),
                                 reads=[b_hT, b_w], writes=[b_pm])
                        dst = stg[:, c0:c0 + n]
                        if fam in ("aq", "ak") and t0 >= 256 and os.environ.get("K1R", "1") == "1":
                            zb, b_zb = zbs[irope % 2]; t1, b_t1 = t1s[irope % 2]; t2, b_t2 = t2s[irope % 2]
                            pw, b_pw = pws[irope % 2]; irope += 1
                            s0 = t0 - 256
                            S.op("act", lambda e, pm=pm, zb=zb: e.activation(out=zb[:], in_=pm[:], func=AF.Copy), reads=[b_pm], writes=[b_zb])
                            S.op("pe", lambda e, pw=pw, zb=zb: e.matmul(pw[:], lhsT=perm[:], rhs=zb[:], start=True, stop=True),
                                 reads=[b_perm, b_zb], writes=[b_pw])
                            S.op("dve", lambda e, pm=pm, t1=t1, s0=s0: e.tensor_tensor(out=t1[:], in0=pm[:], in1=cosT[:, s0:s0 + 512], op=ALU.mult),
                                 reads=[b_pm, b_cos, b_zb, b_pw], writes=[b_t1])
                            S.op("dve", lambda e, pw=pw, t2=t2, s0=s0: e.tensor_tensor(out=t2[:], in0=pw[:], in1=sinT[:, s0:s0 + 512], op=ALU.mult),
                                 reads=[b_pw, b_sin], writes=[b_t2])
                            S.op("dve", lambda e, t1=t1, t2=t2, dst=dst: e.tensor_tensor(out=dst, in0=t1[:], in1=t2[:], op=ALU.add),
                                 reads=[b_t1, b_t2], writes=[b_stg])
                        else:
                            func = {"ak": AF.Copy, "aq": AF.Copy, "rf0": AF.Copy, "rf1": AF.Copy, "rq": AF.Silu, "rg": AF.Silu, "ga": AF.Sigmoid, "gr": AF.Sigmoid}[fam]
                            S.op("act", lambda e, pm=pm, n=n, dst=dst, func=func: e.activation(out=dst, in_=pm[:, 0:n], func=func),
                                 reads=[b_pm], writes=[b_stg])
                    ncol = TA if has_ctx else T
                    if fam == "ak":
                        dd, bd = kT_s[fi], B["kT_s"]
                    elif fam == "aq":
                        dd, bd = qT_s[fi], B["qT_s"]
                    elif isf32:
                        dd, bd = rf_s[int(fam[2]), fi * 128:(fi + 1) * 128, :], B["rf_s"]
                    else:
                        scr = {"rq": rq_s, "rg": rg_s, "ga": ga_s, "gr": gr_s}[fam]
                        dd, bd = scr[fi * 128:(fi + 1) * 128, :], B[fam + "_s"]
                    store("sp", dd, stg[:, 0:ncol], b_stg, bd)
            S.flush()
        if upto <= 2:
            return nc, S

        with contextlib.ExitStack() as st:
            vaug, b_vaug = sb(st, "vaug", [128, 18, 8 * 130], BF16)
            kTs = [sb(st, "kT%d" % i, [128, TA], BF16) for i in range(2)]
            qTs = [sb(st, "qT%d" % i, [128, T], BF16) for i in range(2)]
            eTs = [sb(st, "eT%d" % i, [128, 1024], BF16) for i in range(3)]
            attst = [sb(st, "attst%d" % i, [128, T], BF16) for i in range(2)]
            o_t = [sb(st, "o_t%d" % i, [128, 128], F32) for i in range(2)]
            t_t = [sb(st, "t_t%d" % i, [128, 128], F32) for i in range(2)]
            a_t = [sb(st, "a_t%d" % i, [128, 128], BF16) for i in range(2)]
            jk, b_jk = sb(st, "jk", [128, 128], F32)
            sms = [sb(st, "sm%d" % i, [128, 4], F32) for i in range(2)]
            scs = [ps(st, "sc%d" % i, [128, 1024], F32) for i in range(2)]
            accs = [ps(st, "acc%d" % i, [128, 512], F32) for i in range(3)]
            b_acc = [Buf() for _ in range(8)]
            pT, b_pT = ps(st, "pT", [128, 1024], BF16)

            def accap(idx, lo, hi):
                return accs[idx // 3][0][:, (idx % 3) * 130 + lo:(idx % 3) * 130 + hi]

            load("sp", vaug[:], v_s.rearrange("(t p) f -> p t f", p=128), b_vaug, B["v_s"])

            def head_loads(h):
                load("sp", kTs[h % 2][0][:], kT_s[h], kTs[h % 2][1], B["kT_s"])
                load("sp", qTs[h % 2][0][:], qT_s[h], qTs[h % 2][1], B["qT_s"])

            steps = [(h, qb, kt) for h in range(8) for qb in range(4) for kt in range(18)]

            def emit_scores(i):
                h, qb, kt = steps[i]
                kT, b_kT = kTs[h % 2]; qT, b_qT = qTs[h % 2]
                sc, b_sc = scs[i % 2]; eT, b_eT = eTs[i % 3]
                for c in range(2):
                    S.op("pe", lambda e, c=c, kt=kt, qb=qb, sc=sc, kT=kT, qT=qT: e.matmul(
                        sc[:, c * 512:(c + 1) * 512], lhsT=kT[c * 64:(c + 1) * 64, kt * 128:(kt + 1) * 128],
                        rhs=qT[c * 64:(c + 1) * 64, qb * 512:(qb + 1) * 512], start=True, stop=True),
                        reads=[b_kT, b_qT], writes=[b_sc])
                S.op("act", lambda e, sc=sc, eT=eT: e.activation(out=eT[:], in_=sc[:], func=AF.Exp, scale=0.125),
                     reads=[b_sc], writes=[b_eT])

            def emit_pv(i):
                h, qb, kt = steps[i]
                eT, b_eT = eTs[i % 3]
                for c in range(2):
                    for qt in range(4):
                        idx = c * 4 + qt
                        first = (kt == 0 and idx % 3 == 0)
                        S.op("pe", lambda e, idx=idx, c=c, qt=qt, kt=kt, h=h, eT=eT, first=first: e.matmul(
                            accap(idx, 0, 129), lhsT=eT[:, c * 512 + qt * 128:c * 512 + (qt + 1) * 128],
                            rhs=vaug[:, kt, h * 130:h * 130 + 129], start=first, stop=(kt == 17)),
                            reads=[b_eT, b_vaug], writes=[b_acc[idx]] + ([b_acc[j] for j in range(idx, min(idx + 3, 8))] if first else []))

            def emit_norm(h, qb):
                ast, b_ast = attst[h % 2]
                for qt in range(4):
                    sm, b_sm = sms[qt % 2]; o, b_o = o_t[qt % 2]; tq, b_tq = t_t[qt % 2]; at, b_at = a_t[qt % 2]
                    S.op("dve", lambda e, qt=qt, sm=sm: e.reciprocal(out=sm[:, 0:1], in_=accap(qt, 128, 129)),
                         reads=[b_acc[qt]], writes=[b_sm])
                    S.op("dve", lambda e, qt=qt, sm=sm: e.reciprocal(out=sm[:, 1:2], in_=accap(4 + qt, 128, 129)),
                         reads=[b_acc[4 + qt], b_sm], writes=[b_sm])
                    S.op("dve", lambda e, sm=sm: e.tensor_tensor(out=sm[:, 1:2], in0=sm[:, 1:2], in1=nlam[:], op=ALU.mult),
                         reads=[b_sm, b_nlam], writes=[b_sm])
                    S.op("dve", lambda e, qt=qt, sm=sm, tq=tq: e.tensor_scalar(out=tq[:], in0=accap(4 + qt, 0, 128), scalar1=sm[:, 1:2],
                                                                                scalar2=None, op0=ALU.mult),
                         reads=[b_acc[4 + qt], b_sm], writes=[b_tq])
                    S.op("dve", lambda e, qt=qt, sm=sm, tq=tq, o=o: e.scalar_tensor_tensor(out=o[:], in0=accap(qt, 0, 128), scalar=sm[:, 0:1],
                                                                                            in1=tq[:], op0=ALU.mult, op1=ALU.add),
                         reads=[b_acc[qt], b_sm, b_tq], writes=[b_o])
                    S.op("pool", lambda e, o=o: e.tensor_tensor(out=jk[:], in0=o[:], in1=o[:], op=ALU.mult), reads=[b_o], writes=[b_jk])
                    S.op("dve", lambda e, sm=sm: e.reduce_sum(out=sm[:, 2:3], in_=jk[:], axis=mybir.AxisListType.X), reads=[b_jk, b_sm], writes=[b_sm])
                    S.op("pool", lambda e, sm=sm: e.tensor_scalar(out=sm[:, 2:3], in0=sm[:, 2:3], scalar1=1.0 / 128, scalar2=EPS,
                                                                   op0=ALU.mult, op1=ALU.add), reads=[b_sm], writes=[b_sm])
                    S.op("pool", lambda e, sm=sm: e.tensor_tensor(out=sm[:, 3:4], in0=sm[:, 2:3], in1=mhalf[:], op=ALU.pow),
                         reads=[b_sm, b_mhalf], writes=[b_sm])
                    S.op("dve", lambda e, o=o, sm=sm, at=at: e.scalar_tensor_tensor(out=at[:], in0=o[:], scalar=sm[:, 3:4], in1=slw[:],
                                                                                     op0=ALU.mult, op1=ALU.mult),
                         reads=[b_o, b_sm, b_slw], writes=[b_at])
                    S.op("pe", lambda e, qt=qt, at=at: e.transpose(out=pT[:, qt * 128:(qt + 1) * 128], in_=at[:], identity=identb[:]),
                         reads=[b_at, b_idb], writes=[b_pT])
                S.op("act", lambda e, qb=qb, ast=ast: e.activation(out=ast[:, qb * 512:(qb + 1) * 512], in_=pT[:, 0:512], func=AF.Copy),
                     reads=[b_pT], writes=[b_ast])
                if qb == 3:
                    store("sp", attT_s[h * 128:(h + 1) * 128, :], ast[:], b_ast, B["attT_s"])

            head_loads(0)
            head_loads(1)
            emit_scores(0)
            for i, (h, qb, kt) in enumerate(steps):
                if i + 1 < len(steps):
                    emit_scores(i + 1)
                emit_pv(i)
                if kt == 17:
                    emit_norm(h, qb)
                    if qb == 3 and h + 2 < 8:
                        head_loads(h + 2)
            S.flush()
        if upto <= 3:
            return nc, S

        with contextlib.ExitStack() as st:
            smask, b_smask = sb(st, "smask", [128, TA], F32)
            tri, b_tri = sb(st, "tri", [64, 2 * 8 * 64], BF16)
            frs = [sb(st, "fr%d" % i, [128, TA], F32) for i in range(2)]
            T1s = [sb(st, "T1_%d" % i, [128, TA], F32) for i in range(2)]
            T2s = [sb(st, "T2_%d" % i, [128, TA], F32) for i in range(2)]
            T3s = [sb(st, "T3_%d" % i, [128, TA], F32) for i in range(2)]
            rq, b_rq = sb(st, "rq", [128, T], BF16)
            v64, b_v64 = sb(st, "v64", [64, 36, 128], BF16)
            QT = [sb(st, "QT%d" % i, [128, T], BF16) for i in range(2)]
            Q2 = [sb(st, "Q2%d" % i, [128, T], BF16) for i in range(2)]
            KT = [sb(st, "KT%d" % i, [128, TA], BF16) for i in range(2)]
            K2 = [sb(st, "K2%d" % i, [128, TA], BF16) for i in range(2)]
            K2tok = [sb(st, "K2tok%d" % i, [64, 36, 128], BF16) for i in range(2)]
            msc = [sb(st, "msc%d" % i, [64, 32 * 64], BF16) for i in range(2)]
            o_d = [sb(st, "o_d%d" % i, [128, T], F32) for i in range(2)]
            osts = [sb(st, "ost%d" % i, [128, T], BF16) for i in range(2)]
            state = [sb(st, "state%d" % i, [128, 128], F32) for i in range(2)]
            statebf = [sb(st, "statebf%d" % i, [128, 128], BF16) for i in range(2)]
            statebf2 = [sb(st, "statebf2_%d" % i, [128, 128], BF16) for i in range(2)]
            dec = [sb(st, "dec%d" % i, [128, 36], F32) for i in range(2)]
            pks = [ps(st, "pk%d" % i, [128, 1024], BF16) for i in range(2)]
            pscs = [ps(st, "psc%d" % i, [128, 512], F32) for i in range(2)]
            pout = [[ps(st, "pout%d%d" % (d, i), [128, 512], F32) for i in range(1)] for d in range(2)]
            pupd = [ps(st, "pupd%d" % d, [128, 512], F32) for d in range(2)]
            load("sp", smask[:], smask_d[:, :], b_smask)
            load("sp", tri[:], tri_d[:, :], b_tri)
            v3 = lambda t: t[:].rearrange("p (c t) -> p c t", t=64)
            rvv = rv_s.rearrange("(c p) f -> p c f", p=64)

            def prep_ops(h, d):
                L = []
                Ft, b_F = frs[d]
                T1, b_T1 = T1s[d]; T2, b_T2 = T2s[d]; T3, b_T3 = T3s[d]
                pk, b_pk = pks[d]; psc, b_psc = pscs[d]
                col = d * 8 + h
                L.append(lambda: S.op("act", lambda e: e.activation(out=Ft[:], in_=Ft[:], func=AF.Sigmoid), reads=[b_F], writes=[b_F]))
                L.append(lambda: S.op("dve", lambda e: e.tensor_scalar(out=Ft[:], in0=Ft[:], scalar1=oml[:, col:col + 1], scalar2=low[:, col:col + 1],
                                                                        op0=ALU.mult, op1=ALU.add), reads=[b_F, b_oml, b_low], writes=[b_F]))
                L.append(lambda: S.op("act", lambda e: e.activation(out=T1[:], in_=Ft[:], func=AF.Ln), reads=[b_F], writes=[b_T1]))
                L.append(lambda: S.op("dve", lambda e: e.tensor_scalar(out=Ft[:], in0=Ft[:], scalar1=-1.0, scalar2=1.0, op0=ALU.mult, op1=ALU.add),
                                      reads=[b_F, b_T1], writes=[b_F]))
                L.append(lambda: S.op("dve", lambda e: e.tensor_tensor_scan(out=T2[:], data0=smask[:], data1=T1[:], initial=0.0, op0=ALU.mult, op1=ALU.add),
                                      reads=[b_smask, b_T1], writes=[b_T2]))
                if d == 0:
                    Tb, b_Tb, Tf, b_Tf = T2, b_T2, T1, b_T1
                    refi, endi = 31, 63
                else:
                    L.append(lambda: S.op("pool", lambda e: e.tensor_tensor(out=T1[:], in0=T1[:], in1=T2[:], op=ALU.subtract), reads=[b_T1, b_T2], writes=[b_T1]))
                    L.append(lambda: S.op("pool", lambda e: e.tensor_tensor(out=v3(T1), in0=v3(T1), in1=v3(T2)[:, :, 63:64].to_broadcast([128, 36, 64]), op=ALU.add),
                                          reads=[b_T1, b_T2], writes=[b_T1]))
                    Tb, b_Tb, Tf, b_Tf = T1, b_T1, T2, b_T2
                    refi, endi = 32, 0
                L.append(lambda: S.op("pool", lambda e: e.tensor_tensor(out=v3(Tf), in0=v3(Tb), in1=v3(Tb)[:, :, refi:refi + 1].to_broadcast([128, 36, 64]),
                                                                         op=ALU.subtract), reads=[b_Tb, b_Tf], writes=[b_Tf]))
                L.append(lambda: S.op("act", lambda e: e.activation(out=T3[:], in_=Tf[:], func=AF.Exp), reads=[b_Tf], writes=[b_T3]))
                L.append(lambda: S.op("dve", lambda e: e.tensor_tensor(out=QT[d][0][:], in0=rq[:], in1=T3[:, TC:TA], op=ALU.mult), reads=[b_rq, b_T3], writes=[QT[d][1]]))
                L.append(lambda: S.op("act", lambda e: e.activation(out=T3[:], in_=Tf[:], func=AF.Exp, scale=-1.0), reads=[b_Tf, QT[d][1]], writes=[b_T3]))
                L.append(lambda: S.op("dve", lambda e: e.tensor_tensor(out=KT[d][0][:], in0=Ft[:], in1=T3[:], op=ALU.mult), reads=[b_F, b_T3], writes=[KT[d][1]]))
                L.append(lambda: S.op("act", lambda e: e.activation(out=dec[d][0][:], in_=v3(Tb)[:, :, endi], func=AF.Exp), reads=[b_Tb], writes=[dec[d][1]]))
                L.append(lambda: S.op("pool", lambda e: e.tensor_tensor(out=v3(Tf), in0=v3(Tb), in1=v3(Tb)[:, :, endi:endi + 1].to_broadcast([128, 36, 64]),
                                                                         op=ALU.subtract), reads=[b_Tb, b_Tf, b_T3], writes=[b_Tf]))
                L.append(lambda: S.op("act", lambda e: e.activation(out=T3[:], in_=Tf[:], func=AF.Exp, scale=-1.0), reads=[b_Tf, KT[d][1]], writes=[b_T3]))
                L.append(lambda: S.op("dve", lambda e: e.tensor_tensor(out=K2[d][0][:], in0=Ft[:], in1=T3[:], op=ALU.mult), reads=[b_F, b_T3], writes=[K2[d][1]]))
                L.append(lambda: S.op("act", lambda e: e.activation(out=T3[:], in_=Tb[:], func=AF.Exp), reads=[b_Tb, K2[d][1]], writes=[b_T3]))
                L.append(lambda: S.op("dve", lambda e: e.tensor_tensor(out=Q2[d][0][:], in0=rq[:], in1=T3[:, TC:TA], op=ALU.mult), reads=[b_rq, b_T3], writes=[Q2[d][1]]))

                def transposes():
                    for c0 in range(0, 36, 8):
                        n = min(8, 36 - c0)
                        for cc in range(n):
                            c = c0 + cc
                            S.op("pe", lambda e, c=c, cc=cc: e.transpose(out=pk[0:64, cc * 128:(cc + 1) * 128], in_=K2[d][0][:, c * 64:(c + 1) * 64], identity=identb[:]),
                                 reads=[K2[d][1], b_idb], writes=[b_pk])
                        S.op("act", lambda e, c0=c0, n=n: e.activation(out=K2tok[d][0][:, c0:c0 + n, :].rearrange("p c k -> p (c k)"), in_=pk[0:64, 0:n * 128], func=AF.Copy),
                             reads=[b_pk], writes=[K2tok[d][1]])
                L.append(transposes)

                def scores():
                    for l0 in range(0, 32, 8):
                        for cc in range(8):
                            lc = l0 + cc
                            c = 4 + lc
                            S.op("pe", lambda e, c=c, lc=lc, cc=cc: e.matmul(psc[0:64, cc * 64:(cc + 1) * 64], lhsT=KT[d][0][:, c * 64:(c + 1) * 64],
                                                                              rhs=QT[d][0][:, lc * 64:(lc + 1) * 64], start=True, stop=True),
                                 reads=[KT[d][1], QT[d][1]], writes=[b_psc])
                        S.op("dve", lambda e, l0=l0: e.tensor_tensor(out=msc[d][0][:, l0 * 64:(l0 + 8) * 64], in0=psc[0:64, :], in1=tri[:, d * 512:(d + 1) * 512], op=ALU.mult),
                             reads=[b_psc, b_tri], writes=[msc[d][1]])
                L.append(scores)
                return L

            for h in range(8):
                for d in range(2):
                    load("sp", frs[d][0][:], rf_s[d, h * 128:(h + 1) * 128, :], frs[d][1], B["rf_s"])
                load("sp", rq[:], rq_s[h * 128:(h + 1) * 128, :], b_rq, B["rq_s"])
                load("sp", v64[:], rvv[:, :, h * 128:(h + 1) * 128], b_v64, B["rv_s"])
                Ls = [prep_ops(h, 0), prep_ops(h, 1)]
                for k in range(max(len(Ls[0]), len(Ls[1]))):
                    for d in range(2):
                        if k < len(Ls[d]):
                            Ls[d][k]()
                for d in range(2):
                    S.op("pool", lambda e, d=d: e.memset(state[d][0][:], 0.0), writes=[state[d][1]])
                    S.op("pool", lambda e, d=d: e.memset(statebf[d][0][:], 0.0), writes=[statebf[d][1]])
                order = [list(range(36)), [3, 2, 1, 0] + list(range(35, 3, -1))]
                for step in range(36):
                    for d in range(2):
                        c = order[d][step]
                        sb_prev, b_sbp = (statebf, statebf2)[(step + 1) % 2][d]
                        sb_next, b_sbn = (statebf, statebf2)[step % 2][d]
                        pu, b_pu = pupd[d]
                        if step < 35:
                            S.op("pe", lambda e, d=d, c=c, pu=pu: e.matmul(pu[:, 0:128], lhsT=K2tok[d][0][:, c, :], rhs=v64[:, c, :], start=True, stop=True),
                                 reads=[K2tok[d][1], b_v64], writes=[b_pu])
                        if c >= 4:
                            lc = c - 4
                            grp = lc // 8
                            po, b_po = pout[d][0]
                            slot = lc % 8
                            S.op("pe", lambda e, d=d, c=c, lc=lc, slot=slot, po=po: e.matmul(po[:, slot * 64:(slot + 1) * 64], lhsT=v64[:, c, :],
                                                                                           rhs=msc[d][0][:, lc * 64:(lc + 1) * 64], start=True, stop=False),
                                 reads=[b_v64, msc[d][1]], writes=[b_po])
                            S.op("pe", lambda e, d=d, lc=lc, slot=slot, po=po, sb_prev=sb_prev: e.matmul(po[:, slot * 64:(slot + 1) * 64], lhsT=sb_prev[:],
                                                                                                       rhs=Q2[d][0][:, lc * 64:(lc + 1) * 64], start=False, stop=True),
                                 reads=[b_sbp, Q2[d][1]], writes=[b_po])
                            last_in_grp = (slot == 7) if d == 0 else (slot == 0)
                            if last_in_grp:
                                S.op("act", lambda e, d=d, grp=grp, po=po: e.activation(out=o_d[d][0][:, grp * 512:(grp + 1) * 512], in_=po[:], func=AF.Copy),
                                     reads=[b_po], writes=[o_d[d][1]])
                        if step == 35:
                            continue
                        S.op("dve", lambda e, d=d, c=c, pu=pu: e.scalar_tensor_tensor(out=state[d][0][:], in0=state[d][0][:], scalar=dec[d][0][:, c:c + 1], in1=pu[:, 0:128],
                                                                                       op0=ALU.mult, op1=ALU.add),
                             reads=[state[d][1], dec[d][1], b_pu], writes=[state[d][1]])
                        S.op("act", lambda e, d=d, sb_next=sb_next: e.activation(out=sb_next[:], in_=state[d][0][:], func=AF.Copy),
                             reads=[state[d][1]], writes=[b_sbn])
                ost, b_ost = osts[h % 2]
                S.op("dve", lambda e, ost=ost: e.tensor_tensor(out=ost[:], in0=o_d[0][0][:], in1=o_d[1][0][:], op=ALU.add),
                     reads=[o_d[0][1], o_d[1][1]], writes=[b_ost])
                store("sp", o_s[h * 128:(h + 1) * 128, :], ost[:], b_ost, B["o_s"])
            pss = [pout[0][0], pout[1][0], pupd[0], pupd[1]]
            sq, b_sq = frs[0]
            rstd, b_rstd = T1s[0]
            mh2, b_mh2 = T2s[0]
            rec32, b_rec32 = T3s[0]
            S.op("pool", lambda e: e.memset(mh2[:, 0:T], -0.5), reads=[], writes=[b_mh2])
            sqb = sq[:].bitcast(BF16)
            for h in range(8):
                oh, b_oh = QT[h % 2]
                load("sp", oh[:], o_s[h * 128:(h + 1) * 128, :], b_oh, B["o_s"])
                S.op("dve", lambda e, oh=oh: e.tensor_tensor(out=sqb[:, 0:T], in0=oh[:], in1=oh[:], op=ALU.mult),
                     reads=[b_oh], writes=[b_sq])
                for tb in range(4):
                    S.op("pe", lambda e, h=h, tb=tb: e.matmul(pss[tb][0][:], lhsT=onesb[:], rhs=sqb[:, tb * 512:(tb + 1) * 512], start=(h == 0), stop=(h == 7)),
                         reads=[b_sq, b_onesb], writes=[pss[tb][1]])
            for tb in range(4):
                S.op("dve", lambda e, tb=tb: e.tensor_scalar(out=rstd[:, tb * 512:(tb + 1) * 512], in0=pss[tb][0][:], scalar1=1.0 / 1024, scalar2=EPS, op0=ALU.mult, op1=ALU.add),
                     reads=[pss[tb][1]], writes=[b_rstd])
            S.op("pool", lambda e: e.tensor_tensor(out=rstd[:, 0:T], in0=rstd[:, 0:T], in1=mh2[:, 0:T], op=ALU.pow), reads=[b_rstd, b_mh2], writes=[b_rstd])
            for h in range(8):
                oh, b_oh = QT[h % 2]; rgt, b_rgt = Q2[h % 2]; rst_, b_rst_ = osts[h % 2]
                load("sp", oh[:], o_s[h * 128:(h + 1) * 128, :], b_oh, B["o_s"])
                load("sp", rgt[:], rg_s[h * 128:(h + 1) * 128, :], b_rgt, B["rg_s"])
                S.op("dve", lambda e, h=h, oh=oh: e.scalar_tensor_tensor(out=rec32[:, 0:T], in0=oh[:], scalar=fm[:, V_GN + h:V_GN + h + 1], in1=rstd[:, 0:T],
                                                                          op0=ALU.mult, op1=ALU.mult), reads=[b_oh, b_fm, b_rstd], writes=[b_rec32])
                S.op("dve", lambda e, rst_=rst_, rgt=rgt: e.tensor_tensor(out=rst_[:], in0=rec32[:, 0:T], in1=rgt[:], op=ALU.mult), reads=[b_rec32, b_rgt], writes=[b_rst_])
                store("sp", recT_s[h * 128:(h + 1) * 128, :], rst_[:], b_rst_, B["recT_s"])
            S.flush()
        if upto <= 4:
            return nc, S

        with contextlib.ExitStack() as st:
            wba, b_wba = sb(st, "wba", [128, 8, D], BF16)
            wbr, b_wbr = sb(st, "wbr", [128, 8, D], BF16)
            load("pool", wba[:], wba_d.rearrange("(kc p) n -> p kc n", p=128), b_wba)
            load("pool", wbr[:], wbr_d.rearrange("(kc p) n -> p kc n", p=128), b_wbr)
            attb = [sb(st, "attb%d" % i, [128, 8, 256], BF16) for i in range(2)]
            recb = [sb(st, "recb%d" % i, [128, 8, 256], BF16) for i in range(2)]
            gab = [sb(st, "gab%d" % i, [128, 16, 256], BF16) for i in range(2)]
            grb = [sb(st, "grb%d" % i, [128, 16, 256], BF16) for i in range(2)]
            yTb = [sb(st, "yTb%d" % i, [128, 16, 256], BF16) for i in range(2)]
            tas = [sb(st, "ta%d" % i, [128, 256], F32) for i in range(2)]
            trs = [sb(st, "tr%d" % i, [128, 256], F32) for i in range(2)]
            pas = [ps(st, "pa%d" % i, [128, 512], F32) for i in range(2)]
            prs = [ps(st, "pr%d" % i, [128, 512], F32) for i in range(2)]
            attv = attT_s.rearrange("(kc p) t -> p kc t", p=128)
            recv = recT_s.rearrange("(kc p) t -> p kc t", p=128)
            gav = ga_s.rearrange("(kc p) t -> p kc t", p=128)
            grv = gr_s.rearrange("(kc p) t -> p kc t", p=128)
            yTv = yT_s.rearrange("(kc p) t -> p kc t", p=128)
            def p4_loads(tb):
                s_ = tb % 2
                t0 = tb * 256
                load("sp", attb[s_][0][:], attv[:, :, t0:t0 + 256], attb[s_][1], B["attT_s"])
                load("sp", recb[s_][0][:], recv[:, :, t0:t0 + 256], recb[s_][1], B["recT_s"])
                load("sp", gab[s_][0][:], gav[:, :, t0:t0 + 256], gab[s_][1], B["ga_s"])
                load("sp", grb[s_][0][:], grv[:, :, t0:t0 + 256], grb[s_][1], B["gr_s"])

            p4_loads(0)
            for tb in range(8):
                s_ = tb % 2
                t0 = tb * 256
                if tb + 1 < 8:
                    p4_loads(tb + 1)
                for dc in range(16):
                    pa, b_pa = pas[dc % 2]; pr, b_pr = prs[dc % 2]
                    ta, b_ta = tas[dc % 2]; tr, b_tr = trs[dc % 2]
                    for kc in range(8):
                        S.op("pe", lambda e, kc=kc, dc=dc, pa=pa, s_=s_: e.matmul(pa[:, 0:256], lhsT=wba[:, kc, dc * 128:(dc + 1) * 128], rhs=attb[s_][0][:, kc, :],
                                                                                start=(kc == 0), stop=(kc == 7)), reads=[b_wba, attb[s_][1]], writes=[b_pa])
                    for kc in range(8):
                        S.op("pe", lambda e, kc=kc, dc=dc, pr=pr, s_=s_: e.matmul(pr[:, 0:256], lhsT=wbr[:, kc, dc * 128:(dc + 1) * 128], rhs=recb[s_][0][:, kc, :],
                                                                                start=(kc == 0), stop=(kc == 7)), reads=[b_wbr, recb[s_][1]], writes=[b_pr])
                    S.op("dve", lambda e, dc=dc, pa=pa, ta=ta, s_=s_: e.tensor_tensor(out=ta[:], in0=pa[:, 0:256], in1=gab[s_][0][:, dc, :], op=ALU.mult),
                         reads=[b_pa, gab[s_][1]], writes=[b_ta])
                    S.op("dve", lambda e, dc=dc, pr=pr, tr=tr, s_=s_: e.tensor_tensor(out=tr[:], in0=pr[:, 0:256], in1=grb[s_][0][:, dc, :], op=ALU.mult),
                         reads=[b_pr, grb[s_][1]], writes=[b_tr])
                    S.op("dve", lambda e, dc=dc, ta=ta, tr=tr, s_=s_: e.tensor_tensor(out=yTb[s_][0][:, dc, :], in0=ta[:], in1=tr[:], op=ALU.add),
                         reads=[b_ta, b_tr], writes=[yTb[s_][1]])
                store("sp", yTv[:, :, t0:t0 + 256], yTb[s_][0][:], yTb[s_][1], B["yT_s"])
            S.flush()
        if upto <= 5:
            return nc, S

        with contextlib.ExitStack() as st:
            wout, b_wout = sb(st, "wout", [128, 16, D], BF16)
            wov = wout_d.rearrange("(kc p) n -> p kc n", p=128)
            load("pool", wout[:, 0:8, :], wov[:, 0:8, :], b_wout)
            load("pool", wout[:, 8:16, :], wov[:, 8:16, :], b_wout)
            g1bc, b_g1 = sb(st, "g1bc", [128, D], F32)
            load("sp", g1bc[:], mod_s[0:1, 2 * D:3 * D].partition_broadcast(128), b_g1, B["mod_s"])
            ybs = [sb(st, "yb%d" % i, [128, 16, 512], BF16) for i in range(2)]
            xts = [sb(st, "xt%d" % i, [128, D], F32) for i in range(2)]
            x1ts = [sb(st, "x1t%d" % i, [128, D], F32) for i in range(2)]
            tts = [sb(st, "tt%d" % i, [128, 512], F32) for i in range(2)]
            junk, b_junk = sb(st, "junk", [128, D], BF16)
            ssqs = [sb(st, "ssq%d" % i, [128, 1], F32) for i in range(2)]
            xns = [sb(st, "xn%d" % i, [128, D], BF16) for i in range(2)]
            h2st, b_h2st = sb(st, "h2st", [128, 16, 512], BF16)
            pts = [ps(st, "pt%d" % i, [128, 1024], BF16) for i in range(4)]
            pos = [ps(st, "po%d" % i, [128, 512], F32) for i in range(4)]
            yTv = yT_s.rearrange("(kc p) t -> p kc t", p=128)
            h2v = h2T_s.rearrange("(kc p) t -> p kc t", p=128)
            def p5a(tt):
                tb, q = tt // 4, tt % 4
                yb, b_yb = ybs[tb % 2]
                if tt == 0:
                    load("sp", yb[:], yTv[:, :, 0:512], b_yb, B["yT_s"])
                if q == 0 and tb + 1 < 4:
                    load("sp", ybs[(tb + 1) % 2][0][:], yTv[:, :, (tb + 1) * 512:(tb + 2) * 512], ybs[(tb + 1) % 2][1], B["yT_s"])
                xt, b_xt = xts[tt % 2]; x1t, b_x1t = x1ts[tt % 2]
                load("sp", xt[:], x_d[tt * 128:(tt + 1) * 128, :], b_xt)
                for db in range(4):
                    po, b_po = pos[db]
                    tq, b_tq = tts[db % 2]
                    for dc in range(16):
                        S.op("pe", lambda e, dc=dc, q=q, db=db, po=po, yb=yb: e.matmul(po[:], lhsT=yb[:, dc, q * 128:(q + 1) * 128], rhs=wout[:, dc, db * 512:(db + 1) * 512],
                                                                                     start=(dc == 0), stop=(dc == 15)), reads=[b_yb, b_wout], writes=[b_po])
                    S.op("dve", lambda e, db=db, po=po, tq=tq: e.tensor_tensor(out=tq[:], in0=po[:], in1=g1bc[:, db * 512:(db + 1) * 512], op=ALU.mult),
                         reads=[b_po, b_g1], writes=[b_tq])
                    S.op("dve", lambda e, db=db, tq=tq, xt=xt, x1t=x1t: e.tensor_tensor(out=x1t[:, db * 512:(db + 1) * 512], in0=tq[:], in1=xt[:, db * 512:(db + 1) * 512], op=ALU.add),
                         reads=[b_tq, b_xt], writes=[b_x1t])
                store("sp", x1_s[tt * 128:(tt + 1) * 128, :], x1t[:], b_x1t, B["x1_s"])

            def p5n(tt):
                x1t, b_x1t = x1ts[tt % 2]
                norm_p1(x1t[:], b_x1t, junk, b_junk, ssqs[tt % 2][0], ssqs[tt % 2][1], xns[tt % 2][0], xns[tt % 2][1])

            def p5b(tt):
                tb, q = tt // 4, tt % 4
                norm_p2(xns[tt % 2][0], xns[tt % 2][1], pts, lambda kc, q=q: h2st[:, kc, q * 128:(q + 1) * 128], b_h2st,
                        lambda kc: A2[:, kc:kc + 1], lambda kc: modT[:, 48 + kc, 0:1], b_A2, b_modT)
                if q == 3:
                    store("sp", h2v[:, :, tb * 512:(tb + 1) * 512], h2st[:], b_h2st, B["h2T_s"])

            p5a(0)
            p5n(0)
            for tt in range(16):
                if tt + 1 < 16:
                    p5a(tt + 1)
                p5b(tt)
                if tt + 1 < 16:
                    p5n(tt + 1)
            S.flush()
        if upto <= 6:
            return nc, S

        wupv = wup_d.rearrange("(kc p) n -> p kc n", p=128)
        wdnv = wdn_d.rearrange("(fc p) n -> p fc n", p=128)
        h2v = h2T_s.rearrange("(kc p) t -> p kc t", p=128)
        for blk in range(2):
            tok0 = blk * 1024
            with contextlib.ExitStack() as st:
                gT, b_gT = sb(st, "gT", [128, 44, 1024], BF16)
                with contextlib.ExitStack() as st2:
                    h2b, b_h2b = sb(st2, "h2b", [128, 16, 1024], BF16)
                    halo, b_halo = sb(st2, "halo", [128, 16, 2], BF16)
                    was = [sb(st2, "wa%d" % i, [128, 16, 256], BF16) for i in range(2)]
                    wbs = [sb(st2, "wb%d" % i, [128, 16, 256], BF16) for i in range(2)]
                    uxs = [sb(st2, "ux%d" % i, [128, 1026], F32) for i in range(2)]
                    tcs = [sb(st2, "tc%d" % i, [128, 1024], F32) for i in range(2)]
                    pus = [ps(st2, "pu%d" % i, [128, 512], F32) for i in range(6)]
                    ph, b_ph = ps(st2, "ph", [128, 512], F32)
                    load("sp", h2b[:], h2v[:, :, tok0:tok0 + 1024], b_h2b, B["h2T_s"])
                    S.op("pool", lambda e: e.memset(halo[:], 0.0), writes=[b_halo])
                    if blk == 1:
                        load("sp", halo[:, :, 0:1], h2v[:, :, tok0 - 1:tok0], b_halo, B["h2T_s"], slow=True)
                    else:
                        load("sp", halo[:, :, 1:2], h2v[:, :, tok0 + 1024:tok0 + 1025], b_halo, B["h2T_s"], slow=True)
                    ipu = 0
                    iph = 0
                    for i2 in range(22):
                        wa, b_wa = was[i2 % 2]; wb, b_wb = wbs[i2 % 2]
                        load("pool", wa[:], wupv[:, :, i2 * 256:(i2 + 1) * 256], b_wa)
                        load("pool", wb[:], wupv[:, :, FF + i2 * 256:FF + (i2 + 1) * 256], b_wb)
                        for jj in range(2):
                            i = i2 * 2 + jj
                            for part in range(2):
                                wtile, b_wt_ = (wa, b_wa) if part == 0 else (wb, b_wb)
                                ux, b_ux = uxs[part]; tcv, b_tc = tcs[part]
                                col = i + 44 * part
                                for sbk in range(2):
                                    pu, b_pu = pus[ipu % 6]; ipu += 1
                                    for kc in range(16):
                                        S.op("pe", lambda e, kc=kc, jj=jj, sbk=sbk, pu=pu, wtile=wtile: e.matmul(pu[:], lhsT=wtile[:, kc, jj * 128:(jj + 1) * 128],
                                                                                                                rhs=h2b[:, kc, sbk * 512:(sbk + 1) * 512], start=(kc == 0), stop=(kc == 15)),
                                             reads=[b_wt_, b_h2b], writes=[b_pu])
                                    S.op("act", lambda e, sbk=sbk, pu=pu, ux=ux: e.activation(out=ux[:, 1 + sbk * 512:1 + (sbk + 1) * 512], in_=pu[:], func=AF.Copy),
                                         reads=[b_pu], writes=[b_ux])
                                hs = (iph % 8) * 2; iph += 1
                                for kc in range(16):
                                    S.op("pe", lambda e, kc=kc, jj=jj, hs=hs, wtile=wtile: e.matmul(ph[:, hs:hs + 2], lhsT=wtile[:, kc, jj * 128:(jj + 1) * 128], rhs=halo[:, kc, :],
                                                                                                  start=(kc == 0), stop=(kc == 15)), reads=[b_wt_, b_halo], writes=[b_ph])
                                S.op("act", lambda e, hs=hs, ux=ux: e.activation(out=ux[:, 0:1], in_=ph[:, hs:hs + 1], func=AF.Copy), reads=[b_ph], writes=[b_ux])
                                S.op("act", lambda e, hs=hs, ux=ux: e.activation(out=ux[:, 1025:1026], in_=ph[:, hs + 1:hs + 2], func=AF.Copy), reads=[b_ph], writes=[b_ux])
                                S.op("act", lambda e, ux=ux, tcv=tcv, col=col: e.activation(out=tcv[:], in_=ux[:, 1:1025], func=AF.Identity,
                                                                                           scale=fm[:, V_CW + 88 + col:V_CW + 88 + col + 1], bias=fm[:, V_CB + col:V_CB + col + 1]),
                                     reads=[b_ux, b_fm], writes=[b_tc])
                                S.op("dve", lambda e, ux=ux, tcv=tcv, col=col: e.scalar_tensor_tensor(out=tcv[:], in0=ux[:, 0:1024], scalar=fm[:, V_CW + col:V_CW + col + 1], in1=tcv[:],
                                                                                                     op0=ALU.mult, op1=ALU.add), reads=[b_ux, b_fm, b_tc], writes=[b_tc])
                                S.op("dve", lambda e, ux=ux, tcv=tcv, col=col: e.scalar_tensor_tensor(out=tcv[:], in0=ux[:, 2:1026], scalar=fm[:, V_CW + 176 + col:V_CW + 176 + col + 1], in1=tcv[:],
                                                                                                     op0=ALU.mult, op1=ALU.add), reads=[b_ux, b_fm, b_tc], writes=[b_tc])
                            S.op("act", lambda e: e.activation(out=tcs[0][0][:], in_=tcs[0][0][:], func=AF.Silu), reads=[tcs[0][1]], writes=[tcs[0][1]])
                            S.op("dve", lambda e, i=i: e.tensor_tensor(out=gT[:, i, :], in0=tcs[0][0][:], in1=tcs[1][0][:], op=ALU.mult),
                                 reads=[tcs[0][1], tcs[1][1]], writes=[b_gT])
                    S.flush()
                with contextlib.ExitStack() as st2:
                    g2bc, b_g2 = sb(st2, "g2bc", [128, D], F32)
                    load("sp", g2bc[:], mod_s[0:1, 5 * D:6 * D].partition_broadcast(128), b_g2, B["mod_s"])
                    wds = [sb(st2, "wd%d" % i, [128, 4, 512], BF16) for i in range(4)]
                    wdf = [sb(st2, "wdf%d" % i, [128, 4, 512], F32) for i in range(4)]
                    x1p = [sb(st2, "x1p%d" % i, [128, 512], F32) for i in range(8)]
                    tps = [sb(st2, "tp%d" % i, [128, 512], F32) for i in range(8)]
                    pds = [ps(st2, "pd%d" % i, [128, 512], F32) for i in range(8)]
                    iw = 0
                    for db in range(4):
                        for tt in range(8):
                            r0 = tok0 + tt * 128
                            load("sp", x1p[tt][0][:], x1_s[r0:r0 + 128, db * 512:(db + 1) * 512], x1p[tt][1], B["x1_s"])
                        for f4 in range(11):
                            wd, b_wd = wds[iw % 4]; wf, b_wf = wdf[iw % 4]; iw += 1
                            load("act", wf[:], wdnv[:, f4 * 4:(f4 + 1) * 4, db * 512:(db + 1) * 512], b_wf)
                            S.op("act", lambda e, wf=wf, wd=wd: e.activation(out=wd[:].rearrange("p a b -> p (a b)"), in_=wf[:].rearrange("p a b -> p (a b)"), func=AF.Copy),
                                 reads=[b_wf], writes=[b_wd])
                            for fj in range(4):
                                fc = f4 * 4 + fj
                                for tt in range(8):
                                    S.op("pe", lambda e, fc=fc, fj=fj, tt=tt, wd=wd: e.matmul(pds[tt][0][:], lhsT=gT[:, fc, tt * 128:(tt + 1) * 128], rhs=wd[:, fj, :],
                                                                                            start=(fc == 0), stop=(fc == 43)), reads=[b_gT, b_wd], writes=[pds[tt][1]])
                        for tt in range(8):
                            S.op("dve", lambda e, tt=tt, db=db: e.tensor_tensor(out=tps[tt][0][:], in0=pds[tt][0][:], in1=g2bc[:, db * 512:(db + 1) * 512], op=ALU.mult),
                                 reads=[pds[tt][1], b_g2], writes=[tps[tt][1]])
                        for tt in range(8):
                            r0 = tok0 + tt * 128
                            S.op("pool", lambda e, tt=tt: e.tensor_tensor(out=tps[tt][0][:], in0=tps[tt][0][:], in1=x1p[tt][0][:], op=ALU.add),
                                 reads=[tps[tt][1], x1p[tt][1]], writes=[tps[tt][1]])
                            store("sp", x2_s[r0:r0 + 128, db * 512:(db + 1) * 512], tps[tt][0][:], tps[tt][1], B["x2_s"])
                    S.flush()
        if upto <= 7:
            return nc, S

        with contextlib.ExitStack() as st:
            fnw, b_fnw = sb(st, "fnw", [128, D], F32)
            load("sp", fnw[:], fnw_d.partition_broadcast(128), b_fnw)
            xts = [sb(st, "xf%d" % i, [128, D], F32) for i in range(4)]
            ots = [sb(st, "of%d" % i, [128, D], F32) for i in range(2)]
            junk, b_junk = sb(st, "junk", [128, D], BF16)
            ssqs = [sb(st, "ssq%d" % i, [128, 1], F32) for i in range(2)]
            for tt in range(3):
                load("sp", xts[tt][0][:], x2_s[tt * 128:(tt + 1) * 128, :], xts[tt][1], B["x2_s"])
            for tt in range(16):
                xt, b_xt = xts[tt % 4]; ot, b_ot = ots[tt % 2]; ssq, b_ssq = ssqs[tt % 2]
                if tt + 3 < 16:
                    load("sp", xts[(tt + 3) % 4][0][:], x2_s[(tt + 3) * 128:(tt + 4) * 128, :], xts[(tt + 3) % 4][1], B["x2_s"])
                S.op("act", lambda e, xt=xt: e.activation(out=junk[:], in_=xt[:], func=AF.Square), reads=[b_xt], writes=[b_junk])
                S.op("dve", lambda e, ssq=ssq: e.reduce_sum(out=ssq[:], in_=junk[:], axis=mybir.AxisListType.X), reads=[b_junk], writes=[b_ssq])
                S.op("pool", lambda e, ssq=ssq: e.tensor_scalar(out=ssq[:], in0=ssq[:], scalar1=1.0 / D, scalar2=EPS, op0=ALU.mult, op1=ALU.add), reads=[b_ssq], writes=[b_ssq])
                S.op("pool", lambda e, ssq=ssq: e.tensor_tensor(out=ssq[:], in0=ssq[:], in1=mhalf[:], op=ALU.pow), reads=[b_ssq, b_mhalf], writes=[b_ssq])
                S.op("dve", lambda e, xt=xt, ot=ot, ssq=ssq: e.scalar_tensor_tensor(out=ot[:], in0=xt[:], scalar=ssq[:, 0:1], in1=fnw[:], op0=ALU.mult, op1=ALU.mult),
                     reads=[b_xt, b_ssq, b_fnw], writes=[b_ot])
                store("sp", out_d[tt * 128:(tt + 1) * 128, :], ot[:], b_ot, B["out"])
            S.flush()
        return nc, S


_CONST = None


def _consts():
    global _CONST
    if _CONST is not None:
        return _CONST
    bf = ml_dtypes.bfloat16
    rows = T // 64
    r, col = np.meshgrid(np.arange(rows), np.arange(64), indexing="ij")
    pos = np.stack([r.reshape(-1), col.reshape(-1)], axis=-1).astype(np.float32)
    inv = (np.float32(10000.0) ** (-(np.arange(16, dtype=np.float32)) / np.float32(16))).astype(np.float32)
    ang = (pos[:, :, None] * inv).astype(np.float32)
    cs, sn = np.cos(ang).astype(np.float32), np.sin(ang).astype(np.float32)
    cosT = np.zeros((128, T), np.float32); sinT = np.zeros((128, T), np.float32)
    perm = np.zeros((128, 128), np.float32)
    for p in range(128):
        a, h, i = (p % 64) // 32, (p % 32) // 16, p % 16
        cosT[p] = cs[:, a, i]
        sinT[p] = sn[:, a, i] * (-1.0 if h == 0 else 1.0)
        partner = p + 16 if h == 0 else p - 16
        perm[partner, p] = 1.0
    s_, t_ = np.meshgrid(np.arange(64), np.arange(64), indexing="ij")
    tri = np.stack([(t_ >= s_), (t_ <= s_)], 0).astype(np.float32)
    tri = np.broadcast_to(tri.transpose(1, 0, 2)[:, :, None, :], (64, 2, 8, 64)).reshape(64, 1024)
    smask = np.ones((128, TA), np.float32); smask[:, ::64] = 0.0
    _CONST = dict(cosT=cosT, sinT=sinT, perm=perm.astype(bf), identb=np.eye(128, dtype=np.float32).astype(bf),
                  identf=np.eye(128, dtype=np.float32), tri=np.ascontiguousarray(tri).astype(bf), smask=smask)
    return _CONST


def make_in_maps(inputs):
    f = lambda a: np.ascontiguousarray(np.asarray(a, dtype=np.float32))
    x = f(inputs["x"]); c = f(inputs["c"]); ctx = f(inputs["ctx"]); c_ctx = f(inputs["c_ctx"])
    shared = dict(_consts())
    shared["w_mod"] = f(inputs["w_mod"][0]); shared["w_in"] = f(inputs["w_in"][0])
    shared["b_mod2"] = np.ascontiguousarray(np.stack([f(inputs["b_mod"][0])] * 2, 0))
    shared["lam4"] = np.concatenate([f(inputs[k][0]) for k in ("lam_q1", "lam_k1", "lam_q2", "lam_k2")]).reshape(1, 256)
    shared["subln"] = f(inputs["subln_w"][0]).reshape(1, 128)
    shared["w_ba"] = f(inputs["w_branch_attn"][0]); shared["w_br"] = f(inputs["w_branch_rec"][0]); shared["w_out"] = f(inputs["w_out"][0])
    shared["w_up"] = f(inputs["w_up"][0]); shared["w_down"] = f(inputs["w_down"][0]); shared["fnw"] = f(inputs["final_norm_w"]).reshape(1, D)
    vec_tail = np.concatenate([f(inputs["norm1_w"][0]), f(inputs["norm2_w"][0]), f(inputs["rec_gnorm_w"][0]),
                               f(inputs["rec_lb"]).reshape(-1), f(inputs["conv_w"][0]).reshape(-1), f(inputs["conv_b"][0])])
    maps = []
    for b in range(8):
        m = dict(shared)
        m["x"] = x[b]; m["ctx"] = ctx[b]
        m["vecs"] = np.ascontiguousarray(np.concatenate([c[b], c_ctx, vec_tail]).reshape(NV, 128))
        maps.append(m)
    return maps


_NC = None


def kernel(**inputs):
    global _NC
    if _NC is None:
        _NC = build()[0]
    maps = make_in_maps(inputs)
    res = run_bass_kernel_spmd(_NC, maps, core_ids=list(range(8)))
    return np.stack([np.asarray(r["out"], dtype=np.float32) for r in res.results], 0)
```

```python
import contextlib
import numpy as np
import ml_dtypes
import concourse.bass as bass
import concourse.mybir as mybir
from concourse.bass_utils import run_bass_kernel_spmd

F32 = mybir.dt.float32
BF16 = mybir.dt.bfloat16
AF = mybir.ActivationFunctionType
ALU = mybir.AluOpType

D = 2048
T = 2048
TC = 256
TA = T + TC
NIN = 12288
FF = 5632
EPS = 1e-6
LAM_INIT = 0.2

V_C, V_CC, V_N1, V_N2, V_GN, V_LB, V_CW, V_CB = 0, 16, 32, 48, 64, 72, 104, 368
NV = 456


class Buf:
    __slots__ = ("w", "rs")

    def __init__(self):
        self.w = None
        self.rs = []


class Op:
    __slots__ = ("eng", "fn", "deps", "signaled", "val", "is_dma", "sem", "epoch")

    def __init__(self, eng, fn, is_dma, epoch):
        self.eng = eng
        self.fn = fn
        self.deps = []
        self.signaled = False
        self.val = None
        self.is_dma = is_dma
        self.sem = None
        self.epoch = epoch


class Sched:
    ENGS = ("pe", "act", "dve", "pool", "sp")
    NDMA = 8

    def __init__(self, nc, es):
        self.nc = nc
        self.csem = {e: es.enter_context(nc.semaphore("c_" + e)) for e in ("pe", "act", "dve", "pool")}
        self.dsem = {q: [es.enter_context(nc.semaphore("d_%s%d" % (q, i))) for i in range(self.NDMA)]
                     for q in ("sp", "pool", "act")}
        self.psem = es.enter_context(nc.semaphore("phase"))
        self.cval = {e: 0 for e in self.csem}
        self.dval = {q: [0] * self.NDMA for q in self.dsem}
        self.dlast = {q: [None] * self.NDMA for q in self.dsem}
        self.drr = {q: 0 for q in self.dsem}
        self.waited = {e: {} for e in self.ENGS}
        self.ops = {e: [] for e in self.ENGS}
        self.epoch = 0
        self.nops = 0

    def _add_dep(self, op, dep):
        if dep is None or dep is op or dep.epoch != self.epoch:
            return
        if dep.eng == "pe" and op.eng == "pe" and not dep.is_dma and not op.is_dma:
            return
        if dep not in op.deps:
            op.deps.append(dep)

    def op(self, eng, fn, reads=(), writes=()):
        o = Op(eng, fn, False, self.epoch)
        self._track(o, reads, writes)
        self.ops[eng].append(o)
        return o

    def dma(self, q, fn, reads=(), writes=()):
        o = Op(q, fn, True, self.epoch)
        self._track(o, reads, writes)
        self.ops[q].append(o)
        return o

    def _track(self, o, reads, writes):
        for b in reads:
            self._add_dep(o, b.w)
        for b in writes:
            for r in b.rs:
                self._add_dep(o, r)
            self._add_dep(o, b.w)
        for b in reads:
            b.rs.append(o)
        for b in writes:
            b.w = o
            b.rs = []

    def flush(self):
        nc = self.nc
        ops = self.ops
        fence = Op("sp", None, False, self.epoch)
        for e in self.ENGS:
            last = None
            for o in ops[e]:
                if o.is_dma:
                    fence.deps.append(o)
                else:
                    last = o
            if last is not None and e != "sp":
                fence.deps.append(last)
        ops["sp"].append(fence)
        for e in self.ENGS:
            for o in ops[e]:
                for d in o.deps:
                    d.signaled = True
        for e in self.ENGS:
            for o in ops[e]:
                if o.fn is None:
                    continue
                if o.is_dma:
                    q = o.eng
                    i = self.drr[q]
                    self.drr[q] = (i + 1) % self.NDMA
                    prev = self.dlast[q][i]
                    if prev is not None and prev.epoch == self.epoch:
                        o.deps.append(prev)
                    self.dval[q][i] += 16
                    o.sem = self.dsem[q][i]
                    o.val = self.dval[q][i]
                    self.dlast[q][i] = o
                elif o.signaled:
                    self.cval[e] += 1
                    o.sem = self.csem[e]
                    o.val = self.cval[e]
        waited = self.waited
        epoch = self.epoch
        psem = self.psem

        def emit(ename, eng):
            w = waited[ename]
            if epoch > 0:
                eng.wait_ge(psem, epoch)
            for o in ops[ename]:
                for d in o.deps:
                    k = d.sem.num
                    if w.get(k, 0) >= d.val:
                        continue
                    eng.wait_ge(d.sem, d.val)
                    w[k] = d.val
                if o.fn is None:
                    eng.sem_inc(psem, 1)
                    continue
                ins = o.fn(eng)
                self.nops += 1
                if o.is_dma:
                    ins.then_inc(o.sem, 16)
                elif o.signaled:
                    ins.then_inc(o.sem, 1)

        with nc.Block() as block:
            @block.sync
            def _(e):
                emit("sp", e)

            @block.gpsimd
            def _(e):
                emit("pool", e)

            @block.scalar
            def _(e):
                emit("act", e)

            @block.vector
            def _(e):
                emit("dve", e)

            @block.tensor
            def _(e):
                emit("pe", e)
        self.ops = {e: [] for e in self.ENGS}
        self.epoch += 1


def build(upto=99, debug=False):
    nc = bass.Bass("TRN2", target_bir_lowering=False)
    kscr = "ExternalOutput" if debug else "Internal"

    def din(name, shape, dt=F32):
        return nc.dram_tensor(name, list(shape), dt, kind="ExternalInput").ap()

    def dscr(name, shape, dt):
        return nc.dram_tensor(name, list(shape), dt, kind=kscr).ap()

    x_d = din("x", [T, D]); ctx_d = din("ctx", [TC, D]); vecs_d = din("vecs", [NV, 128])
    wmod_d = din("w_mod", [D, NIN]); bmod_d = din("b_mod2", [2, NIN]); win_d = din("w_in", [D, NIN])
    lam_d = din("lam4", [1, 256]); subln_d = din("subln", [1, 128])
    wba_d = din("w_ba", [1024, D]); wbr_d = din("w_br", [1024, D]); wout_d = din("w_out", [D, D])
    wup_d = din("w_up", [D, 2 * FF]); wdn_d = din("w_down", [FF, D]); fnw_d = din("fnw", [1, D])
    cos_d = din("cosT", [128, T]); sin_d = din("sinT", [128, T])
    identb_d = din("identb", [128, 128], BF16); identf_d = din("identf", [128, 128]); perm_d = din("perm", [128, 128], BF16)
    tri_d = din("tri", [64, 2 * 8 * 64], BF16); smask_d = din("smask", [128, TA])
    out_d = nc.dram_tensor("out", [T, D], F32, kind="ExternalOutput").ap()

    mod_s = dscr("mod_s", [2, NIN], F32)
    kT_s = dscr("kT_s", [8, 128, TA], BF16); qT_s = dscr("qT_s", [8, 128, T], BF16)
    v_s = dscr("v_s", [TA, 8 * 130], BF16); rf_s = dscr("rf_s", [2, 1024, TA], F32)
    rv_s = dscr("rv_s", [TA, 1024], BF16); rq_s = dscr("rq_s", [1024, T], BF16); rg_s = dscr("rg_s", [1024, T], BF16)
    ga_s = dscr("ga_s", [D, T], BF16); gr_s = dscr("gr_s", [D, T], BF16)
    attT_s = dscr("attT_s", [1024, T], BF16); recT_s = dscr("recT_s", [1024, T], BF16)
    yT_s = dscr("yT_s", [D, T], BF16); x1_s = dscr("x1_s", [T, D], F32); h2T_s = dscr("h2T_s", [D, T], BF16)
    x2_s = dscr("x2_s", [T, D], F32)
    o_s = dscr("o_s", [1024, T], BF16)
    B = {k: Buf() for k in ("mod_s", "kT_s", "qT_s", "v_s", "rf_s", "rv_s", "rq_s", "rg_s", "ga_s", "gr_s",
                            "attT_s", "recT_s", "yT_s", "x1_s", "h2T_s", "x2_s", "out", "o_s")}

    with contextlib.ExitStack() as es:
        S = Sched(nc, es)

        uid = [0]

        def sb(st, name, shape, dt):
            uid[0] += 1
            return st.enter_context(nc.sbuf_tensor("s%d_%s" % (uid[0], name), list(shape), dt)), Buf()

        def ps(st, name, shape, dt=F32):
            uid[0] += 1
            return st.enter_context(nc.psum_tensor("p%d_%s" % (uid[0], name), list(shape), dt)), Buf()

        def load(q, dst, src, bdst, bsrc=None, slow=False):
            if slow:
                S.dma(q, lambda e: e.dma_start(out=dst, in_=src, allow_slow_non_contiguous=True), reads=[bsrc] if bsrc else [], writes=[bdst])
            else:
                S.dma(q, lambda e: e.dma_start(out=dst, in_=src), reads=[bsrc] if bsrc else [], writes=[bdst])

        def store(q, dst, src, bsrc, bdst=None):
            S.dma(q, lambda e: e.dma_start(out=dst, in_=src), reads=[bsrc], writes=[bdst] if bdst else [])

        fm, b_fm = sb(es, "fm", [128, NV], F32)
        modT, b_modT = sb(es, "modT", [128, 96, 2], F32)
        A1, b_A1 = sb(es, "A1", [128, 16, 2], F32)
        A2, b_A2 = sb(es, "A2", [128, 16], F32)
        nlam, b_nlam = sb(es, "nlam", [128, 1], F32)
        low, b_low = sb(es, "low", [128, 16], F32)
        oml, b_oml = sb(es, "oml", [128, 16], F32)
        slw, b_slw = sb(es, "slw", [128, 128], F32)
        identb, b_idb = sb(es, "identb", [128, 128], BF16)
        identf, b_idf = sb(es, "identf", [128, 128], F32)
        mhalf, b_mhalf = sb(es, "mhalf", [128, 1], F32)
        onesb, b_onesb = sb(es, "onesb", [128, 128], BF16)

        with contextlib.ExitStack() as st:
            vrow, b_vrow = sb(st, "vrow", [128, 4, 128], F32)
            scb, b_scb = sb(st, "scb", [128, 16, 2], BF16)
            bmod, b_bmod = sb(st, "bmod", [2, NIN], F32)
            modrow, b_modrow = sb(st, "modrow", [2, NIN], F32)
            wt = [sb(st, "wt%d" % i, [128, 16, 512], BF16) for i in range(2)]
            lamt, b_lamt = sb(st, "lamt", [128, 256], F32)
            lamp, b_lamp = sb(st, "lamp", [128, 2, 64], F32)
            lams, b_lams = sb(st, "lams", [128, 2], F32)
            tmp16, b_tmp16 = sb(st, "tmp16", [128, 16, 2], F32)
            pv, b_pv = ps(st, "pv", [128, 512], F32)
            pmm = [ps(st, "pmm%d" % i, [128, 512], F32) for i in range(2)]
            pmt, b_pmt = ps(st, "pmt", [128, 512], F32)

            load("sp", identb[:], identb_d[:, :], b_idb)
            load("sp", identf[:], identf_d[:, :], b_idf)
            S.op("pool", lambda e: e.memset(mhalf[:], -0.5), writes=[b_mhalf])
            S.op("pool", lambda e: e.memset(onesb[:], 1.0), writes=[b_onesb])
            nrows = [128, 128, 128, NV - 384]
            for i in range(4):
                load("sp", vrow[0:nrows[i], i, :], vecs_d[i * 128:i * 128 + nrows[i], :], b_vrow)
            for i in range(4):
                n = nrows[i]
                S.op("pe", lambda e, i=i, n=n: e.transpose(out=pv[:, 0:n], in_=vrow[0:n, i, :], identity=identf[0:n, 0:n]),
                     reads=[b_vrow, b_idf], writes=[b_pv])
                S.op("dve", lambda e, i=i, n=n: e.tensor_copy(out=fm[:, i * 128:i * 128 + n], in_=pv[:, 0:n]),
                     reads=[b_pv], writes=[b_fm])
            for j, off in enumerate((V_C, V_CC)):
                S.op("act", lambda e, j=j, off=off: e.activation(out=scb[:, :, j], in_=fm[:, off:off + 16], func=AF.Silu),
                     reads=[b_fm], writes=[b_scb])
            load("sp", bmod[:], bmod_d[:, :], b_bmod)
            wv = wmod_d.rearrange("(kc p) n -> p kc n", p=128)
            for g in range(24):
                s = g % 2
                w_t, b_w = wt[s]
                load("pool", w_t[:], wv[:, :, g * 512:(g + 1) * 512], b_w)
                pm, b_pm = pmm[s]
                for kc in range(16):
                    S.op("pe", lambda e, kc=kc, w_t=w_t, pm=pm: e.matmul(pm[0:2, :], lhsT=scb[:, kc, :], rhs=w_t[:, kc, :],
                                                                         start=(kc == 0), stop=(kc == 15)),
                         reads=[b_scb, b_w], writes=[b_pm])
                S.op("dve", lambda e, g=g, pm=pm: e.tensor_tensor(out=modrow[:, g * 512:(g + 1) * 512], in0=pm[0:2, :],
                                                                  in1=bmod[:, g * 512:(g + 1) * 512], op=ALU.add),
                     reads=[b_pm, b_bmod], writes=[b_modrow])
            store("sp", mod_s[:, :], modrow[:], b_modrow, B["mod_s"])
            for j in range(96):
                S.op("pe", lambda e, j=j: e.matmul(pmt[:, 2 * j:2 * j + 2], lhsT=modrow[:, j * 128:(j + 1) * 128],
                                                   rhs=identf[0:2, 0:2], start=True, stop=True),
                     reads=[b_modrow, b_idf], writes=[b_pmt])
            S.op("dve", lambda e: e.tensor_copy(out=modT[:].rearrange("p a b -> p (a b)"), in_=pmt[:, 0:192]),
                 reads=[b_pmt], writes=[b_modT])
            S.op("dve", lambda e: e.tensor_scalar(out=tmp16[:], in0=modT[:, 16:32, :], scalar1=1.0, scalar2=None, op0=ALU.add),
                 reads=[b_modT], writes=[b_tmp16])
            for j in range(2):
                S.op("dve", lambda e, j=j: e.tensor_tensor(out=A1[:, :, j], in0=tmp16[:, :, j], in1=fm[:, V_N1:V_N1 + 16], op=ALU.mult),
                     reads=[b_tmp16, b_fm], writes=[b_A1])
            S.op("dve", lambda e: e.tensor_scalar(out=tmp16[:, :, 0], in0=modT[:, 64:80, 0], scalar1=1.0, scalar2=None, op0=ALU.add),
                 reads=[b_modT, b_A1], writes=[b_tmp16])
            S.op("dve", lambda e: e.tensor_tensor(out=A2[:], in0=tmp16[:, :, 0], in1=fm[:, V_N2:V_N2 + 16], op=ALU.mult),
                 reads=[b_tmp16, b_fm], writes=[b_A2])
            load("sp", lamt[:], lam_d.partition_broadcast(128), b_lamt)
            S.op("dve", lambda e: e.tensor_tensor(out=lamp[:], in0=lamt[:].rearrange("p (a b c) -> p a b c", a=2, b=2)[:, :, 0, :],
                                                  in1=lamt[:].rearrange("p (a b c) -> p a b c", a=2, b=2)[:, :, 1, :], op=ALU.mult),
                 reads=[b_lamt], writes=[b_lamp])
            for j in range(2):
                S.op("dve", lambda e, j=j: e.reduce_sum(out=lams[:, j:j + 1], in_=lamp[:, j, :], axis=mybir.AxisListType.X),
                     reads=[b_lamp], writes=[b_lams])
            S.op("act", lambda e: e.activation(out=lams[:], in_=lams[:], func=AF.Exp), reads=[b_lams], writes=[b_lams])
            S.op("dve", lambda e: e.tensor_tensor(out=nlam[:], in0=lams[:, 1:2], in1=lams[:, 0:1], op=ALU.subtract),
                 reads=[b_lams], writes=[b_nlam])
            S.op("dve", lambda e: e.tensor_scalar(out=nlam[:], in0=nlam[:], scalar1=-LAM_INIT, scalar2=None, op0=ALU.add),
                 reads=[b_nlam], writes=[b_nlam])
            lbv = fm[:, V_LB:V_LB + 32].rearrange("p (d l h) -> p d l h", d=2, l=2)
            S.op("dve", lambda e: e.tensor_tensor(out=low[:].rearrange("p (d h) -> p d h", d=2), in0=lbv[:, :, 0, :], in1=lbv[:, :, 1, :],
                                                  op=ALU.subtract), reads=[b_fm], writes=[b_low])
            S.op("act", lambda e: e.activation(out=low[:], in_=low[:], func=AF.Sigmoid), reads=[b_low], writes=[b_low])
            S.op("dve", lambda e: e.tensor_scalar(out=oml[:], in0=low[:], scalar1=-1.0, scalar2=1.0, op0=ALU.mult, op1=ALU.add),
                 reads=[b_low], writes=[b_oml])
            load("sp", slw[:], subln_d.partition_broadcast(128), b_slw)
            S.op("dve", lambda e: e.tensor_scalar(out=slw[:], in0=slw[:], scalar1=1.0 - LAM_INIT, scalar2=None, op0=ALU.mult),
                 reads=[b_slw], writes=[b_slw])
            S.flush()
        if upto <= 0:
            return nc, S

        def norm_p1(xt, b_xt, junk, b_junk, ssq, b_ssq, xn, b_xn):
            S.op("act", lambda e: e.activation(out=junk[:], in_=xt, func=AF.Square), reads=[b_xt], writes=[b_junk])
            S.op("dve", lambda e: e.reduce_sum(out=ssq[:], in_=junk[:], axis=mybir.AxisListType.X), reads=[b_junk], writes=[b_ssq])
            S.op("pool", lambda e: e.tensor_scalar(out=ssq[:], in0=ssq[:], scalar1=1.0 / D, scalar2=EPS, op0=ALU.mult, op1=ALU.add),
                 reads=[b_ssq], writes=[b_ssq])
            S.op("pool", lambda e: e.tensor_tensor(out=ssq[:], in0=ssq[:], in1=mhalf[:], op=ALU.pow),
                 reads=[b_ssq, b_mhalf], writes=[b_ssq])
            S.op("dve", lambda e: e.tensor_scalar(out=xn[:], in0=xt, scalar1=ssq[:, 0:1], scalar2=None, op0=ALU.mult),
                 reads=[b_xt, b_ssq], writes=[b_xn])

        def norm_p2(xn, b_xn, pts, dst_fn, b_dst, Ascal, Bscal, bA, bB):
            for g in range(4):
                pt, b_pt = pts[g % len(pts)]
                for j in range(4):
                    kc = g * 4 + j
                    S.op("pe", lambda e, kc=kc, j=j, pt=pt: e.transpose(out=pt[:, j * 128:(j + 1) * 128], in_=xn[:, kc * 128:(kc + 1) * 128],
                                                                        identity=identb[:]),
                         reads=[b_xn, b_idb], writes=[b_pt])
                for j in range(4):
                    kc = g * 4 + j
                    if False:
                        S.op("dve", lambda e, kc=kc, j=j, pt=pt: e.tensor_scalar(out=dst_fn(kc), in0=pt[:, j * 128:(j + 1) * 128],
                                                                                 scalar1=Ascal(kc), scalar2=Bscal(kc), op0=ALU.mult, op1=ALU.add),
                             reads=[b_pt, bA, bB], writes=[b_dst])
                    else:
                        S.op("act", lambda e, kc=kc, j=j, pt=pt: e.activation(out=dst_fn(kc), in_=pt[:, j * 128:(j + 1) * 128], func=AF.Identity,
                                                                              scale=Ascal(kc), bias=Bscal(kc)),
                             reads=[b_pt, bA, bB], writes=[b_dst])

        with contextlib.ExitStack() as st:
            hT, b_hT = sb(st, "hT", [128, 16, TA], BF16)
            with contextlib.ExitStack() as st2:
                xts = [sb(st2, "xt%d" % i, [128, D], F32) for i in range(2)]
                junk, b_junk = sb(st2, "junk", [128, D], BF16)
                ssqs = [sb(st2, "ssq%d" % i, [128, 1], F32) for i in range(2)]
                xns = [sb(st2, "xn%d" % i, [128, D], BF16) for i in range(2)]
                pts = [ps(st2, "pt%d" % i, [128, 1024], BF16) for i in range(4)]
                def p1a(tt):
                    s_ = tt % 2
                    xt, b_xt = xts[s_]
                    src = ctx_d[tt * 128:(tt + 1) * 128, :] if tt < 2 else x_d[(tt - 2) * 128:(tt - 1) * 128, :]
                    load("sp", xt[:], src, b_xt)
                    norm_p1(xt[:], b_xt, junk, b_junk, ssqs[s_][0], ssqs[s_][1], xns[s_][0], xns[s_][1])

                def p1b(tt):
                    s_ = tt % 2
                    jc = 1 if tt < 2 else 0
                    norm_p2(xns[s_][0], xns[s_][1], pts, lambda kc, tt=tt: hT[:, kc, tt * 128:(tt + 1) * 128], b_hT,
                            lambda kc, jc=jc: A1[:, kc, jc:jc + 1], lambda kc, jc=jc: modT[:, kc, jc:jc + 1], b_A1, b_modT)

                p1a(0)
                for tt in range(18):
                    if tt + 1 < 18:
                        p1a(tt + 1)
                    p1b(tt)
                S.flush()
            if upto <= 1:
                return nc, S
            wt = [sb(st, "wt%d" % i, [128, 16, 512], BF16) for i in range(2)]
            stf = [sb(st, "stf%d" % i, [128, TA], F32) for i in range(2)]
            stb = [sb(st, "stb%d" % i, [128, TA], BF16) for i in range(2)]
            cosT, b_cos = sb(st, "cosT", [128, T], F32)
            sinT, b_sin = sb(st, "sinT", [128, T], F32)
            perm, b_perm = sb(st, "perm", [128, 128], BF16)
            zbs = [sb(st, "zb%d" % i, [128, 512], BF16) for i in range(2)]
            t1s = [sb(st, "t1_%d" % i, [128, 512], F32) for i in range(2)]
            t2s = [sb(st, "t2_%d" % i, [128, 512], F32) for i in range(2)]
            vst = [sb(st, "vst%d" % i, [128, 4, 130], BF16) for i in range(2)]
            rst = [sb(st, "rst%d" % i, [128, 512], BF16) for i in range(2)]
            pms = [ps(st, "pm%d" % i, [128, 512], F32) for i in range(4)]
            pws = [ps(st, "pw%d" % i, [128, 512], F32) for i in range(2)]
            load("sp", cosT[:], cos_d[:, :], b_cos)
            load("sp", sinT[:], sin_d[:, :], b_sin)
            load("sp", perm[:], perm_d[:, :], b_perm)
            for i in range(2):
                S.op("pool", lambda e, i=i: e.memset(vst[i][0][:], 1.0), writes=[vst[i][1]])
            fams = ["ak"] * 2 + ["av"] * 2 + ["rf0"] * 2 + ["rf1"] * 2 + ["ri"] * 2 + ["aq"] * 2 + ["rq"] * 2 + ["rg"] * 2 + ["ga"] * 4 + ["gr"] * 4
            fstart = {}
            for g, f in enumerate(fams):
                fstart.setdefault(f, g)
            wv = win_d.rearrange("(kc p) n -> p kc n", p=128)
            ipm = 0
            irope = 0
            ist = 0
            ivs = 0
            import os
            for g in range(int(os.environ.get("K1B", "24"))):
                fam = fams[g]
                w_t, b_w = wt[g % 2]
                load("pool", w_t[:], wv[:, :, g * 512:(g + 1) * 512], b_w)
                has_ctx = g < 10
                if fam in ("av", "ri"):
                    for tt in range(18):
                        pm, b_pm = pms[ipm % 4]; ipm += 1
                        for kc in range(16):
                            S.op("pe", lambda e, kc=kc, tt=tt, pm=pm, w_t=w_t: e.matmul(pm[:], lhsT=hT[:, kc, tt * 128:(tt + 1) * 128], rhs=w_t[:, kc, :],
                                                                                         start=(kc == 0), stop=(kc == 15)),
                                 reads=[b_hT, b_w], writes=[b_pm])
                        if fam == "av":
                            v_t, b_v = vst[ivs % 2]; ivs += 1
                            S.op("dve", lambda e, pm=pm, v_t=v_t: e.tensor_copy(out=v_t[:, :, 0:128], in_=pm[:].rearrange("p (h e) -> p h e", h=4)),
                                 reads=[b_pm], writes=[b_v])
                            hh = (g - 2) * 4
                            store("sp", v_s[tt * 128:(tt + 1) * 128, hh * 130:(hh + 4) * 130], v_t[:].rearrange("p h e -> p (h e)"), b_v, B["v_s"])
                        else:
                            r_t, b_r = rst[ivs % 2]; ivs += 1
                            S.op("dve", lambda e, pm=pm, r_t=r_t: e.tensor_copy(out=r_t[:], in_=pm[:]), reads=[b_pm], writes=[b_r])
                            store("sp", rv_s[tt * 128:(tt + 1) * 128, (g - 8) * 512:(g - 7) * 512], r_t[:], b_r, B["rv_s"])
                    continue
                blocks = ([(0, 256)] if has_ctx else []) + [(256 + 512 * i, 512) for i in range(4)]
                for j in range(4):
                    fi = (g - fstart[fam]) * 4 + j
                    isf32 = fam in ("rf0", "rf1")
                    stg, b_stg = (stf if isf32 else stb)[ist % 2]; ist += 1
                    for (t0, n) in blocks:
                        c0 = t0 if has_ctx else t0 - 256
                        pm, b_pm = pms[ipm % 4]; ipm += 1
                        for kc in range(16):
                            S.op("pe", lambda e, kc=kc, j=j, t0=t0, n=n, pm=pm, w_t=w_t: e.matmul(pm[:, 0:n], lhsT=w_t[:, kc, j * 128:(j + 1) * 128],
                                                                                                 rhs=hT[:, kc, t0:t0 + n], start=(kc == 0), stop=(kc == 15)),
                                 reads=[b_hT, b_w], writes=[b_pm])
                        dst = stg[:, c0:c0 + n]
                        if fam in ("aq", "ak") and t0 >= 256 and os.environ.get("K1R", "1") == "1":
                            zb, b_zb = zbs[irope % 2]; t1, b_t1 = t1s[irope % 2]; t2, b_t2 = t2s[irope % 2]
                            pw, b_pw = pws[irope % 2]; irope += 1
                            s0 = t0 - 256
                            S.op("act", lambda e, pm=pm, zb=zb: e.activation(out=zb[:], in_=pm[:], func=AF.Copy), reads=[b_pm], writes=[b_zb])
                            S.op("pe", lambda e, pw=pw, zb=zb: e.matmul(pw[:], lhsT=perm[:], rhs=zb[:], start=True, stop=True),
                                 reads=[b_perm, b_zb], writes=[b_pw])
                            S.op("dve", lambda e, pm=pm, t1=t1, s0=s0: e.tensor_tensor(out=t1[:], in0=pm[:], in1=cosT[:, s0:s0 + 512], op=ALU.mult),
                                 reads=[b_pm, b_cos, b_zb, b_pw], writes=[b_t1])
                            S.op("dve", lambda e, pw=pw, t2=t2, s0=s0: e.tensor_tensor(out=t2[:], in0=pw[:], in1=sinT[:, s0:s0 + 512], op=ALU.mult),
                                 reads=[b_pw, b_sin], writes=[b_t2])
                            S.op("dve", lambda e, t1=t1, t2=t2, dst=dst: e.tensor_tensor(out=dst, in0=t1[:], in1=t2[:], op=ALU.add),
                                 reads=[b_t1, b_t2], writes=[b_stg])
                        else:
                            func = {"ak": AF.Copy, "aq": AF.Copy, "rf0": AF.Copy, "rf1": AF.Copy, "rq": AF.Silu, "rg": AF.Silu, "ga": AF.Sigmoid, "gr": AF.Sigmoid}[fam]
                            S.op("act", lambda e, pm=pm, n=n, dst=dst, func=func: e.activation(out=dst, in_=pm[:, 0:n], func=func),
                                 reads=[b_pm], writes=[b_stg])
                    ncol = TA if has_ctx else T
                    if fam == "ak":
                        dd, bd = kT_s[fi], B["kT_s"]
                    elif fam == "aq":
                        dd, bd = qT_s[fi], B["qT_s"]
                    elif isf32:
                        dd, bd = rf_s[int(fam[2]), fi * 128:(fi + 1) * 128, :], B["rf_s"]
                    else:
                        scr = {"rq": rq_s, "rg": rg_s, "ga": ga_s, "gr": gr_s}[fam]
                        dd, bd = scr[fi * 128:(fi + 1) * 128, :], B[fam + "_s"]
                    store("sp", dd, stg[:, 0:ncol], b_stg, bd)
            S.flush()
        if upto <= 2:
            return nc, S

        with contextlib.ExitStack() as st:
            vaug, b_vaug = sb(st, "vaug", [128, 18, 8 * 130], BF16)
            kTs = [sb(st, "kT%d" % i, [128, TA], BF16) for i in range(2)]
            qTs = [sb(st, "qT%d" % i, [128, T], BF16) for i in range(2)]
            eTs = [sb(st, "eT%d" % i, [128, 1024], BF16) for i in range(3)]
            attst = [sb(st, "attst%d" % i, [128, T], BF16) for i in range(2)]
            o_t = [sb(st, "o_t%d" % i, [128, 128], F32) for i in range(2)]
            t_t = [sb(st, "t_t%d" % i, [128, 128], F32) for i in range(2)]
            a_t = [sb(st, "a_t%d" % i, [128, 128], BF16) for i in range(2)]
            jk, b_jk = sb(st, "jk", [128, 128], F32)
            sms = [sb(st, "sm%d" % i, [128, 4], F32) for i in range(2)]
            scs = [ps(st, "sc%d" % i, [128, 1024], F32) for i in range(2)]
            accs = [ps(st, "acc%d" % i, [128, 512], F32) for i in range(3)]
            b_acc = [Buf() for _ in range(8)]
            pT, b_pT = ps(st, "pT", [128, 1024], BF16)

            def accap(idx, lo, hi):
                return accs[idx // 3][0][:, (idx % 3) * 130 + lo:(idx % 3) * 130 + hi]

            load("sp", vaug[:], v_s.rearrange("(t p) f -> p t f", p=128), b_vaug, B["v_s"])

            def head_loads(h):
                load("sp", kTs[h % 2][0][:], kT_s[h], kTs[h % 2][1], B["kT_s"])
                load("sp", qTs[h % 2][0][:], qT_s[h], qTs[h % 2][1], B["qT_s"])

            steps = [(h, qb, kt) for h in range(8) for qb in range(4) for kt in range(18)]

            def emit_scores(i):
                h, qb, kt = steps[i]
                kT, b_kT = kTs[h % 2]; qT, b_qT = qTs[h % 2]
                sc, b_sc = scs[i % 2]; eT, b_eT = eTs[i % 3]
                for c in range(2):
                    S.op("pe", lambda e, c=c, kt=kt, qb=qb, sc=sc, kT=kT, qT=qT: e.matmul(
                        sc[:, c * 512:(c + 1) * 512], lhsT=kT[c * 64:(c + 1) * 64, kt * 128:(kt + 1) * 128],
                        rhs=qT[c * 64:(c + 1) * 64, qb * 512:(qb + 1) * 512], start=True, stop=True),
                        reads=[b_kT, b_qT], writes=[b_sc])
                S.op("act", lambda e, sc=sc, eT=eT: e.activation(out=eT[:], in_=sc[:], func=AF.Exp, scale=0.125),
                     reads=[b_sc], writes=[b_eT])

            def emit_pv(i):
                h, qb, kt = steps[i]
                eT, b_eT = eTs[i % 3]
                for c in range(2):
                    for qt in range(4):
                        idx = c * 4 + qt
                        first = (kt == 0 and idx % 3 == 0)
                        S.op("pe", lambda e, idx=idx, c=c, qt=qt, kt=kt, h=h, eT=eT, first=first: e.matmul(
                            accap(idx, 0, 129), lhsT=eT[:, c * 512 + qt * 128:c * 512 + (qt + 1) * 128],
                            rhs=vaug[:, kt, h * 130:h * 130 + 129], start=first, stop=(kt == 17)),
                            reads=[b_eT, b_vaug], writes=[b_acc[idx]] + ([b_acc[j] for j in range(idx, min(idx + 3, 8))] if first else []))

            def emit_norm(h, qb):
                ast, b_ast = attst[h % 2]
                for qt in range(4):
                    sm, b_sm = sms[qt % 2]; o, b_o = o_t[qt % 2]; tq, b_tq = t_t[qt % 2]; at, b_at = a_t[qt % 2]
                    S.op("dve", lambda e, qt=qt, sm=sm: e.reciprocal(out=sm[:, 0:1], in_=accap(qt, 128, 129)),
                         reads=[b_acc[qt]], writes=[b_sm])
                    S.op("dve", lambda e, qt=qt, sm=sm: e.reciprocal(out=sm[:, 1:2], in_=accap(4 + qt, 128, 129)),
                         reads=[b_acc[4 + qt], b_sm], writes=[b_sm])
                    S.op("dve", lambda e, sm=sm: e.tensor_tensor(out=sm[:, 1:2], in0=sm[:, 1:2], in1=nlam[:], op=ALU.mult),
                         reads=[b_sm, b_nlam], writes=[b_sm])
                    S.op("dve", lambda e, qt=qt, sm=sm, tq=tq: e.tensor_scalar(out=tq[:], in0=accap(4 + qt, 0, 128), scalar1=sm[:, 1:2],
                                                                                scalar2=None, op0=ALU.mult),
                         reads=[b_acc[4 + qt], b_sm], writes=[b_tq])
                    S.op("dve", lambda e, qt=qt, sm=sm, tq=tq, o=o: e.scalar_tensor_tensor(out=o[:], in0=accap(qt, 0, 128), scalar=sm[:, 0:1],
                                                                                            in1=tq[:], op0=ALU.mult, op1=ALU.add),
                         reads=[b_acc[qt], b_sm, b_tq], writes=[b_o])
                    S.op("pool", lambda e, o=o: e.tensor_tensor(out=jk[:], in0=o[:], in1=o[:], op=ALU.mult), reads=[b_o], writes=[b_jk])
                    S.op("dve", lambda e, sm=sm: e.reduce_sum(out=sm[:, 2:3], in_=jk[:], axis=mybir.AxisListType.X), reads=[b_jk, b_sm], writes=[b_sm])
                    S.op("pool", lambda e, sm=sm: e.tensor_scalar(out=sm[:, 2:3], in0=sm[:, 2:3], scalar1=1.0 / 128, scalar2=EPS,
                                                                   op0=ALU.mult, op1=ALU.add), reads=[b_sm], writes=[b_sm])
                    S.op("pool", lambda e, sm=sm: e.tensor_tensor(out=sm[:, 3:4], in0=sm[:, 2:3], in1=mhalf[:], op=ALU.pow),
                         reads=[b_sm, b_mhalf], writes=[b_sm])
                    S.op("dve", lambda e, o=o, sm=sm, at=at: e.scalar_tensor_tensor(out=at[:], in0=o[:], scalar=sm[:, 3:4], in1=slw[:],
                                                                                     op0=ALU.mult, op1=ALU.mult),
                         reads=[b_o, b_sm, b_slw], writes=[b_at])
                    S.op("pe", lambda e, qt=qt, at=at: e.transpose(out=pT[:, qt * 128:(qt + 1) * 128], in_=at[:], identity=identb[:]),
                         reads=[b_at, b_idb], writes=[b_pT])
                S.op("act", lambda e, qb=qb, ast=ast: e.activation(out=ast[:, qb * 512:(qb + 1) * 512], in_=pT[:, 0:512], func=AF.Copy),
                     reads=[b_pT], writes=[b_ast])
                if qb == 3:
                    store("sp", attT_s[h * 128:(h + 1) * 128, :], ast[:], b_ast, B["attT_s"])

            head_loads(0)
            head_loads(1)
            emit_scores(0)
            for i, (h, qb, kt) in enumerate(steps):
                if i + 1 < len(steps):
                    emit_scores(i + 1)
                emit_pv(i)
                if kt == 17:
                    emit_norm(h, qb)
                    if qb == 3 and h + 2 < 8:
                        head_loads(h + 2)
            S.flush()
        if upto <= 3:
            return nc, S

        with contextlib.ExitStack() as st:
            smask, b_smask = sb(st, "smask", [128, TA], F32)
            tri, b_tri = sb(st, "tri", [64, 2 * 8 * 64], BF16)
            frs = [sb(st, "fr%d" % i, [128, TA], F32) for i in range(2)]
            T1s = [sb(st, "T1_%d" % i, [128, TA], F32) for i in range(2)]
            T2s = [sb(st, "T2_%d" % i, [128, TA], F32) for i in range(2)]
            T3s = [sb(st, "T3_%d" % i, [128, TA], F32) for i in range(2)]
            rq, b_rq = sb(st, "rq", [128, T], BF16)
            v64, b_v64 = sb(st, "v64", [64, 36, 128], BF16)
            QT = [sb(st, "QT%d" % i, [128, T], BF16) for i in range(2)]
            Q2 = [sb(st, "Q2%d" % i, [128, T], BF16) for i in range(2)]
            KT = [sb(st, "KT%d" % i, [128, TA], BF16) for i in range(2)]
            K2 = [sb(st, "K2%d" % i, [128, TA], BF16) for i in range(2)]
            K2tok = [sb(st, "K2tok%d" % i, [64, 36, 128], BF16) for i in range(2)]
            msc = [sb(st, "msc%d" % i, [64, 32 * 64], BF16) for i in range(2)]
            o_d = [sb(st, "o_d%d" % i, [128, T], F32) for i in range(2)]
            osts = [sb(st, "ost%d" % i, [128, T], BF16) for i in range(2)]
            state = [sb(st, "state%d" % i, [128, 128], F32) for i in range(2)]
            statebf = [sb(st, "statebf%d" % i, [128, 128], BF16) for i in range(2)]
            statebf2 = [sb(st, "statebf2_%d" % i, [128, 128], BF16) for i in range(2)]
            dec = [sb(st, "dec%d" % i, [128, 36], F32) for i in range(2)]
            pks = [ps(st, "pk%d" % i, [128, 1024], BF16) for i in range(2)]
            pscs = [ps(st, "psc%d" % i, [128, 512], F32) for i in range(2)]
            pout = [[ps(st, "pout%d%d" % (d, i), [128, 512], F32) for i in range(1)] for d in range(2)]
            pupd = [ps(st, "pupd%d" % d, [128, 512], F32) for d in range(2)]
            load("sp", smask[:], smask_d[:, :], b_smask)
            load("sp", tri[:], tri_d[:, :], b_tri)
            v3 = lambda t: t[:].rearrange("p (c t) -> p c t", t=64)
            rvv = rv_s.rearrange("(c p) f -> p c f", p=64)

            def prep_ops(h, d):
                L = []
                Ft, b_F = frs[d]
                T1, b_T1 = T1s[d]; T2, b_T2 = T2s[d]; T3, b_T3 = T3s[d]
                pk, b_pk = pks[d]; psc, b_psc = pscs[d]
                col = d * 8 + h
                L.append(lambda: S.op("act", lambda e: e.activation(out=Ft[:], in_=Ft[:], func=AF.Sigmoid), reads=[b_F], writes=[b_F]))
                L.append(lambda: S.op("dve", lambda e: e.tensor_scalar(out=Ft[:], in0=Ft[:], scalar1=oml[:, col:col + 1], scalar2=low[:, col:col + 1],
                                                                        op0=ALU.mult, op1=ALU.add), reads=[b_F, b_oml, b_low], writes=[b_F]))
                L.append(lambda: S.op("act", lambda e: e.activation(out=T1[:], in_=Ft[:], func=AF.Ln), reads=[b_F], writes=[b_T1]))
                L.append(lambda: S.op("dve", lambda e: e.tensor_scalar(out=Ft[:], in0=Ft[:], scalar1=-1.0, scalar2=1.0, op0=ALU.mult, op1=ALU.add),
                                      reads=[b_F, b_T1], writes=[b_F]))
                L.append(lambda: S.op("dve", lambda e: e.tensor_tensor_scan(out=T2[:], data0=smask[:], data1=T1[:], initial=0.0, op0=ALU.mult, op1=ALU.add),
                                      reads=[b_smask, b_T1], writes=[b_T2]))
                if d == 0:
                    Tb, b_Tb, Tf, b_Tf = T2, b_T2, T1, b_T1
                    refi, endi = 31, 63
                else:
                    L.append(lambda: S.op("pool", lambda e: e.tensor_tensor(out=T1[:], in0=T1[:], in1=T2[:], op=ALU.subtract), reads=[b_T1, b_T2], writes=[b_T1]))
                    L.append(lambda: S.op("pool", lambda e: e.tensor_tensor(out=v3(T1), in0=v3(T1), in1=v3(T2)[:, :, 63:64].to_broadcast([128, 36, 64]), op=ALU.add),
                                          reads=[b_T1, b_T2], writes=[b_T1]))
                    Tb, b_Tb, Tf, b_Tf = T1, b_T1, T2, b_T2
                    refi, endi = 32, 0
                L.append(lambda: S.op("pool", lambda e: e.tensor_tensor(out=v3(Tf), in0=v3(Tb), in1=v3(Tb)[:, :, refi:refi + 1].to_broadcast([128, 36, 64]),
                                                                         op=ALU.subtract), reads=[b_Tb, b_Tf], writes=[b_Tf]))
                L.append(lambda: S.op("act", lambda e: e.activation(out=T3[:], in_=Tf[:], func=AF.Exp), reads=[b_Tf], writes=[b_T3]))
                L.append(lambda: S.op("dve", lambda e: e.tensor_tensor(out=QT[d][0][:], in0=rq[:], in1=T3[:, TC:TA], op=ALU.mult), reads=[b_rq, b_T3], writes=[QT[d][1]]))
                L.append(lambda: S.op("act", lambda e: e.activation(out=T3[:], in_=Tf[:], func=AF.Exp, scale=-1.0), reads=[b_Tf, QT[d][1]], writes=[b_T3]))
                L.append(lambda: S.op("dve", lambda e: e.tensor_tensor(out=KT[d][0][:], in0=Ft[:], in1=T3[:], op=ALU.mult), reads=[b_F, b_T3], writes=[KT[d][1]]))
                L.append(lambda: S.op("pool", lambda e: e.tensor_tensor(out=v3(Tf), in0=v3(Tb), in1=v3(Tb)[:, :, endi:endi + 1].to_broadcast([128, 36, 64]),
                                                                         op=ALU.subtract), reads=[b_Tb, b_Tf, b_T3], writes=[b_Tf]))
                L.append(lambda: S.op("act", lambda e: e.activation(out=T3[:], in_=Tf[:], func=AF.Exp, scale=-1.0), reads=[b_Tf, KT[d][1]], writes=[b_T3]))
                L.append(lambda: S.op("dve", lambda e: e.tensor_tensor(out=K2[d][0][:], in0=Ft[:], in1=T3[:], op=ALU.mult), reads=[b_F, b_T3], writes=[K2[d][1]]))
                n_early = len(L)
                L.append(lambda: S.op("act", lambda e: e.activation(out=dec[d][0][:], in_=v3(Tb)[:, :, endi], func=AF.Exp), reads=[b_Tb], writes=[dec[d][1]]))
                L.append(lambda: S.op("act", lambda e: e.activation(out=T3[:], in_=Tb[:], func=AF.Exp), reads=[b_Tb, K2[d][1]], writes=[b_T3]))
                L.append(lambda: S.op("dve", lambda e: e.tensor_tensor(out=Q2[d][0][:], in0=rq[:], in1=T3[:, TC:TA], op=ALU.mult), reads=[b_rq, b_T3], writes=[Q2[d][1]]))

                def transposes():
                    for c0 in range(0, 36, 8):
                        n = min(8, 36 - c0)
                        for cc in range(n):
                            c = c0 + cc
                            S.op("pe", lambda e, c=c, cc=cc: e.transpose(out=pk[0:64, cc * 128:(cc + 1) * 128], in_=K2[d][0][:, c * 64:(c + 1) * 64], identity=identb[:]),
                                 reads=[K2[d][1], b_idb], writes=[b_pk])
                        S.op("act", lambda e, c0=c0, n=n: e.activation(out=K2tok[d][0][:, c0:c0 + n, :].rearrange("p c k -> p (c k)"), in_=pk[0:64, 0:n * 128], func=AF.Copy),
                             reads=[b_pk], writes=[K2tok[d][1]])
                L.append(transposes)

                def scores():
                    for l0 in range(0, 32, 8):
                        for cc in range(8):
                            lc = l0 + cc
                            c = 4 + lc
                            S.op("pe", lambda e, c=c, lc=lc, cc=cc: e.matmul(psc[0:64, cc * 64:(cc + 1) * 64], lhsT=KT[d][0][:, c * 64:(c + 1) * 64],
                                                                              rhs=QT[d][0][:, lc * 64:(lc + 1) * 64], start=True, stop=True),
                                 reads=[KT[d][1], QT[d][1]], writes=[b_psc])
                        S.op("dve", lambda e, l0=l0: e.tensor_tensor(out=msc[d][0][:, l0 * 64:(l0 + 8) * 64], in0=psc[0:64, :], in1=tri[:, d * 512:(d + 1) * 512], op=ALU.mult),
                             reads=[b_psc, b_tri], writes=[msc[d][1]])
                L.append(scores)
                return L[:n_early], L[n_early:]

            def loads_frrq(h):
                for d in range(2):
                    load("sp", frs[d][0][:], rf_s[d, h * 128:(h + 1) * 128, :], frs[d][1], B["rf_s"])
                load("sp", rq[:], rq_s[h * 128:(h + 1) * 128, :], b_rq, B["rq_s"])

            def run_interleaved(lists):
                for k in range(max(len(l) for l in lists)):
                    for l in lists:
                        if k < len(l):
                            l[k]()

            loads_frrq(0)
            load("sp", v64[:], rvv[:, :, 0:128], b_v64, B["rv_s"])
            pe0 = [prep_ops(0, 0), prep_ops(0, 1)]
            run_interleaved([pe0[0][0], pe0[1][0]])
            run_interleaved([pe0[0][1], pe0[1][1]])
            for h in range(8):
                nxt = None
                if h + 1 < 8:
                    loads_frrq(h + 1)
                    nxt = [prep_ops(h + 1, 0), prep_ops(h + 1, 1)]
                    early_q = [list(nxt[0][0]), list(nxt[1][0])]
                else:
                    early_q = [[], []]
                for d in range(2):
                    S.op("pool", lambda e, d=d: e.memset(state[d][0][:], 0.0), writes=[state[d][1]])
                    S.op("pool", lambda e, d=d: e.memset(statebf[d][0][:], 0.0), writes=[statebf[d][1]])
                order = [list(range(36)), [3, 2, 1, 0] + list(range(35, 3, -1))]
                for step in range(36):
                    for d in range(2):
                        c = order[d][step]
                        sb_prev, b_sbp = (statebf, statebf2)[(step + 1) % 2][d]
                        sb_next, b_sbn = (statebf, statebf2)[step % 2][d]
                        pu, b_pu = pupd[d]
                        if step < 35:
                            S.op("pe", lambda e, d=d, c=c, pu=pu: e.matmul(pu[:, 0:128], lhsT=K2tok[d][0][:, c, :], rhs=v64[:, c, :], start=True, stop=True),
                                 reads=[K2tok[d][1], b_v64], writes=[b_pu])
                        if c >= 4:
                            lc = c - 4
                            grp = lc // 8
                            po, b_po = pout[d][0]
                            slot = lc % 8
                            S.op("pe", lambda e, d=d, c=c, lc=lc, slot=slot, po=po: e.matmul(po[:, slot * 64:(slot + 1) * 64], lhsT=v64[:, c, :],
                                                                                           rhs=msc[d][0][:, lc * 64:(lc + 1) * 64], start=True, stop=False),
                                 reads=[b_v64, msc[d][1]], writes=[b_po])
                            S.op("pe", lambda e, d=d, lc=lc, slot=slot, po=po, sb_prev=sb_prev: e.matmul(po[:, slot * 64:(slot + 1) * 64], lhsT=sb_prev[:],
                                                                                                       rhs=Q2[d][0][:, lc * 64:(lc + 1) * 64], start=False, stop=True),
                                 reads=[b_sbp, Q2[d][1]], writes=[b_po])
                            last_in_grp = (slot == 7) if d == 0 else (slot == 0)
                            if last_in_grp:
                                S.op("act", lambda e, d=d, grp=grp, po=po: e.activation(out=o_d[d][0][:, grp * 512:(grp + 1) * 512], in_=po[:], func=AF.Copy),
                                     reads=[b_po], writes=[o_d[d][1]])
                        if step == 35:
                            continue
                        S.op("dve", lambda e, d=d, c=c, pu=pu: e.scalar_tensor_tensor(out=state[d][0][:], in0=state[d][0][:], scalar=dec[d][0][:, c:c + 1], in1=pu[:, 0:128],
                                                                                       op0=ALU.mult, op1=ALU.add),
                             reads=[state[d][1], dec[d][1], b_pu], writes=[state[d][1]])
                        S.op("act", lambda e, d=d, sb_next=sb_next: e.activation(out=sb_next[:], in_=state[d][0][:], func=AF.Copy),
                             reads=[state[d][1]], writes=[b_sbn])
                    if step % 2 == 1:
                        for q_ in early_q:
                            if q_:
                                q_.pop(0)()
                for q_ in early_q:
                    while q_:
                        q_.pop(0)()
                ost, b_ost = osts[h % 2]
                S.op("dve", lambda e, ost=ost: e.tensor_tensor(out=ost[:], in0=o_d[0][0][:], in1=o_d[1][0][:], op=ALU.add),
                     reads=[o_d[0][1], o_d[1][1]], writes=[b_ost])
                store("sp", o_s[h * 128:(h + 1) * 128, :], ost[:], b_ost, B["o_s"])
                if nxt is not None:
                    load("sp", v64[:], rvv[:, :, (h + 1) * 128:(h + 2) * 128], b_v64, B["rv_s"])
                    run_interleaved([nxt[0][1], nxt[1][1]])
            pss = [pout[0][0], pout[1][0], pupd[0], pupd[1]]
            sq, b_sq = frs[0]
            rstd, b_rstd = T1s[0]
            mh2, b_mh2 = T2s[0]
            rec32, b_rec32 = T3s[0]
            sqb = sq[:].bitcast(BF16)
            for h in range(8):
                oh, b_oh = QT[h % 2]
                load("sp", oh[:], o_s[h * 128:(h + 1) * 128, :], b_oh, B["o_s"])
                S.op("dve", lambda e, oh=oh: e.tensor_tensor(out=sqb[:, 0:T], in0=oh[:], in1=oh[:], op=ALU.mult),
                     reads=[b_oh], writes=[b_sq])
                for tb in range(4):
                    S.op("pe", lambda e, h=h, tb=tb: e.matmul(pss[tb][0][:], lhsT=onesb[:], rhs=sqb[:, tb * 512:(tb + 1) * 512], start=(h == 0), stop=(h == 7)),
                         reads=[b_sq, b_onesb], writes=[pss[tb][1]])
            for tb in range(4):
                S.op("dve", lambda e, tb=tb: e.tensor_scalar(out=rstd[:, tb * 512:(tb + 1) * 512], in0=pss[tb][0][:], scalar1=1.0 / 1024, scalar2=EPS, op0=ALU.mult, op1=ALU.add),
                     reads=[pss[tb][1]], writes=[b_rstd])
            S.op("act", lambda e: e.activation(out=rstd[:, 0:T], in_=rstd[:, 0:T], func=AF.Ln), reads=[b_rstd], writes=[b_rstd])
            S.op("act", lambda e: e.activation(out=rstd[:, 0:T], in_=rstd[:, 0:T], func=AF.Exp, scale=-0.5), reads=[b_rstd], writes=[b_rstd])
            for h in range(8):
                oh, b_oh = QT[h % 2]; rgt, b_rgt = Q2[h % 2]; rst_, b_rst_ = osts[h % 2]
                load("sp", oh[:], o_s[h * 128:(h + 1) * 128, :], b_oh, B["o_s"])
                load("sp", rgt[:], rg_s[h * 128:(h + 1) * 128, :], b_rgt, B["rg_s"])
                S.op("dve", lambda e, h=h, oh=oh: e.scalar_tensor_tensor(out=rec32[:, 0:T], in0=oh[:], scalar=fm[:, V_GN + h:V_GN + h + 1], in1=rstd[:, 0:T],
                                                                          op0=ALU.mult, op1=ALU.mult), reads=[b_oh, b_fm, b_rstd], writes=[b_rec32])
                S.op("dve", lambda e, rst_=rst_, rgt=rgt: e.tensor_tensor(out=rst_[:], in0=rec32[:, 0:T], in1=rgt[:], op=ALU.mult), reads=[b_rec32, b_rgt], writes=[b_rst_])
                store("sp", recT_s[h * 128:(h + 1) * 128, :], rst_[:], b_rst_, B["recT_s"])
            S.flush()
        if upto <= 4:
            return nc, S

        with contextlib.ExitStack() as st:
            wba, b_wba = sb(st, "wba", [128, 8, D], BF16)
            wbr, b_wbr = sb(st, "wbr", [128, 8, D], BF16)
            load("pool", wba[:], wba_d.rearrange("(kc p) n -> p kc n", p=128), b_wba)
            load("pool", wbr[:], wbr_d.rearrange("(kc p) n -> p kc n", p=128), b_wbr)
            attb = [sb(st, "attb%d" % i, [128, 8, 256], BF16) for i in range(2)]
            recb = [sb(st, "recb%d" % i, [128, 8, 256], BF16) for i in range(2)]
            gab = [sb(st, "gab%d" % i, [128, 16, 256], BF16) for i in range(2)]
            grb = [sb(st, "grb%d" % i, [128, 16, 256], BF16) for i in range(2)]
            yTb = [sb(st, "yTb%d" % i, [128, 16, 256], BF16) for i in range(2)]
            tas = [sb(st, "ta%d" % i, [128, 256], F32) for i in range(2)]
            trs = [sb(st, "tr%d" % i, [128, 256], F32) for i in range(2)]
            pas = [ps(st, "pa%d" % i, [128, 512], F32) for i in range(2)]
            prs = [ps(st, "pr%d" % i, [128, 512], F32) for i in range(2)]
            attv = attT_s.rearrange("(kc p) t -> p kc t", p=128)
            recv = recT_s.rearrange("(kc p) t -> p kc t", p=128)
            gav = ga_s.rearrange("(kc p) t -> p kc t", p=128)
            grv = gr_s.rearrange("(kc p) t -> p kc t", p=128)
            yTv = yT_s.rearrange("(kc p) t -> p kc t", p=128)
            def p4_loads(tb):
                s_ = tb % 2
                t0 = tb * 256
                load("sp", attb[s_][0][:], attv[:, :, t0:t0 + 256], attb[s_][1], B["attT_s"])
                load("sp", recb[s_][0][:], recv[:, :, t0:t0 + 256], recb[s_][1], B["recT_s"])
                load("sp", gab[s_][0][:], gav[:, :, t0:t0 + 256], gab[s_][1], B["ga_s"])
                load("sp", grb[s_][0][:], grv[:, :, t0:t0 + 256], grb[s_][1], B["gr_s"])

            p4_loads(0)
            for tb in range(8):
                s_ = tb % 2
                t0 = tb * 256
                if tb + 1 < 8:
                    p4_loads(tb + 1)
                for dc in range(16):
                    pa, b_pa = pas[dc % 2]; pr, b_pr = prs[dc % 2]
                    ta, b_ta = tas[dc % 2]; tr, b_tr = trs[dc % 2]
                    for kc in range(8):
                        S.op("pe", lambda e, kc=kc, dc=dc, pa=pa, s_=s_: e.matmul(pa[:, 0:256], lhsT=wba[:, kc, dc * 128:(dc + 1) * 128], rhs=attb[s_][0][:, kc, :],
                                                                                start=(kc == 0), stop=(kc == 7)), reads=[b_wba, attb[s_][1]], writes=[b_pa])
                    for kc in range(8):
                        S.op("pe", lambda e, kc=kc, dc=dc, pr=pr, s_=s_: e.matmul(pr[:, 0:256], lhsT=wbr[:, kc, dc * 128:(dc + 1) * 128], rhs=recb[s_][0][:, kc, :],
                                                                                start=(kc == 0), stop=(kc == 7)), reads=[b_wbr, recb[s_][1]], writes=[b_pr])
                    S.op("dve", lambda e, dc=dc, pa=pa, ta=ta, s_=s_: e.tensor_tensor(out=ta[:], in0=pa[:, 0:256], in1=gab[s_][0][:, dc, :], op=ALU.mult),
                         reads=[b_pa, gab[s_][1]], writes=[b_ta])
                    S.op("dve", lambda e, dc=dc, pr=pr, tr=tr, s_=s_: e.tensor_tensor(out=tr[:], in0=pr[:, 0:256], in1=grb[s_][0][:, dc, :], op=ALU.mult),
                         reads=[b_pr, grb[s_][1]], writes=[b_tr])
                    S.op("dve", lambda e, dc=dc, ta=ta, tr=tr, s_=s_: e.tensor_tensor(out=yTb[s_][0][:, dc, :], in0=ta[:], in1=tr[:], op=ALU.add),
                         reads=[b_ta, b_tr], writes=[yTb[s_][1]])
                store("sp", yTv[:, :, t0:t0 + 256], yTb[s_][0][:], yTb[s_][1], B["yT_s"])
            S.flush()
        if upto <= 5:
            return nc, S

        with contextlib.ExitStack() as st:
            wout, b_wout = sb(st, "wout", [128, 16, D], BF16)
            wov = wout_d.rearrange("(kc p) n -> p kc n", p=128)
            load("pool", wout[:, 0:8, :], wov[:, 0:8, :], b_wout)
            load("pool", wout[:, 8:16, :], wov[:, 8:16, :], b_wout)
            g1bc, b_g1 = sb(st, "g1bc", [128, D], F32)
            load("sp", g1bc[:], mod_s[0:1, 2 * D:3 * D].partition_broadcast(128), b_g1, B["mod_s"])
            ybs = [sb(st, "yb%d" % i, [128, 16, 512], BF16) for i in range(2)]
            xts = [sb(st, "xt%d" % i, [128, D], F32) for i in range(2)]
            x1ts = [sb(st, "x1t%d" % i, [128, D], F32) for i in range(2)]
            tts = [sb(st, "tt%d" % i, [128, 512], F32) for i in range(2)]
            junk, b_junk = sb(st, "junk", [128, D], BF16)
            ssqs = [sb(st, "ssq%d" % i, [128, 1], F32) for i in range(2)]
            xns = [sb(st, "xn%d" % i, [128, D], BF16) for i in range(2)]
            h2st, b_h2st = sb(st, "h2st", [128, 16, 512], BF16)
            pts = [ps(st, "pt%d" % i, [128, 1024], BF16) for i in range(4)]
            pos = [ps(st, "po%d" % i, [128, 512], F32) for i in range(4)]
            yTv = yT_s.rearrange("(kc p) t -> p kc t", p=128)
            h2v = h2T_s.rearrange("(kc p) t -> p kc t", p=128)
            def p5a(tt):
                tb, q = tt // 4, tt % 4
                yb, b_yb = ybs[tb % 2]
                if tt == 0:
                    load("sp", yb[:], yTv[:, :, 0:512], b_yb, B["yT_s"])
                if q == 0 and tb + 1 < 4:
                    load("sp", ybs[(tb + 1) % 2][0][:], yTv[:, :, (tb + 1) * 512:(tb + 2) * 512], ybs[(tb + 1) % 2][1], B["yT_s"])
                xt, b_xt = xts[tt % 2]; x1t, b_x1t = x1ts[tt % 2]
                load("sp", xt[:], x_d[tt * 128:(tt + 1) * 128, :], b_xt)
                for db in range(4):
                    po, b_po = pos[db]
                    tq, b_tq = tts[db % 2]
                    for dc in range(16):
                        S.op("pe", lambda e, dc=dc, q=q, db=db, po=po, yb=yb: e.matmul(po[:], lhsT=yb[:, dc, q * 128:(q + 1) * 128], rhs=wout[:, dc, db * 512:(db + 1) * 512],
                                                                                     start=(dc == 0), stop=(dc == 15)), reads=[b_yb, b_wout], writes=[b_po])
                    S.op("dve", lambda e, db=db, po=po, tq=tq: e.tensor_tensor(out=tq[:], in0=po[:], in1=g1bc[:, db * 512:(db + 1) * 512], op=ALU.mult),
                         reads=[b_po, b_g1], writes=[b_tq])
                    S.op("dve", lambda e, db=db, tq=tq, xt=xt, x1t=x1t: e.tensor_tensor(out=x1t[:, db * 512:(db + 1) * 512], in0=tq[:], in1=xt[:, db * 512:(db + 1) * 512], op=ALU.add),
                         reads=[b_tq, b_xt], writes=[b_x1t])
                store("sp", x1_s[tt * 128:(tt + 1) * 128, :], x1t[:], b_x1t, B["x1_s"])

            def p5n(tt):
                x1t, b_x1t = x1ts[tt % 2]
                norm_p1(x1t[:], b_x1t, junk, b_junk, ssqs[tt % 2][0], ssqs[tt % 2][1], xns[tt % 2][0], xns[tt % 2][1])

            def p5b(tt):
                tb, q = tt // 4, tt % 4
                norm_p2(xns[tt % 2][0], xns[tt % 2][1], pts, lambda kc, q=q: h2st[:, kc, q * 128:(q + 1) * 128], b_h2st,
                        lambda kc: A2[:, kc:kc + 1], lambda kc: modT[:, 48 + kc, 0:1], b_A2, b_modT)
                if q == 3:
                    store("sp", h2v[:, :, tb * 512:(tb + 1) * 512], h2st[:], b_h2st, B["h2T_s"])

            p5a(0)
            p5n(0)
            for tt in range(16):
                if tt + 1 < 16:
                    p5a(tt + 1)
                p5b(tt)
                if tt + 1 < 16:
                    p5n(tt + 1)
            S.flush()
        if upto <= 6:
            return nc, S

        wupv = wup_d.rearrange("(kc p) n -> p kc n", p=128)
        wdnv = wdn_d.rearrange("(fc p) n -> p fc n", p=128)
        h2v = h2T_s.rearrange("(kc p) t -> p kc t", p=128)
        for blk in range(2):
            tok0 = blk * 1024
            with contextlib.ExitStack() as st:
                gT, b_gT = sb(st, "gT", [128, 44, 1024], BF16)
                with contextlib.ExitStack() as st2:
                    h2b, b_h2b = sb(st2, "h2b", [128, 16, 1024], BF16)
                    halo, b_halo = sb(st2, "halo", [128, 16, 2], BF16)
                    was = [sb(st2, "wa%d" % i, [128, 16, 256], BF16) for i in range(2)]
                    wbs = [sb(st2, "wb%d" % i, [128, 16, 256], BF16) for i in range(2)]
                    uxs = [sb(st2, "ux%d" % i, [128, 1026], F32) for i in range(2)]
                    tcs = [sb(st2, "tc%d" % i, [128, 1024], F32) for i in range(2)]
                    pus = [ps(st2, "pu%d" % i, [128, 512], F32) for i in range(6)]
                    ph, b_ph = ps(st2, "ph", [128, 512], F32)
                    load("sp", h2b[:], h2v[:, :, tok0:tok0 + 1024], b_h2b, B["h2T_s"])
                    S.op("pool", lambda e: e.memset(halo[:], 0.0), writes=[b_halo])
                    if blk == 1:
                        load("sp", halo[:, :, 0:1], h2v[:, :, tok0 - 1:tok0], b_halo, B["h2T_s"], slow=True)
                    else:
                        load("sp", halo[:, :, 1:2], h2v[:, :, tok0 + 1024:tok0 + 1025], b_halo, B["h2T_s"], slow=True)
                    ipu = 0
                    iph = 0
                    for i2 in range(22):
                        wa, b_wa = was[i2 % 2]; wb, b_wb = wbs[i2 % 2]
                        load("pool", wa[:], wupv[:, :, i2 * 256:(i2 + 1) * 256], b_wa)
                        load("pool", wb[:], wupv[:, :, FF + i2 * 256:FF + (i2 + 1) * 256], b_wb)
                        for jj in range(2):
                            i = i2 * 2 + jj
                            for part in range(2):
                                wtile, b_wt_ = (wa, b_wa) if part == 0 else (wb, b_wb)
                                ux, b_ux = uxs[part]; tcv, b_tc = tcs[part]
                                col = i + 44 * part
                                for sbk in range(2):
                                    pu, b_pu = pus[ipu % 6]; ipu += 1
                                    for kc in range(16):
                                        S.op("pe", lambda e, kc=kc, jj=jj, sbk=sbk, pu=pu, wtile=wtile: e.matmul(pu[:], lhsT=wtile[:, kc, jj * 128:(jj + 1) * 128],
                                                                                                                rhs=h2b[:, kc, sbk * 512:(sbk + 1) * 512], start=(kc == 0), stop=(kc == 15)),
                                             reads=[b_wt_, b_h2b], writes=[b_pu])
                                    S.op("act", lambda e, sbk=sbk, pu=pu, ux=ux: e.activation(out=ux[:, 1 + sbk * 512:1 + (sbk + 1) * 512], in_=pu[:], func=AF.Copy),
                                         reads=[b_pu], writes=[b_ux])
                                hs = (iph % 8) * 2; iph += 1
                                for kc in range(16):
                                    S.op("pe", lambda e, kc=kc, jj=jj, hs=hs, wtile=wtile: e.matmul(ph[:, hs:hs + 2], lhsT=wtile[:, kc, jj * 128:(jj + 1) * 128], rhs=halo[:, kc, :],
                                                                                                  start=(kc == 0), stop=(kc == 15)), reads=[b_wt_, b_halo], writes=[b_ph])
                                S.op("act", lambda e, hs=hs, ux=ux: e.activation(out=ux[:, 0:1], in_=ph[:, hs:hs + 1], func=AF.Copy), reads=[b_ph], writes=[b_ux])
                                S.op("act", lambda e, hs=hs, ux=ux: e.activation(out=ux[:, 1025:1026], in_=ph[:, hs + 1:hs + 2], func=AF.Copy), reads=[b_ph], writes=[b_ux])
                                S.op("act", lambda e, ux=ux, tcv=tcv, col=col: e.activation(out=tcv[:], in_=ux[:, 1:1025], func=AF.Identity,
                                                                                           scale=fm[:, V_CW + 88 + col:V_CW + 88 + col + 1], bias=fm[:, V_CB + col:V_CB + col + 1]),
                                     reads=[b_ux, b_fm], writes=[b_tc])
                                S.op("dve", lambda e, ux=ux, tcv=tcv, col=col: e.scalar_tensor_tensor(out=tcv[:], in0=ux[:, 0:1024], scalar=fm[:, V_CW + col:V_CW + col + 1], in1=tcv[:],
                                                                                                     op0=ALU.mult, op1=ALU.add), reads=[b_ux, b_fm, b_tc], writes=[b_tc])
                                S.op("dve", lambda e, ux=ux, tcv=tcv, col=col: e.scalar_tensor_tensor(out=tcv[:], in0=ux[:, 2:1026], scalar=fm[:, V_CW + 176 + col:V_CW + 176 + col + 1], in1=tcv[:],
                                                                                                     op0=ALU.mult, op1=ALU.add), reads=[b_ux, b_fm, b_tc], writes=[b_tc])
                            S.op("act", lambda e: e.activation(out=tcs[0][0][:], in_=tcs[0][0][:], func=AF.Silu), reads=[tcs[0][1]], writes=[tcs[0][1]])
                            S.op("dve", lambda e, i=i: e.tensor_tensor(out=gT[:, i, :], in0=tcs[0][0][:], in1=tcs[1][0][:], op=ALU.mult),
                                 reads=[tcs[0][1], tcs[1][1]], writes=[b_gT])
                    S.flush()
                with contextlib.ExitStack() as st2:
                    g2bc, b_g2 = sb(st2, "g2bc", [128, D], F32)
                    load("sp", g2bc[:], mod_s[0:1, 5 * D:6 * D].partition_broadcast(128), b_g2, B["mod_s"])
                    wds = [sb(st2, "wd%d" % i, [128, 4, 512], BF16) for i in range(4)]
                    wdf = [sb(st2, "wdf%d" % i, [128, 4, 512], F32) for i in range(4)]
                    x1p = [sb(st2, "x1p%d" % i, [128, 512], F32) for i in range(8)]
                    tps = [sb(st2, "tp%d" % i, [128, 512], F32) for i in range(8)]
                    pds = [ps(st2, "pd%d" % i, [128, 512], F32) for i in range(8)]
                    iw = 0
                    for db in range(4):
                        for tt in range(8):
                            r0 = tok0 + tt * 128
                            load("sp", x1p[tt][0][:], x1_s[r0:r0 + 128, db * 512:(db + 1) * 512], x1p[tt][1], B["x1_s"])
                        for f4 in range(11):
                            wd, b_wd = wds[iw % 4]; wf, b_wf = wdf[iw % 4]; iw += 1
                            load("act", wf[:], wdnv[:, f4 * 4:(f4 + 1) * 4, db * 512:(db + 1) * 512], b_wf)
                            S.op("act", lambda e, wf=wf, wd=wd: e.activation(out=wd[:].rearrange("p a b -> p (a b)"), in_=wf[:].rearrange("p a b -> p (a b)"), func=AF.Copy),
                                 reads=[b_wf], writes=[b_wd])
                            for fj in range(4):
                                fc = f4 * 4 + fj
                                for tt in range(8):
                                    S.op("pe", lambda e, fc=fc, fj=fj, tt=tt, wd=wd: e.matmul(pds[tt][0][:], lhsT=gT[:, fc, tt * 128:(tt + 1) * 128], rhs=wd[:, fj, :],
                                                                                            start=(fc == 0), stop=(fc == 43)), reads=[b_gT, b_wd], writes=[pds[tt][1]])
                        for tt in range(8):
                            S.op("dve", lambda e, tt=tt, db=db: e.tensor_tensor(out=tps[tt][0][:], in0=pds[tt][0][:], in1=g2bc[:, db * 512:(db + 1) * 512], op=ALU.mult),
                                 reads=[pds[tt][1], b_g2], writes=[tps[tt][1]])
                        for tt in range(8):
                            r0 = tok0 + tt * 128
                            S.op("pool", lambda e, tt=tt: e.tensor_tensor(out=tps[tt][0][:], in0=tps[tt][0][:], in1=x1p[tt][0][:], op=ALU.add),
                                 reads=[tps[tt][1], x1p[tt][1]], writes=[tps[tt][1]])
                            store("sp", x2_s[r0:r0 + 128, db * 512:(db + 1) * 512], tps[tt][0][:], tps[tt][1], B["x2_s"])
                    S.flush()
        if upto <= 7:
            return nc, S

        with contextlib.ExitStack() as st:
            fnw, b_fnw = sb(st, "fnw", [128, D], F32)
            load("sp", fnw[:], fnw_d.partition_broadcast(128), b_fnw)
            xts = [sb(st, "xf%d" % i, [128, D], F32) for i in range(4)]
            ots = [sb(st, "of%d" % i, [128, D], F32) for i in range(2)]
            junk, b_junk = sb(st, "junk", [128, D], BF16)
            ssqs = [sb(st, "ssq%d" % i, [128, 1], F32) for i in range(2)]
            for tt in range(3):
                load("sp", xts[tt][0][:], x2_s[tt * 128:(tt + 1) * 128, :], xts[tt][1], B["x2_s"])
            for tt in range(16):
                xt, b_xt = xts[tt % 4]; ot, b_ot = ots[tt % 2]; ssq, b_ssq = ssqs[tt % 2]
                if tt + 3 < 16:
                    load("sp", xts[(tt + 3) % 4][0][:], x2_s[(tt + 3) * 128:(tt + 4) * 128, :], xts[(tt + 3) % 4][1], B["x2_s"])
                S.op("act", lambda e, xt=xt: e.activation(out=junk[:], in_=xt[:], func=AF.Square), reads=[b_xt], writes=[b_junk])
                S.op("dve", lambda e, ssq=ssq: e.reduce_sum(out=ssq[:], in_=junk[:], axis=mybir.AxisListType.X), reads=[b_junk], writes=[b_ssq])
                S.op("pool", lambda e, ssq=ssq: e.tensor_scalar(out=ssq[:], in0=ssq[:], scalar1=1.0 / D, scalar2=EPS, op0=ALU.mult, op1=ALU.add), reads=[b_ssq], writes=[b_ssq])
                S.op("pool", lambda e, ssq=ssq: e.tensor_tensor(out=ssq[:], in0=ssq[:], in1=mhalf[:], op=ALU.pow), reads=[b_ssq, b_mhalf], writes=[b_ssq])
                S.op("dve", lambda e, xt=xt, ot=ot, ssq=ssq: e.scalar_tensor_tensor(out=ot[:], in0=xt[:], scalar=ssq[:, 0:1], in1=fnw[:], op0=ALU.mult, op1=ALU.mult),
                     reads=[b_xt, b_ssq, b_fnw], writes=[b_ot])
                store("sp", out_d[tt * 128:(tt + 1) * 128, :], ot[:], b_ot, B["out"])
            S.flush()
        return nc, S


_CONST = None


def _consts():
    global _CONST
    if _CONST is not None:
        return _CONST
    bf = ml_dtypes.bfloat16
    rows = T // 64
    r, col = np.meshgrid(np.arange(rows), np.arange(64), indexing="ij")
    pos = np.stack([r.reshape(-1), col.reshape(-1)], axis=-1).astype(np.float32)
    inv = (np.float32(10000.0) ** (-(np.arange(16, dtype=np.float32)) / np.float32(16))).astype(np.float32)
    ang = (pos[:, :, None] * inv).astype(np.float32)
    cs, sn = np.cos(ang).astype(np.float32), np.sin(ang).astype(np.float32)
    cosT = np.zeros((128, T), np.float32); sinT = np.zeros((128, T), np.float32)
    perm = np.zeros((128, 128), np.float32)
    for p in range(128):
        a, h, i = (p % 64) // 32, (p % 32) // 16, p % 16
        cosT[p] = cs[:, a, i]
        sinT[p] = sn[:, a, i] * (-1.0 if h == 0 else 1.0)
        partner = p + 16 if h == 0 else p - 16
        perm[partner, p] = 1.0
    s_, t_ = np.meshgrid(np.arange(64), np.arange(64), indexing="ij")
    tri = np.stack([(t_ >= s_), (t_ <= s_)], 0).astype(np.float32)
    tri = np.broadcast_to(tri.transpose(1, 0, 2)[:, :, None, :], (64, 2, 8, 64)).reshape(64, 1024)
    smask = np.ones((128, TA), np.float32); smask[:, ::64] = 0.0
    _CONST = dict(cosT=cosT, sinT=sinT, perm=perm.astype(bf), identb=np.eye(128, dtype=np.float32).astype(bf),
                  identf=np.eye(128, dtype=np.float32), tri=np.ascontiguousarray(tri).astype(bf), smask=smask)
    return _CONST


def make_in_maps(inputs):
    f = lambda a: np.ascontiguousarray(np.asarray(a, dtype=np.float32))
    x = f(inputs["x"]); c = f(inputs["c"]); ctx = f(inputs["ctx"]); c_ctx = f(inputs["c_ctx"])
    shared = dict(_consts())
    shared["w_mod"] = f(inputs["w_mod"][0]); shared["w_in"] = f(inputs["w_in"][0])
    shared["b_mod2"] = np.ascontiguousarray(np.stack([f(inputs["b_mod"][0])] * 2, 0))
    shared["lam4"] = np.concatenate([f(inputs[k][0]) for k in ("lam_q1", "lam_k1", "lam_q2", "lam_k2")]).reshape(1, 256)
    shared["subln"] = f(inputs["subln_w"][0]).reshape(1, 128)
    shared["w_ba"] = f(inputs["w_branch_attn"][0]); shared["w_br"] = f(inputs["w_branch_rec"][0]); shared["w_out"] = f(inputs["w_out"][0])
    shared["w_up"] = f(inputs["w_up"][0]); shared["w_down"] = f(inputs["w_down"][0]); shared["fnw"] = f(inputs["final_norm_w"]).reshape(1, D)
    vec_tail = np.concatenate([f(inputs["norm1_w"][0]), f(inputs["norm2_w"][0]), f(inputs["rec_gnorm_w"][0]),
                               f(inputs["rec_lb"]).reshape(-1), f(inputs["conv_w"][0]).reshape(-1), f(inputs["conv_b"][0])])
    maps = []
    for b in range(8):
        m = dict(shared)
        m["x"] = x[b]; m["ctx"] = ctx[b]
        m["vecs"] = np.ascontiguousarray(np.concatenate([c[b], c_ctx, vec_tail]).reshape(NV, 128))
        maps.append(m)
    return maps


_NC = None


def kernel(**inputs):
    global _NC
    if _NC is None:
        _NC = build()[0]
    maps = make_in_maps(inputs)
    res = run_bass_kernel_spmd(_NC, maps, core_ids=list(range(8)))
    return np.stack([np.asarray(r["out"], dtype=np.float32) for r in res.results], 0)
```
